# Optimizing a Trainium2 kernel written in Bass

```python
import jax, jax.numpy as jnp
from jax import lax
import numpy as np

D_MODEL = 1024
BATCH = 1
SEQ = 16384
DEPTH = 4

GRID_W = 64
CTX_LEN = 256
N_EVEN = (DEPTH + 1) // 2
N_ODD = DEPTH // 2
EPS = 1e-6

FOURIER_GROUPS = 4
FOURIER_GROUP_DIM = 128
FOURIER_WIDTH = FOURIER_GROUPS * FOURIER_GROUP_DIM
CONV_GROUPS = 4
CONV_WIDTH = 512
CONV_K = 3
EVEN_IN = FOURIER_WIDTH + 3 * CONV_WIDTH
EVEN_OUT = FOURIER_WIDTH + CONV_WIDTH

N_HEADS = 16
N_KV_HEADS = 4
GQA_GROUP = N_HEADS // N_KV_HEADS
HEAD_DIM = 64
Q_DIM = N_HEADS * HEAD_DIM
KV_DIM = N_KV_HEADS * HEAD_DIM
QKV_OUT = Q_DIM + 2 * KV_DIM
WINDOW = 128
ATT_BLOCK = 128
ROPE_BASE = 10000.0
ROPE_PAIRS_PER_AXIS = HEAD_DIM // 4
NEG_INF = -1e30

N_GROUPS = 4
EXPERTS_PER_GROUP = 8
N_EXPERTS = N_GROUPS * EXPERTS_PER_GROUP
TOP_K = 2
D_FF_EXPERT = 512
MOE_BLOCK = 128

kernel_name = "hybrid_fourier_conv_swa_hmoe_dit"


def rms_norm(x, g):
    xf = x.astype(jnp.float32)
    y = xf * lax.rsqrt(jnp.mean(xf * xf, axis=-1, keepdims=True) + EPS)
    return (y * g.astype(jnp.float32)).astype(x.dtype)


def modulate(h, shift, scale):
    return h * (1 + scale) + shift


def fourier_mix(a):
    b_, l_, _ = a.shape
    a4 = a.reshape(b_, l_, FOURIER_GROUPS, FOURIER_GROUP_DIM).astype(jnp.float32)
    f = jnp.fft.fftn(a4, axes=(1, 3), norm="ortho").real
    return f.reshape(b_, l_, FOURIER_WIDTH).astype(a.dtype)


def short_conv_mix(b_gate, c_gate, h, w):
    u = c_gate * h
    up = jnp.pad(u, ((0, 0), (1, 1), (0, 0)))
    y = w[0] * up[:, :-2] + w[1] * up[:, 1:-1] + w[2] * up[:, 2:]
    return b_gate * y


def even_mixer(h, w_in, conv_w, w_out):
    p = h @ w_in
    a, b_gate, c_gate, hv = jnp.split(
        p, [FOURIER_WIDTH, FOURIER_WIDTH + CONV_WIDTH, FOURIER_WIDTH + 2 * CONV_WIDTH], axis=-1)
    y = jnp.concatenate([fourier_mix(a), short_conv_mix(b_gate, c_gate, hv, conv_w)], axis=-1)
    return y @ w_out


def axial_rope_tables(n_tok):
    t = jnp.arange(n_tok, dtype=jnp.int32)
    pos = jnp.stack([t // GRID_W, t % GRID_W], axis=-1).astype(jnp.float32)
    freqs = ROPE_BASE ** (-jnp.arange(ROPE_PAIRS_PER_AXIS, dtype=jnp.float32) / ROPE_PAIRS_PER_AXIS)
    ang = pos[:, :, None] * freqs
    return jnp.cos(ang), jnp.sin(ang)


def apply_axial_rope(x, cos, sin):
    b_, l_, h_, _ = x.shape
    xf = x.astype(jnp.float32).reshape(b_, l_, h_, 2, 2, ROPE_PAIRS_PER_AXIS)
    x1, x2 = xf[..., 0, :], xf[..., 1, :]
    cs, sn = cos[None, :, None], sin[None, :, None]
    out = jnp.stack([x1 * cs - x2 * sn, x1 * sn + x2 * cs], axis=-2)
    return out.reshape(x.shape).astype(x.dtype)


def project_q(h, w_qkv, q_g):
    b_, l_, _ = h.shape
    q = (h @ w_qkv[:, :Q_DIM]).reshape(b_, l_, N_HEADS, HEAD_DIM)
    return rms_norm(q, q_g)


def project_kv(h, w_qkv, k_g):
    b_, l_, _ = h.shape
    kv = h @ w_qkv[:, Q_DIM:]
    k = rms_norm(kv[..., :KV_DIM].reshape(b_, l_, N_KV_HEADS, HEAD_DIM), k_g)
    v = kv[..., KV_DIM:].reshape(b_, l_, N_KV_HEADS, HEAD_DIM)
    return k, v


def window_attention(hl, hc, w_qkv, q_g, k_g, sink, w_o, need_ctx_out):
    B, L, _ = hl.shape
    Lc = hc.shape[1]
    scale = HEAD_DIM ** -0.5
    cos, sin = axial_rope_tables(L)
    ql = apply_axial_rope(project_q(hl, w_qkv, q_g), cos, sin) * scale
    kl, vl = project_kv(hl, w_qkv, k_g)
    kl = apply_axial_rope(kl, cos, sin)
    kc, vc = project_kv(hc, w_qkv, k_g)
    sink_g = sink.reshape(N_KV_HEADS, GQA_GROUP).astype(jnp.float32)

    n_blocks = L // ATT_BLOCK
    qg = ql.reshape(B, L, N_KV_HEADS, GQA_GROUP, HEAD_DIM)
    pad = ((0, 0), (ATT_BLOCK, ATT_BLOCK), (0, 0), (0, 0))
    kp, vp = jnp.pad(kl, pad), jnp.pad(vl, pad)
    qi = jnp.arange(ATT_BLOCK)
    kj = jnp.arange(3 * ATT_BLOCK)
    band = jnp.abs(kj[None, :] - ATT_BLOCK - qi[:, None]) <= WINDOW

    def block(bidx):
        start = bidx * ATT_BLOCK
        qb = lax.dynamic_slice_in_dim(qg, start, ATT_BLOCK, axis=1)
        kb = lax.dynamic_slice_in_dim(kp, start, 3 * ATT_BLOCK, axis=1)
        vb = lax.dynamic_slice_in_dim(vp, start, 3 * ATT_BLOCK, axis=1)
        kpos = start - ATT_BLOCK + kj
        mask = band & ((kpos >= 0) & (kpos < L))[None, :]
        s_loc = jnp.einsum("bqkgd,bskd->bkgqs", qb, kb, preferred_element_type=jnp.float32)
        s_loc = jnp.where(mask, s_loc, NEG_INF)
        s_ctx = jnp.einsum("bqkgd,bskd->bkgqs", qb, kc, preferred_element_type=jnp.float32)
        s_sink = jnp.broadcast_to(sink_g[None, :, :, None, None], s_loc.shape[:-1] + (1,))
        p = jax.nn.softmax(jnp.concatenate([s_loc, s_ctx, s_sink], axis=-1), axis=-1)
        p_loc = p[..., :3 * ATT_BLOCK].astype(vb.dtype)
        p_ctx = p[..., 3 * ATT_BLOCK:3 * ATT_BLOCK + Lc].astype(vc.dtype)
        return (jnp.einsum("bkgqs,bskd->bqkgd", p_loc, vb)
                + jnp.einsum("bkgqs,bskd->bqkgd", p_ctx, vc))

    ob = lax.map(block, jnp.arange(n_blocks))
    yl = jnp.moveaxis(ob, 0, 1).reshape(B, L, Q_DIM) @ w_o

    if not need_ctx_out:
        return yl, None
    qc = project_q(hc, w_qkv, q_g).reshape(B, Lc, N_KV_HEADS, GQA_GROUP, HEAD_DIM) * scale
    s = jnp.einsum("bqkgd,bskd->bkgqs", qc, kc, preferred_element_type=jnp.float32)
    s_sink = jnp.broadcast_to(sink_g[None, :, :, None, None], s.shape[:-1] + (1,))
    p = jax.nn.softmax(jnp.concatenate([s, s_sink], axis=-1), axis=-1)[..., :Lc].astype(vc.dtype)
    yc = jnp.einsum("bkgqs,bskd->bqkgd", p, vc).reshape(B, Lc, Q_DIM) @ w_o
    return yl, yc


def hier_moe(xf, w_rg, b_rg, w_re, b_re, w1, w3, w2):
    T, D = xf.shape
    xr = xf.astype(jnp.float32)
    g_logits = xr @ w_rg.astype(jnp.float32) + b_rg.astype(jnp.float32)
    g_prob = jax.nn.softmax(g_logits, axis=-1)
    g_val, g_idx = lax.top_k(g_prob, 1)
    e_logits = (xr @ w_re.astype(jnp.float32) + b_re.astype(jnp.float32)).reshape(
        T, N_GROUPS, EXPERTS_PER_GROUP)
    e_in = jnp.take_along_axis(e_logits, g_idx[:, :, None], axis=1)[:, 0]
    top_v, top_i = lax.top_k(e_in, TOP_K)
    gate = jax.nn.softmax(top_v, axis=-1) * g_val
    expert = g_idx * EXPERTS_PER_GROUP + top_i

    A = T * TOP_K
    e_flat = expert.reshape(A)
    t_flat = jnp.repeat(jnp.arange(T, dtype=jnp.int32), TOP_K)
    g_flat = gate.reshape(A)
    counts = jnp.bincount(e_flat, length=N_EXPERTS)
    padded = (counts + MOE_BLOCK - 1) // MOE_BLOCK * MOE_BLOCK
    pad_end = jnp.cumsum(padded)
    pad_start = pad_end - padded
    raw_start = jnp.cumsum(counts) - counts
    order = jnp.argsort(e_flat)
    e_s = e_flat[order]
    dest = pad_start[e_s] + jnp.arange(A) - raw_start[e_s]
    n_blocks = -(-A // MOE_BLOCK) + N_EXPERTS
    R = n_blocks * MOE_BLOCK
    row_tok = jnp.full((R,), T, jnp.int32).at[dest].set(t_flat[order])
    row_gate = jnp.zeros((R,), jnp.float32).at[dest].set(g_flat[order])
    block_exp = jnp.clip(jnp.searchsorted(pad_end, jnp.arange(n_blocks) * MOE_BLOCK, side="right"),
                         0, N_EXPERTS - 1)

    x_pad = jnp.concatenate([xf, jnp.zeros((1, D), xf.dtype)], axis=0)
    xb = x_pad[row_tok].reshape(n_blocks, MOE_BLOCK, D)

    def expert_block(args):
        xblk, e = args
        return (jax.nn.silu(xblk @ w1[e]) * (xblk @ w3[e])) @ w2[e]

    yb = lax.map(expert_block, (xb, block_exp)).reshape(R, D)
    out = jnp.zeros((T + 1, D), yb.dtype).at[row_tok].add(yb * row_gate[:, None].astype(yb.dtype))
    return out[:T]


def setup_inputs(seed: int = 0) -> dict:
    key = jax.random.key(seed)
    ks = jax.random.split(key, 24)
    D = D_MODEL

    def nrm(k, shape, s):
        return jax.random.normal(k, shape, jnp.float32) * s

    return {
        "x": nrm(ks[0], (BATCH, SEQ, D), 1.0),
        "c": nrm(ks[1], (BATCH, D), 1.0),
        "ctx": nrm(ks[2], (BATCH, CTX_LEN, D), 1.0),
        "c_ctx": nrm(ks[3], (D,), 1.0),
        "w_mod": nrm(ks[4], (DEPTH, D, 6 * D), 0.5 * D ** -0.5),
        "b_mod": nrm(ks[5], (DEPTH, 6 * D), 0.01),
        "norm_mix_g": 1.0 + nrm(ks[6], (DEPTH, D), 0.02),
        "norm_ffn_g": 1.0 + nrm(ks[7], (DEPTH, D), 0.02),
        "w_in_even": nrm(ks[8], (N_EVEN, D, EVEN_IN), D ** -0.5),
        "conv_w": nrm(ks[9], (N_EVEN, CONV_K, CONV_WIDTH), CONV_K ** -0.5),
        "w_out_even": nrm(ks[10], (N_EVEN, EVEN_OUT, D), EVEN_OUT ** -0.5),
        "w_qkv": nrm(ks[11], (N_ODD, D, QKV_OUT), D ** -0.5),
        "q_norm_g": 1.0 + nrm(ks[12], (N_ODD, HEAD_DIM), 0.02),
        "k_norm_g": 1.0 + nrm(ks[13], (N_ODD, HEAD_DIM), 0.02),
        "sink_logit": nrm(ks[14], (N_ODD, N_HEADS), 1.0),
        "w_o": nrm(ks[15], (N_ODD, Q_DIM, D), Q_DIM ** -0.5),
        "w_router_g": nrm(ks[16], (DEPTH, D, N_GROUPS), D ** -0.5),
        "b_router_g": nrm(ks[17], (DEPTH, N_GROUPS), 0.01),
        "w_router_e": nrm(ks[18], (DEPTH, D, N_EXPERTS), D ** -0.5),
        "b_router_e": nrm(ks[19], (DEPTH, N_EXPERTS), 0.01),
        "w1": nrm(ks[20], (DEPTH, N_EXPERTS, D, D_FF_EXPERT), D ** -0.5),
        "w3": nrm(ks[21], (DEPTH, N_EXPERTS, D, D_FF_EXPERT), D ** -0.5),
        "w2": nrm(ks[22], (DEPTH, N_EXPERTS, D_FF_EXPERT, D), D_FF_EXPERT ** -0.5),
    }


def reference(x, c, ctx, c_ctx, w_mod, b_mod, norm_mix_g, norm_ffn_g, w_in_even, conv_w, w_out_even,
              w_qkv, q_norm_g, k_norm_g, sink_logit, w_o, w_router_g, b_router_g, w_router_e, b_router_e,
              w1, w3, w2):
    B, L, D = x.shape
    Lc = ctx.shape[1]
    xl, xc = x, ctx
    for layer in range(DEPTH):
        last = layer == DEPTH - 1
        is_even = layer % 2 == 0
        j = layer // 2
        mod_l = (jax.nn.silu(c) @ w_mod[layer] + b_mod[layer])[:, None, :]
        mod_c = jax.nn.silu(c_ctx) @ w_mod[layer] + b_mod[layer]
        sh1_l, sc1_l, g1_l, sh2_l, sc2_l, g2_l = jnp.split(mod_l, 6, axis=-1)
        sh1_c, sc1_c, g1_c, sh2_c, sc2_c, g2_c = jnp.split(mod_c, 6, axis=-1)

        hl = modulate(rms_norm(xl, norm_mix_g[layer]), sh1_l, sc1_l)
        if is_even:
            yl = even_mixer(hl, w_in_even[j], conv_w[j], w_out_even[j])
            yc = None
            if not last:
                hc = modulate(rms_norm(xc, norm_mix_g[layer]), sh1_c, sc1_c)
                yc = even_mixer(hc, w_in_even[j], conv_w[j], w_out_even[j])
        else:
            hc = modulate(rms_norm(xc, norm_mix_g[layer]), sh1_c, sc1_c)
            yl, yc = window_attention(hl, hc, w_qkv[j], q_norm_g[j], k_norm_g[j], sink_logit[j], w_o[j],
                                      not last)
        xl = xl + g1_l * yl

        hl2 = modulate(rms_norm(xl, norm_ffn_g[layer]), sh2_l, sc2_l).reshape(B * L, D)
        moe_args = (w_router_g[layer], b_router_g[layer], w_router_e[layer], b_router_e[layer],
                    w1[layer], w3[layer], w2[layer])
        if last:
            xl = xl + g2_l * hier_moe(hl2, *moe_args).reshape(B, L, D)
        else:
            xc = xc + g1_c * yc
            hc2 = modulate(rms_norm(xc, norm_ffn_g[layer]), sh2_c, sc2_c).reshape(B * Lc, D)
            f = hier_moe(jnp.concatenate([hl2, hc2], axis=0), *moe_args)
            xl = xl + g2_l * f[:B * L].reshape(B, L, D)
            xc = xc + g2_c * f[B * L:].reshape(B, Lc, D)
    return xl
```

```python
import numpy as np
import ml_dtypes
from contextlib import ExitStack
import concourse.bass as bass
import concourse.mybir as mybir
from concourse.bass_utils import run_bass_kernel_spmd

F32 = mybir.dt.float32
BF16 = mybir.dt.bfloat16
I32 = mybir.dt.int32
AF = mybir.ActivationFunctionType
ALU = mybir.AluOpType
AX = mybir.AxisListType
NPBF = ml_dtypes.bfloat16

COMPUTE = ("pe", "act", "dve", "pool")
QUEUES = ("sp", "act", "pool")


class Buf:
    __slots__ = ("name", "last_w", "readers")

    def __init__(self, name):
        self.name = name
        self.last_w = None
        self.readers = []


class Op:
    __slots__ = ("eng", "fn", "deps", "is_dma", "idx", "signal", "sigval", "dsem", "dval", "dprev")

    def __init__(self, eng, fn, is_dma):
        self.eng = eng
        self.fn = fn
        self.is_dma = is_dma
        self.deps = set()
        self.signal = False
        self.sigval = 0
        self.dsem = None
        self.dval = 0
        self.dprev = None


class Prog:
    def __init__(self, nc, n_dma_sems=6):
        self.nc = nc
        self.ops = []
        self.n_dma_sems = n_dma_sems
        self.es = ExitStack()
        self._nb = 0

    def buf(self, name=None):
        self._nb += 1
        return Buf(name or f"b{self._nb}")

    def bufs(self, n, name="b"):
        return [self.buf(f"{name}{i}") for i in range(n)]

    def sb(self, name, shape, dt):
        return self.es.enter_context(self.nc.sbuf_tensor(name, shape, dt))

    def ps(self, name, shape, dt):
        return self.es.enter_context(self.nc.psum_tensor(name, shape, dt))

    def din(self, name, shape, dt):
        return self.nc.dram_tensor(name, list(shape), dt, kind="ExternalInput").ap()

    def dout(self, name, shape, dt):
        return self.nc.dram_tensor(name, list(shape), dt, kind="ExternalOutput").ap()

    def op(self, eng, fn, reads=(), writes=(), dma=False):
        o = Op(eng, fn, dma)
        o.idx = len(self.ops)
        for b in reads:
            if b.last_w is not None:
                o.deps.add(b.last_w)
        for b in writes:
            if b.last_w is not None:
                o.deps.add(b.last_w)
            for r in b.readers:
                o.deps.add(r)
        for b in reads:
            b.readers.append(o.idx)
        for b in writes:
            b.last_w = o.idx
            b.readers = []
        o.deps.discard(o.idx)
        self.ops.append(o)
        return o

    def pe(self, fn, reads=(), writes=()):
        return self.op("pe", fn, reads, writes)

    def act(self, fn, reads=(), writes=()):
        return self.op("act", fn, reads, writes)

    def dve(self, fn, reads=(), writes=()):
        return self.op("dve", fn, reads, writes)

    def pool(self, fn, reads=(), writes=()):
        return self.op("pool", fn, reads, writes)

    def dma(self, q, fn, reads=(), writes=()):
        return self.op(q, fn, reads, writes, dma=True)

    def emit(self, final_wait_ops=None):
        nc = self.nc
        ops = self.ops
        for o in ops:
            if o.eng == "pe" and not o.is_dma:
                o.deps = {d for d in o.deps if not (ops[d].eng == "pe" and not ops[d].is_dma)}
        qcount = {q: 0 for q in QUEUES}
        qlast = {}
        for o in ops:
            if o.is_dma:
                slot = qcount[o.eng] % self.n_dma_sems
                qcount[o.eng] += 1
                key = (o.eng, slot)
                if key in qlast:
                    p = ops[qlast[key]]
                    o.dprev = p.idx
                    o.dval = p.dval + 16
                else:
                    o.dval = 16
                o.dsem = key
                qlast[key] = o.idx
        for o in ops:
            for d in o.deps:
                if not ops[d].is_dma:
                    ops[d].signal = True
        final = list(final_wait_ops or [])
        for d in final:
            if not d.is_dma:
                d.signal = True
        sigc = {e: 0 for e in COMPUTE}
        for o in ops:
            if not o.is_dma and o.signal:
                sigc[o.eng] += 1
                o.sigval = sigc[o.eng]
        sems = {}
        es = self.es
        for e in COMPUTE:
            sems[e] = es.enter_context(nc.semaphore("s_" + e))
        for q in QUEUES:
            for s in range(min(self.n_dma_sems, qcount[q])):
                sems[(q, s)] = es.enter_context(nc.semaphore(f"d_{q}{s}"))
        by_eng = {e: [] for e in ("pe", "act", "dve", "pool", "sp")}
        for o in ops:
            by_eng[o.eng].append(o)
        self.stats = {e: len(v) for e, v in by_eng.items()}

        def semkey_val(d):
            p = ops[d]
            if p.is_dma:
                return p.dsem, p.dval
            return p.eng, p.sigval

        def run_engine(ename, handle):
            waited = {}
            for o in by_eng[ename]:
                need = {}
                for d in o.deps:
                    k, v = semkey_val(d)
                    if need.get(k, 0) < v:
                        need[k] = v
                if o.is_dma and o.dprev is not None:
                    k, v = semkey_val(o.dprev)
                    if need.get(k, 0) < v:
                        need[k] = v
                for k, v in need.items():
                    if waited.get(k, 0) < v:
                        handle.wait_ge(sems[k], v)
                        waited[k] = v
                ins = o.fn(handle)
                if o.is_dma:
                    ins.then_inc(sems[o.dsem], 16)
                elif o.signal:
                    ins.then_inc(sems[o.eng], 1)
            if ename == "sp":
                for d in final:
                    k, v = semkey_val(d.idx)
                    if waited.get(k, 0) < v:
                        handle.wait_ge(sems[k], v)
                        waited[k] = v

        with nc.Block() as block:
            block.sync(lambda e: run_engine("sp", e))
            if by_eng["pe"]:
                block.tensor(lambda e: run_engine("pe", e))
            if by_eng["act"]:
                block.scalar(lambda e: run_engine("act", e))
            if by_eng["dve"]:
                block.vector(lambda e: run_engine("dve", e))
            if by_eng["pool"]:
                block.gpsimd(lambda e: run_engine("pool", e))
        self.es.close()


def make_ident(P, dt=BF16, name="ident"):
    idf = P.sb(name + "_f", [128, 128], F32)
    bf = P.buf()
    P.pool(lambda e: e.memset(idf[:], 0.0), writes=[bf])
    P.pool(lambda e: e.affine_select(out=idf[:], in_=idf[:], pattern=[[-1, 128]], compare_op=ALU.not_equal,
                                     fill=1.0, base=0, channel_multiplier=1), reads=[bf], writes=[bf])
    if dt == F32:
        return idf, bf
    idb = P.sb(name, [128, 128], dt)
    bb = P.buf()
    P.dve(lambda e: e.tensor_copy(out=idb[:], in_=idf[:]), reads=[bf], writes=[bb])
    return idb, bb

EPS = 1e-6


def build_e1():
    nc = bass.Bass("TRN2", target_bir_lowering=False)
    P = Prog(nc)
    xt = P.din("xt", [19, 128, 1024], F32)
    mcols = P.din("mcols", [128, 2, 2, 8], F32)
    win = P.din("win", [1024, 2048], F32)
    cwd = P.din("cw", [128, 3, 4], F32)
    cs128d = P.din("cs128", [128, 256], F32)
    c256d = P.din("c256", [128, 2, 256], F32)
    ns256d = P.din("ns256", [128, 2, 256], F32)
    flagsd = P.din("flags", [128, 2], F32)
    bout = P.dout("bout", [2048, 1024], BF16)
    ycT = P.dout("ycT", [512, 2304], BF16)
    fcT = P.dout("fcT", [512, 256], BF16)

    winb = P.sb("winb", [128, 8, 2048], BF16)
    mc = P.sb("mc", [128, 2, 2, 8], F32)
    cw = P.sb("cwt", [128, 3, 4], F32)
    cs128 = P.sb("cs128b", [128, 256], BF16)
    c256 = P.sb("c256b", [128, 2, 256], BF16)
    ns256 = P.sb("ns256b", [128, 2, 256], BF16)
    flags = P.sb("flagst", [128, 2], F32)
    XS = [P.sb(f"xs{i}", [128, 1024], F32) for i in range(2)]
    junk = P.sb("junk", [128, 1024], BF16)
    XN = [P.sb(f"xn{i}", [128, 1024], BF16) for i in range(2)]
    ss = P.sb("ss", [128, 19], F32)
    rstd = P.sb("rstd", [128, 19], F32)
    HT = [P.sb(f"hT{i}", [128, 8, 512], BF16) for i in range(2)]
    AT = [P.sb(f"aT{i}", [128, 4, 512], BF16) for i in range(2)]
    cgt = P.sb("cgt", [128, 4, 512], F32)
    uh = P.sb("uh", [128, 4, 2], F32)
    BgT = P.sb("BgT", [128, 4, 2304], BF16)
    uT = P.sb("uT", [128, 4, 2050], BF16)
    uTc = P.sb("uTc", [128, 4, 258], BF16)
    BT = [P.sb(f"bt{i}", [128, 1024], BF16) for i in range(2)]
    bctx = P.sb("bctx", [128, 2, 1024], BF16)
    fct = P.sb("fct", [128, 4, 256], BF16)
    yct = P.sb("yct", [128, 4, 2304], BF16)
    T1 = [P.sb(f"t1_{i}", [128, 1024], F32) for i in range(2)]
    T2 = [P.sb(f"t2_{i}", [128, 1024], F32) for i in range(2)]
    ident, b_ident = make_ident(P)
    epsc = P.sb("epsc", [128, 1], F32)
    b_eps = P.buf()
    P.pool(lambda e: e.memset(epsc[:], EPS), writes=[b_eps])

    PT = [P.ps(f"pT{i}", [128, 8, 128], BF16) for i in range(2)]
    PJ = [P.ps(f"pj{i}", [128, 512], F32) for i in range(3)]
    BS = P.ps("bs", [128, 4, 256], F32)
    PF = P.ps("pf", [128, 512], F32)

    b_win, b_mc, b_cw, b_cs, b_c256, b_ns256, b_flags = P.bufs(7, "c")
    b_xs = P.bufs(2, "xs"); b_junk = P.buf(); b_xn = P.bufs(2, "xn")
    b_ss = P.bufs(19, "ss"); b_rstd = P.bufs(19, "rs")
    b_ht = P.bufs(2, "ht"); b_at = P.bufs(2, "at"); b_cgt = P.bufs(4, "cgt"); b_uh = P.buf()
    b_bg = P.bufs(6, "bg"); b_u = P.bufs(7, "u"); b_bt = P.bufs(2, "bt"); b_bctx = P.bufs(2, "bctx")
    b_fct = P.bufs(4, "fct"); b_yct = P.bufs(8, "yct"); b_t1 = P.bufs(2, "t1"); b_t2 = P.bufs(2, "t2")
    b_pt = P.bufs(2, "pt"); b_pj = P.bufs(3, "pj"); b_bs = P.buf(); b_pf = P.buf()
    b_uz = P.buf()

    P.dma("pool", lambda e: e.dma_start(out=winb[:], in_=win.rearrange("(k p) n -> p k n", p=128)), writes=[b_win])
    P.dma("sp", lambda e: e.dma_start(out=mc[:], in_=mcols), writes=[b_mc])
    P.dma("sp", lambda e: e.dma_start(out=cw[:], in_=cwd), writes=[b_cw])
    P.dma("sp", lambda e: e.dma_start(out=flags[:], in_=flagsd), writes=[b_flags])
    P.dma("pool", lambda e: e.dma_start(out=cs128[:], in_=cs128d), writes=[b_cs])
    P.dma("pool", lambda e: e.dma_start(out=c256[:], in_=c256d), writes=[b_c256])
    P.dma("pool", lambda e: e.dma_start(out=ns256[:], in_=ns256d), writes=[b_ns256])
    P.pool(lambda e: e.memset(uTc[:], 0.0), writes=[b_uz])

    groups = [([18], 0, "halo")] + [([4 * g + i for i in range(4)], 0, "lat") for g in range(4)] + [([16, 17], 1, "ctx")]
    outs = []
    ti = 0
    pj_i = 0
    for gi, (tiles, s, kind) in enumerate(groups):
        ncols = 128 * len(tiles)
        hT = HT[gi % 2]; bht = b_ht[gi % 2]
        aT = AT[gi % 2]; bat = b_at[gi % 2]
        for tt, t in enumerate(tiles):
            xs = XS[ti % 2]; bxs = b_xs[ti % 2]
            xn = XN[ti % 2]; bxn = b_xn[ti % 2]
            pT = PT[ti % 2]; bpt = b_pt[ti % 2]
            P.dma("sp", lambda e, xs=xs, t=t: e.dma_start(out=xs[:], in_=xt[t]), writes=[bxs])
            P.act(lambda e, xs=xs, t=t: e.activation(out=junk[:], in_=xs[:], func=AF.Square, accum_out=ss[:, t:t + 1]),
                  reads=[bxs], writes=[b_junk, b_ss[t]])
            P.act(lambda e, t=t: e.activation(out=rstd[:, t:t + 1], in_=ss[:, t:t + 1], func=AF.Sqrt, scale=1.0 / 1024, bias=epsc[:, 0:1]),
                  reads=[b_ss[t], b_eps], writes=[b_rstd[t]])
            P.dve(lambda e, t=t: e.reciprocal(out=rstd[:, t:t + 1], in_=rstd[:, t:t + 1]),
                  reads=[b_rstd[t]], writes=[b_rstd[t]])
            P.dve(lambda e, xs=xs, xn=xn, t=t: e.tensor_scalar(out=xn[:], in0=xs[:], scalar1=rstd[:, t:t + 1], scalar2=None,
                                                               op0=ALU.mult), reads=[bxs, b_rstd[t]], writes=[bxn])
            for k in range(8):
                P.pe(lambda e, k=k, xn=xn, pT=pT: e.transpose(out=pT[:, k, :], in_=xn[:, k * 128:(k + 1) * 128], identity=ident[:]),
                     reads=[bxn, b_ident], writes=[bpt])
            for k in range(8):
                dst = hT[:, k, tt * 128:(tt + 1) * 128]
                if k % 2 == 0:
                    P.act(lambda e, dst=dst, pT=pT, k=k, s=s: e.activation(out=dst, in_=pT[:, k, :], func=AF.Identity,
                                                                          scale=mc[:, s, 1, k:k + 1], bias=mc[:, s, 0, k:k + 1]),
                          reads=[bpt, b_mc], writes=[bht])
                else:
                    P.dve(lambda e, dst=dst, pT=pT, k=k, s=s: e.tensor_scalar(out=dst, in0=pT[:, k, :], scalar1=mc[:, s, 1, k:k + 1],
                                                                             scalar2=mc[:, s, 0, k:k + 1], op0=ALU.mult, op1=ALU.add),
                          reads=[bpt, b_mc], writes=[bht])
            ti += 1
        if kind == "lat":
            tok0 = tiles[0] * 128
            bbg = b_bg[gi - 1]; bu = b_u[gi - 1]
        elif kind == "ctx":
            tok0 = 0
            bbg = b_bg[4]; bu = b_u[4]
        nlist = range(16) if kind != "halo" else range(8, 16)
        for n in nlist:
            pj = PJ[pj_i % 3]; bpj = b_pj[pj_i % 3]; pj_i += 1
            for k in range(8):
                P.pe(lambda e, pj=pj, k=k, n=n, hT=hT, ncols=ncols: e.matmul(pj[:, 0:ncols], lhsT=winb[:, k, n * 128:(n + 1) * 128],
                                                                             rhs=hT[:, k, 0:ncols], start=(k == 0), stop=(k == 7)),
                     reads=[bht, b_win], writes=[bpj])
            if n < 4:
                P.act(lambda e, pj=pj, n=n, aT=aT, ncols=ncols: e.activation(out=aT[:, n, 0:ncols], in_=pj[:, 0:ncols], func=AF.Copy),
                      reads=[bpj], writes=[bat])
            elif n < 8:
                j = n - 4
                if kind == "lat":
                    dst = BgT[:, j, tok0:tok0 + ncols]
                else:
                    dst = BgT[:, j, 2048:2304]
                P.dve(lambda e, pj=pj, dst=dst, ncols=ncols: e.tensor_copy(out=dst, in_=pj[:, 0:ncols]), reads=[bpj], writes=[bbg])
            elif n < 12:
                j = n - 8
                P.act(lambda e, pj=pj, j=j, ncols=ncols: e.activation(out=cgt[:, j, 0:ncols], in_=pj[:, 0:ncols], func=AF.Copy),
                      reads=[bpj], writes=[b_cgt[j]])
            else:
                j = n - 12
                if kind == "halo":
                    P.dve(lambda e, pj=pj, j=j: e.tensor_tensor(out=uh[:, j, :], in0=pj[:, 0:2], in1=cgt[:, j, 0:2], op=ALU.mult),
                          reads=[bpj, b_cgt[j]], writes=[b_uh])
                    P.dve(lambda e, j=j: e.tensor_scalar(out=uT[:, j, 0:1], in0=uh[:, j, 0:1], scalar1=flags[:, 0:1], scalar2=None,
                                                         op0=ALU.mult), reads=[b_uh, b_flags], writes=[b_u[5]])
                    P.dve(lambda e, j=j: e.tensor_scalar(out=uT[:, j, 2049:2050], in0=uh[:, j, 1:2], scalar1=flags[:, 1:2], scalar2=None,
                                                         op0=ALU.mult), reads=[b_uh, b_flags], writes=[b_u[6]])
                else:
                    if kind == "lat":
                        dst = uT[:, j, 1 + tok0:1 + tok0 + ncols]
                        wr = [bu]
                    else:
                        dst = uTc[:, j, 1:257]
                        wr = [bu]

                    rd = [bpj, b_cgt[j]] + ([b_uz] if kind == "ctx" else [])
                    P.dve(lambda e, pj=pj, j=j, dst=dst, ncols=ncols: e.tensor_tensor(out=dst, in0=pj[:, 0:ncols], in1=cgt[:, j, 0:ncols],
                                                                                      op=ALU.mult), reads=rd, writes=wr)
        if kind == "halo":
            continue
        for tt, t in enumerate(tiles):
            for g in range(4):
                P.pe(lambda e, g=g, aT=aT, tt=tt: e.matmul(BS[:, g, :], lhsT=aT[:, g, tt * 128:(tt + 1) * 128], rhs=cs128[:, :],
                                                           start=True, stop=True), reads=[bat, b_cs], writes=[b_bs])
            if kind == "lat":
                bt = BT[t % 2]; bbt = b_bt[t % 2]
                P.act(lambda e, bt=bt: e.activation(out=bt[:], in_=BS[:].rearrange("p g c -> p (g c)"), func=AF.Copy),
                      reads=[b_bs], writes=[bbt])
                outs.append(P.dma("sp", lambda e, bt=bt, t=t: e.dma_start(out=bout[t * 128:(t + 1) * 128, :], in_=bt[:]), reads=[bbt]))
            else:
                P.act(lambda e, tt=tt: e.activation(out=bctx[:, tt, :], in_=BS[:].rearrange("p g c -> p (g c)"), func=AF.Copy),
                      reads=[b_bs], writes=[b_bctx[tt]])
        if kind == "ctx":
            for g in range(4):
                for tt in range(2):
                    P.pe(lambda e, g=g, tt=tt: e.matmul(PF[:, 0:256], lhsT=bctx[:, tt, g * 256:g * 256 + 128], rhs=c256[:, tt, :],
                                                        start=(tt == 0), stop=False), reads=[b_bctx[tt], b_c256], writes=[b_pf])
                    P.pe(lambda e, g=g, tt=tt: e.matmul(PF[:, 0:256], lhsT=bctx[:, tt, g * 256 + 128:g * 256 + 256], rhs=ns256[:, tt, :],
                                                        start=False, stop=(tt == 1)), reads=[b_bctx[tt], b_ns256], writes=[b_pf])
                P.act(lambda e, g=g: e.activation(out=fct[:, g, :], in_=PF[:, 0:256], func=AF.Copy), reads=[b_pf], writes=[b_fct[g]])
                outs.append(P.dma("sp", lambda e, g=g: e.dma_start(out=fcT[g * 128:(g + 1) * 128, :], in_=fct[:, g, :]), reads=[b_fct[g]]))

    allu = b_u[0:4] + [b_u[5], b_u[6]]
    allbg = b_bg[0:4]
    segs = [("lat", 0, 1024), ("lat", 1024, 1024), ("ctx", 0, 256)]
    ci = 0
    for j in range(4):
        for (kind, c0, w) in segs:
            en = "dve"
            t1 = T1[ci % 2]; t2 = T2[ci % 2]; bt1 = b_t1[ci % 2]; bt2 = b_t2[ci % 2]
            ci += 1
            if kind == "lat":
                src = uT; off = c0; ub = allu; bgs = BgT[:, j, c0:c0 + w]; bgb = allbg; ydst = yct[:, j, c0:c0 + w]
                byc = b_yct[j * 2 + (c0 // 1024)]
            else:
                src = uTc; off = 0; ub = [b_u[4], b_uz]; bgs = BgT[:, j, 2048:2304]; bgb = [b_bg[4]]; ydst = yct[:, j, 2048:2304]
                byc = b_yct[j * 2]
                byc = P.buf()
            P.op(en, lambda e, t1=t1, src=src, off=off, w=w, j=j: e.tensor_scalar(out=t1[:, 0:w], in0=src[:, j, off + 1:off + 1 + w],
                                                                                  scalar1=cw[:, 1, j:j + 1], scalar2=None, op0=ALU.mult),
                 reads=ub + [b_cw], writes=[bt1])
            P.op(en, lambda e, t1=t1, t2=t2, src=src, off=off, w=w, j=j: e.scalar_tensor_tensor(
                out=t2[:, 0:w], in0=src[:, j, off:off + w], scalar=cw[:, 0, j:j + 1], in1=t1[:, 0:w], op0=ALU.mult, op1=ALU.add),
                 reads=ub + [b_cw, bt1], writes=[bt2])
            P.op(en, lambda e, t1=t1, t2=t2, src=src, off=off, w=w, j=j: e.scalar_tensor_tensor(
                out=t1[:, 0:w], in0=src[:, j, off + 2:off + 2 + w], scalar=cw[:, 2, j:j + 1], in1=t2[:, 0:w], op0=ALU.mult, op1=ALU.add),
                 reads=ub + [b_cw, bt2], writes=[bt1])
            P.op(en, lambda e, t1=t1, w=w, bgs=bgs, ydst=ydst: e.tensor_tensor(out=ydst, in0=t1[:, 0:w], in1=bgs, op=ALU.mult),
                 reads=[bt1] + bgb, writes=[byc])
            if kind == "lat":
                outs.append(P.dma("sp", lambda e, j=j, c0=c0, w=w: e.dma_start(out=ycT[j * 128:(j + 1) * 128, c0:c0 + w],
                                                                            in_=yct[:, j, c0:c0 + w]), reads=[byc]))
            else:
                outs.append(P.dma("sp", lambda e, j=j: e.dma_start(out=ycT[j * 128:(j + 1) * 128, 2048:2304], in_=yct[:, j, 2048:2304]),
                                  reads=[byc]))
    P.emit(final_wait_ops=outs)
    return nc


def dft_consts():
    n = np.arange(128)
    ang = 2 * np.pi * np.outer(n, n) / 128
    C = np.cos(ang); S = np.sin(ang)
    cs128 = np.concatenate([C, S], 1) / np.sqrt(128)
    t = np.arange(256)
    a256 = 2 * np.pi * np.outer(t, t) / 256
    c256 = (np.cos(a256) / 16).reshape(2, 128, 256).transpose(1, 0, 2)
    ns256 = (-np.sin(a256) / 16).reshape(2, 128, 256).transpose(1, 0, 2)
    return dict(cs128=cs128.astype(np.float32), c256=np.ascontiguousarray(c256).astype(np.float32),
                ns256=np.ascontiguousarray(ns256).astype(np.float32), C=C, S=S)


def build_e2():
    nc = bass.Bass("TRN2", target_bir_lowering=False)
    P = Prog(nc)
    zin = P.din("zin", [128, 2, 64, 128], BF16)
    m1d = P.din("m1", [128, 256], F32)
    m2d = P.din("m2", [128, 256], F32)
    twd = P.din("tw", [128, 2, 128], F32)
    c2d = P.din("c2", [128, 2, 128], F32)
    FT = P.dout("FT", [64, 128, 128], BF16)

    z = P.sb("z", [128, 2, 64, 128], BF16)
    m1 = P.sb("m1b", [128, 256], BF16)
    m2 = P.sb("m2b", [128, 256], BF16)
    tw = P.sb("twt", [128, 2, 128], F32)
    c2 = P.sb("c2b", [128, 2, 128], BF16)
    YS = [P.sb(f"ys{i}", [128, 4, 256], F32) for i in range(2)]
    TA = [P.sb(f"ta{i}", [128, 4, 128], F32) for i in range(2)]
    TB = [P.sb(f"tb{i}", [128, 4, 128], F32) for i in range(2)]
    TC = [P.sb(f"tc{i}", [128, 4, 128], F32) for i in range(2)]
    TD = [P.sb(f"td{i}", [128, 4, 128], F32) for i in range(2)]
    ZC = [P.sb(f"zc{i}", [128, 4, 128], BF16) for i in range(2)]
    ZS = [P.sb(f"zs{i}", [128, 4, 128], BF16) for i in range(2)]
    GS = [P.sb(f"gs{i}", [128, 4, 128], BF16) for i in range(2)]
    PY = [P.ps(f"py{i}", [128, 4, 256], F32) for i in range(2)]
    PG = [P.ps(f"pg{i}", [128, 4, 128], F32) for i in range(2)]

    b_z, b_m1, b_m2, b_tw, b_c2 = P.bufs(5, "c")
    b_ys = P.bufs(2); b_ta = P.bufs(2); b_tb = P.bufs(2); b_tc = P.bufs(2); b_td = P.bufs(2)
    b_zc = P.bufs(2); b_zs = P.bufs(2); b_gs = P.bufs(2); b_py = P.bufs(2); b_pg = P.bufs(2)

    P.dma("sp", lambda e: e.dma_start(out=z[:], in_=zin), writes=[b_z])
    P.dma("pool", lambda e: e.dma_start(out=m1[:], in_=m1d), writes=[b_m1])
    P.dma("pool", lambda e: e.dma_start(out=m2[:], in_=m2d), writes=[b_m2])
    P.dma("pool", lambda e: e.dma_start(out=c2[:], in_=c2d), writes=[b_c2])
    P.dma("sp", lambda e: e.dma_start(out=tw[:], in_=twd), writes=[b_tw])
    outs = []
    for st in range(16):
        i = st % 2
        py = PY[i]; ys = YS[i]; ta = TA[i]; tb = TB[i]; tc_ = TC[i]; td = TD[i]; zc = ZC[i]; zs = ZS[i]; gs = GS[i]; pg = PG[i]
        for q in range(4):
            n = st * 4 + q
            P.pe(lambda e, py=py, q=q, n=n: e.matmul(py[:, q, :], lhsT=z[:, 0, n, :], rhs=m1[:, :], start=True, stop=False),
                 reads=[b_z, b_m1], writes=[b_py[i]])
            P.pe(lambda e, py=py, q=q, n=n: e.matmul(py[:, q, :], lhsT=z[:, 1, n, :], rhs=m2[:, :], start=False, stop=True),
                 reads=[b_z, b_m2], writes=[b_py[i]])
        P.act(lambda e, py=py, ys=ys: e.activation(out=ys[:], in_=py[:], func=AF.Copy), reads=[b_py[i]], writes=[b_ys[i]])
        tcos = tw[:, 0:1, :].to_broadcast([128, 4, 128])
        tsin = tw[:, 1:2, :].to_broadcast([128, 4, 128])
        P.dve(lambda e, ys=ys, ta=ta, tcos=tcos: e.tensor_tensor(out=ta[:], in0=ys[:, :, 0:128], in1=tcos, op=ALU.mult),
              reads=[b_ys[i], b_tw], writes=[b_ta[i]])
        P.dve(lambda e, ys=ys, tb=tb, tsin=tsin: e.tensor_tensor(out=tb[:], in0=ys[:, :, 128:256], in1=tsin, op=ALU.mult),
              reads=[b_ys[i], b_tw], writes=[b_tb[i]])
        P.dve(lambda e, ta=ta, tb=tb, zc=zc: e.tensor_tensor(out=zc[:], in0=ta[:], in1=tb[:], op=ALU.subtract),
              reads=[b_ta[i], b_tb[i]], writes=[b_zc[i]])
        P.pool(lambda e, ys=ys, tc_=tc_, tsin=tsin: e.tensor_tensor(out=tc_[:], in0=ys[:, :, 0:128], in1=tsin, op=ALU.mult),
               reads=[b_ys[i], b_tw], writes=[b_tc[i]])
        P.pool(lambda e, ys=ys, td=td, tcos=tcos: e.tensor_tensor(out=td[:], in0=ys[:, :, 128:256], in1=tcos, op=ALU.mult),
               reads=[b_ys[i], b_tw], writes=[b_td[i]])
        P.pool(lambda e, tc_=tc_, td=td, zs=zs: e.tensor_tensor(out=zs[:], in0=tc_[:], in1=td[:], op=ALU.add),
               reads=[b_tc[i], b_td[i]], writes=[b_zs[i]])
        P.pe(lambda e, pg=pg, zc=zc: e.matmul(pg[:].rearrange("p a b -> p (a b)"), lhsT=c2[:, 0, :], rhs=zc[:].rearrange("p a b -> p (a b)"),
                                              start=True, stop=False), reads=[b_zc[i], b_c2], writes=[b_pg[i]])
        P.pe(lambda e, pg=pg, zs=zs: e.matmul(pg[:].rearrange("p a b -> p (a b)"), lhsT=c2[:, 1, :], rhs=zs[:].rearrange("p a b -> p (a b)"),
                                              start=False, stop=True), reads=[b_zs[i], b_c2], writes=[b_pg[i]])
        P.act(lambda e, pg=pg, gs=gs: e.activation(out=gs[:], in_=pg[:], func=AF.Copy), reads=[b_pg[i]], writes=[b_gs[i]])
        outs.append(P.dma("sp", lambda e, gs=gs, st=st: e.dma_start(out=FT[st * 4:(st + 1) * 4].rearrange("n a b -> a n b"), in_=gs[:]),
                          reads=[b_gs[i]]))
    P.emit(final_wait_ops=outs)
    return nc


def fft_consts():
    n = np.arange(128)
    ang = 2 * np.pi * np.outer(n, n) / 128
    C = np.cos(ang) / np.sqrt(128); S = np.sin(ang) / np.sqrt(128)
    m1 = np.concatenate([C, S], 1); m2 = np.concatenate([-S, C], 1)
    ta = 2 * np.pi * np.outer(n, n) / 16384
    tw = np.stack([np.cos(ta), np.sin(ta)], 1)
    c2 = np.stack([C, -S], 1)
    f = lambda a: np.ascontiguousarray(a).astype(np.float32)
    return dict(m1=f(m1), m2=f(m2), tw=f(tw), c2=f(c2))


def build_e3():
    nc = bass.Bass("TRN2", target_bir_lowering=False)
    P = Prog(nc)
    yT = P.din("yT", [1024, 2304], BF16)
    xt = P.din("xt", [18, 128, 1024], F32)
    g1d = P.din("g1", [128, 2, 1024], F32)
    wd = P.din("w", [1024, 1024], F32)
    xo = P.dout("xo", [18, 128, 1024], F32)
    ysb = P.sb("ysb", [128, 8, 2304], BF16)
    wb = P.sb("wb", [128, 8, 1024], BF16)
    g1 = P.sb("g1t", [128, 2, 1024], F32)
    XS = [P.sb(f"xs{i}", [128, 1024], F32) for i in range(2)]
    TM = [P.sb(f"tm{i}", [128, 1024], F32) for i in range(2)]
    PY = [P.ps(f"py{i}", [128, 1024], F32) for i in range(2)]
    b_y, b_w, b_g = P.bufs(3); b_xs = P.bufs(2); b_tm = P.bufs(2); b_py = P.bufs(2)
    P.dma("sp", lambda e: e.dma_start(out=ysb[:], in_=yT.rearrange("(k p) t -> p k t", p=128)), writes=[b_y])
    P.dma("pool", lambda e: e.dma_start(out=wb[:], in_=wd.rearrange("(k p) n -> p k n", p=128)), writes=[b_w])
    P.dma("sp", lambda e: e.dma_start(out=g1[:], in_=g1d), writes=[b_g])
    outs = []
    for t in range(18):
        i = t % 2
        s = 0 if t < 16 else 1
        xs = XS[i]; tm = TM[i]; py = PY[i]
        P.dma("sp", lambda e, xs=xs, t=t: e.dma_start(out=xs[:], in_=xt[t]), writes=[b_xs[i]])
        for h in range(2):
            for k in range(8):
                P.pe(lambda e, py=py, h=h, k=k, t=t: e.matmul(py[:, h * 512:(h + 1) * 512], lhsT=ysb[:, k, t * 128:(t + 1) * 128],
                                                             rhs=wb[:, k, h * 512:(h + 1) * 512], start=(k == 0), stop=(k == 7)),
                     reads=[b_y, b_w], writes=[b_py[i]])
        P.dve(lambda e, py=py, tm=tm, s=s: e.tensor_tensor(out=tm[:], in0=py[:], in1=g1[:, s, :], op=ALU.mult),
              reads=[b_py[i], b_g], writes=[b_tm[i]])
        P.pool(lambda e, tm=tm, xs=xs: e.tensor_tensor(out=tm[:], in0=tm[:], in1=xs[:], op=ALU.add),
               reads=[b_tm[i], b_xs[i]], writes=[b_tm[i]])
        outs.append(P.dma("sp", lambda e, tm=tm, t=t: e.dma_start(out=xo[t], in_=tm[:]), reads=[b_tm[i]]))
    P.emit(final_wait_ops=outs)
    return nc


def build_moe(NEXP=32):
    nc = bass.Bass("TRN2", target_bir_lowering=False)
    P = Prog(nc)
    xt = P.din("xt", [18, 128, 1024], F32)
    rowsd = P.din("rows", [2, 3, 128, 1024], F32)
    wrd = P.din("wr", [1024, 36], F32)
    brd = P.din("br", [128, 36], F32)
    NE = max(NEXP, 1)
    w1d = P.din("w1", [NE, 1024, 512], F32)
    w3d = P.din("w3", [NE, 1024, 512], F32)
    w2d = P.din("w2", [NE, 512, 1024], F32)
    xo = P.dout("xo", [18, 128, 1024], F32)

    acc = P.sb("acc", [128, 18, 1024], F32)
    hTb = P.sb("hTb", [128, 8, 2304], BF16)
    W1 = [P.sb(f"w1b{i}", [128, 8, 512], BF16) for i in range(2)]
    W3 = [P.sb(f"w3b{i}", [128, 8, 512], BF16) for i in range(2)]
    w2b = P.sb("w2b", [128, 4, 1024], BF16)
    rows = P.sb("rowst", [128, 2, 1024], F32)
    xs = P.sb("xs", [128, 1024], F32)
    h32 = P.sb("h32", [128, 1024], F32)
    HTE = [P.sb(f"hTe{i}", [128, 4, 512], BF16) for i in range(2)]
    junk = P.sb("junk", [128, 1024], BF16)
    G = P.sb("G", [128, 18, 32], F32)
    wrt = P.sb("wrt", [128, 8, 36], F32)
    brt = P.sb("brt", [128, 36], F32)
    wrh = P.sb("wrh", [128, 8, 36], BF16)
    wrl = P.sb("wrl", [128, 8, 36], BF16)
    hhi = P.sb("hhi", [128, 1024], BF16)
    hlo = P.sb("hlo", [128, 1024], BF16)
    hTlo = P.sb("hTlo", [128, 8, 128], BF16)
    identb = P.sb("identb", [128, 128], BF16)
    ones = P.sb("ones", [128, 128], F32)
    DG = [P.sb(f"dg{i}", [128, 128], F32) for i in range(2)]
    sm = P.sb("sm", [128, 64], F32)
    lg = P.sb("lg", [128, 36], F32)
    ss = P.sb("ss", [128, 18], F32)
    rstd = P.sb("rstd", [128, 18], F32)
    epsc = P.sb("epsc", [128, 1], F32)
    identf, b_identf = make_ident(P, F32, "identf")

    PT2 = P.ps("pt2", [128, 1024], F32)
    PTB = P.ps("ptb", [128, 8, 128], BF16)
    PAB0 = P.ps("pab0", [128, 1024], F32)
    PRG = P.ps("prg", [128, 512], F32)
    PY = P.ps("py", [128, 1024], F32)

    b_acc = P.bufs(18, "acc"); b_htb = P.bufs(18, "htb"); b_w1 = P.bufs(2); b_w3 = P.bufs(2); b_w2 = P.buf()
    b_rows = P.buf(); b_xs = P.buf(); b_h32 = P.buf(); b_ht32 = P.buf(); b_hte = P.bufs(2); b_junk = P.buf()
    b_G = P.bufs(18, "G"); b_wr = P.buf(); b_br = P.buf(); b_ones = P.buf(); b_dg = P.bufs(2); b_sm = P.buf(); b_lg = P.buf()
    b_ss = P.bufs(18); b_rstd = P.bufs(18); b_eps = P.buf()
    b_ptb = P.buf(); b_hhi = P.buf(); b_hlo = P.buf(); b_htlo = P.buf(); b_wrh = P.buf(); b_wrl = P.buf(); b_idb = P.buf()
    b_pt2 = P.buf(); b_pab0 = P.buf(); b_prg = P.buf(); b_py = P.buf()

    P.pool(lambda e: e.memset(epsc[:], EPS), writes=[b_eps])
    P.pool(lambda e: e.memset(ones[:], 1.0), writes=[b_ones])
    P.dma("sp", lambda e: e.dma_start(out=wrt[:], in_=wrd.rearrange("(k p) n -> p k n", p=128)), writes=[b_wr])
    P.dma("sp", lambda e: e.dma_start(out=brt[:], in_=brd), writes=[b_br])
    P.dve(lambda e: e.tensor_copy(out=identb[:], in_=identf[:]), reads=[b_identf], writes=[b_idb])
    P.dve(lambda e: e.tensor_copy(out=wrh[:], in_=wrt[:]), reads=[b_wr], writes=[b_wrh])
    P.dve(lambda e: e.tensor_tensor(out=wrl[:], in0=wrt[:], in1=wrh[:], op=ALU.subtract), reads=[b_wr, b_wrh], writes=[b_wrl])

    def load_w(e_):
        i = e_ % 2
        P.dma("pool", lambda e, i=i, e_=e_: e.dma_start(out=W1[i][:], in_=w1d[e_].rearrange("(k p) f -> p k f", p=128)), writes=[b_w1[i]])
        P.dma("pool", lambda e, i=i, e_=e_: e.dma_start(out=W3[i][:], in_=w3d[e_].rearrange("(k p) f -> p k f", p=128)), writes=[b_w3[i]])

    def load_w2(e_):
        P.dma("pool", lambda e, e_=e_: e.dma_start(out=w2b[:], in_=w2d[e_].rearrange("(k p) d -> p k d", p=128)), writes=[b_w2])

    load_w(0)
    load_w2(0)

    def sc(i):
        return sm[:, i:i + 1]

    for t in range(18):
        s = 0 if t < 16 else 1
        if t == 0 or t == 16:
            P.dma("sp", lambda e, s=s: e.dma_start(out=rows[:, 0, :], in_=rowsd[s, 0]), writes=[b_rows])
            P.dma("sp", lambda e, s=s: e.dma_start(out=rows[:, 1, :], in_=rowsd[s, 1]), writes=[b_rows])
        P.dma("sp", lambda e, t=t: e.dma_start(out=xs[:], in_=xt[t]), writes=[b_xs])
        P.act(lambda e, t=t: e.activation(out=junk[:], in_=xs[:], func=AF.Square, accum_out=ss[:, t:t + 1]),
              reads=[b_xs], writes=[b_junk, b_ss[t]])
        P.act(lambda e, t=t: e.activation(out=rstd[:, t:t + 1], in_=ss[:, t:t + 1], func=AF.Sqrt, scale=1.0 / 1024, bias=epsc[:, 0:1]),
              reads=[b_ss[t], b_eps], writes=[b_rstd[t]])
        P.dve(lambda e, t=t: e.reciprocal(out=rstd[:, t:t + 1], in_=rstd[:, t:t + 1]), reads=[b_rstd[t]], writes=[b_rstd[t]])
        P.dve(lambda e, t=t: e.scalar_tensor_tensor(out=h32[:], in0=xs[:], scalar=rstd[:, t:t + 1], in1=rows[:, 0, :],
                                                    op0=ALU.mult, op1=ALU.mult), reads=[b_xs, b_rstd[t], b_rows], writes=[b_h32])
        P.pool(lambda e: e.tensor_tensor(out=h32[:], in0=h32[:], in1=rows[:, 1, :], op=ALU.add), reads=[b_h32, b_rows], writes=[b_h32])
        P.dve(lambda e: e.tensor_copy(out=hhi[:], in_=h32[:]), reads=[b_h32], writes=[b_hhi])
        P.dve(lambda e: e.tensor_tensor(out=hlo[:], in0=h32[:], in1=hhi[:], op=ALU.subtract), reads=[b_h32, b_hhi], writes=[b_hlo])
        for k in range(8):
            P.pe(lambda e, k=k: e.transpose(out=PTB[:, k, :], in_=hhi[:, k * 128:(k + 1) * 128], identity=identb[:]),
                 reads=[b_hhi, b_idb], writes=[b_ptb])
        P.dve(lambda e, t=t: e.tensor_copy(out=hTb[:, :, t * 128:(t + 1) * 128], in_=PTB[:]), reads=[b_ptb], writes=[b_htb[t]])
        for k in range(8):
            P.pe(lambda e, k=k: e.transpose(out=PTB[:, k, :], in_=hlo[:, k * 128:(k + 1) * 128], identity=identb[:]),
                 reads=[b_hlo, b_idb], writes=[b_ptb])
        P.act(lambda e: e.activation(out=hTlo[:], in_=PTB[:], func=AF.Copy), reads=[b_ptb], writes=[b_htlo])
        first = True
        for k in range(8):
            for (lh, bl, rw, bw) in ((0, b_htb[t], wrh, b_wrh), (0, b_htb[t], wrl, b_wrl), (1, b_htlo, wrh, b_wrh)):
                lhs = hTb[:, k, t * 128:(t + 1) * 128] if lh == 0 else hTlo[:, k, :]
                last = (k == 7 and lh == 1)
                P.pe(lambda e, lhs=lhs, rw=rw, k=k, first=first, last=last: e.matmul(PRG[:, 0:36], lhsT=lhs, rhs=rw[:, k, :], start=first, stop=last),
                     reads=[bl, bw], writes=[b_prg])
                first = False
        R = [b_sm, b_lg]
        P.dve(lambda e: e.tensor_tensor(out=lg[:], in0=PRG[:, 0:36], in1=brt[:], op=ALU.add), reads=[b_prg, b_br], writes=[b_lg])
        P.dve(lambda e: e.tensor_reduce(out=sc(0), in_=lg[:, 0:4], axis=AX.X, op=ALU.max), reads=R, writes=[b_sm])
        P.dve(lambda e: e.tensor_scalar(out=sc(1), in0=sc(0), scalar1=-1.0, scalar2=None, op0=ALU.mult), reads=R, writes=[b_sm])
        P.act(lambda e: e.activation(out=sm[:, 8:12], in_=lg[:, 0:4], func=AF.Exp, bias=sc(1), scale=1.0, accum_out=sc(2)),
              reads=R, writes=[b_sm])
        P.dve(lambda e: e.reciprocal(out=sc(3), in_=sc(2)), reads=R, writes=[b_sm])
        P.dve(lambda e: e.tensor_scalar(out=sm[:, 12:16], in0=lg[:, 0:4], scalar1=sc(0), scalar2=None, op0=ALU.is_equal),
              reads=R, writes=[b_sm])
        P.dve(lambda e: e.tensor_scalar(out=sm[:, 16:24], in0=lg[:, 4:12], scalar1=sm[:, 12:13], scalar2=None, op0=ALU.mult),
              reads=R, writes=[b_sm])
        for g in range(1, 4):
            P.dve(lambda e, g=g: e.scalar_tensor_tensor(out=sm[:, 16:24], in0=lg[:, 4 + 8 * g:12 + 8 * g], scalar=sm[:, 12 + g:13 + g],
                                                        in1=sm[:, 16:24], op0=ALU.mult, op1=ALU.add), reads=R, writes=[b_sm])
        P.dve(lambda e: e.tensor_reduce(out=sc(4), in_=sm[:, 16:24], axis=AX.X, op=ALU.max), reads=R, writes=[b_sm])
        P.dve(lambda e: e.tensor_scalar(out=sm[:, 24:32], in0=sm[:, 16:24], scalar1=sc(4), scalar2=None, op0=ALU.is_equal),
              reads=R, writes=[b_sm])
        P.dve(lambda e: e.scalar_tensor_tensor(out=sm[:, 32:40], in0=sm[:, 24:32], scalar=-1e30, in1=sm[:, 16:24],
                                               op0=ALU.mult, op1=ALU.add), reads=R, writes=[b_sm])
        P.dve(lambda e: e.tensor_reduce(out=sc(5), in_=sm[:, 32:40], axis=AX.X, op=ALU.max), reads=R, writes=[b_sm])
        P.dve(lambda e: e.tensor_scalar(out=sm[:, 40:48], in0=sm[:, 32:40], scalar1=sc(5), scalar2=None, op0=ALU.is_equal),
              reads=R, writes=[b_sm])
        P.dve(lambda e: e.tensor_tensor(out=sc(6), in0=sc(5), in1=sc(4), op=ALU.subtract), reads=R, writes=[b_sm])
        P.act(lambda e: e.activation(out=sc(7), in_=sc(6), func=AF.Exp), reads=R, writes=[b_sm])
        P.dve(lambda e: e.tensor_scalar(out=sc(7), in0=sc(7), scalar1=1.0, scalar2=None, op0=ALU.add), reads=R, writes=[b_sm])
        P.dve(lambda e: e.reciprocal(out=sc(7), in_=sc(7)), reads=R, writes=[b_sm])
        P.dve(lambda e: e.tensor_tensor(out=sc(48), in0=sc(7), in1=sc(3), op=ALU.mult), reads=R, writes=[b_sm])
        P.dve(lambda e: e.tensor_tensor(out=sc(49), in0=sc(3), in1=sc(48), op=ALU.subtract), reads=R, writes=[b_sm])
        P.dve(lambda e: e.tensor_scalar(out=sm[:, 50:58], in0=sm[:, 24:32], scalar1=sc(48), scalar2=None, op0=ALU.mult),
              reads=R, writes=[b_sm])
        P.dve(lambda e: e.scalar_tensor_tensor(out=sm[:, 50:58], in0=sm[:, 40:48], scalar=sc(49), in1=sm[:, 50:58],
                                               op0=ALU.mult, op1=ALU.add), reads=R, writes=[b_sm])
        for g in range(4):
            P.dve(lambda e, g=g, t=t: e.tensor_scalar(out=G[:, t, 8 * g:8 * g + 8], in0=sm[:, 50:58], scalar1=sm[:, 12 + g:13 + g],
                                                      scalar2=None, op0=ALU.mult), reads=R, writes=[b_G[t]])

    groups = [(list(range(4 * g, 4 * g + 4))) for g in range(4)] + [[16, 17]]
    pair = 0
    di = 0
    for ex in range(NEXP):
        i = ex % 2
        if ex + 1 < NEXP:
            load_w(ex + 1)
        for tiles in groups:
            n = 128 * len(tiles)
            c0 = tiles[0] * 128
            hte = HTE[pair % 2]; bhte = b_hte[pair % 2]
            for f in range(4):
                pab, bpab = (PAB0, b_pab0) if pair % 2 == 0 else (PT2, b_pt2)
                pair += 1
                for k in range(8):
                    P.pe(lambda e, pab=pab, k=k, f=f, i=i, c0=c0, n=n: e.matmul(pab[:, 0:n], lhsT=W1[i][:, k, f * 128:(f + 1) * 128],
                                                                                rhs=hTb[:, k, c0:c0 + n], start=(k == 0), stop=(k == 7)),
                         reads=[b_w1[i]] + [b_htb[t] for t in tiles], writes=[bpab])
                for k in range(8):
                    P.pe(lambda e, pab=pab, k=k, f=f, i=i, c0=c0, n=n: e.matmul(pab[:, 512:512 + n], lhsT=W3[i][:, k, f * 128:(f + 1) * 128],
                                                                                rhs=hTb[:, k, c0:c0 + n], start=(k == 0), stop=(k == 7)),
                         reads=[b_w3[i]] + [b_htb[t] for t in tiles], writes=[bpab])
                half = (f % 2) * 512
                P.act(lambda e, pab=pab, n=n, half=half: e.activation(out=xs[:, half:half + n], in_=pab[:, 0:n], func=AF.Silu),
                      reads=[bpab], writes=[b_xs])
                P.dve(lambda e, pab=pab, n=n, half=half, hte=hte, f=f: e.tensor_tensor(out=hte[:, f, 0:n], in0=pab[:, 512:512 + n],
                                                                                      in1=xs[:, half:half + n], op=ALU.mult),
                      reads=[bpab, b_xs], writes=[bhte])
            pair += 0
            for tt, t in enumerate(tiles):
                for h in range(2):
                    for f in range(4):
                        P.pe(lambda e, h=h, f=f, tt=tt, hte=hte: e.matmul(PY[:, h * 512:(h + 1) * 512], lhsT=hte[:, f, tt * 128:(tt + 1) * 128],
                                                                          rhs=w2b[:, f, h * 512:(h + 1) * 512], start=(f == 0), stop=(f == 3)),
                             reads=[bhte, b_w2], writes=[b_py])
                if ex == 0:
                    P.dve(lambda e, t=t, ex=ex: e.tensor_scalar(out=acc[:, t, :], in0=PY[:], scalar1=G[:, t, ex:ex + 1], scalar2=None, op0=ALU.mult),
                          reads=[b_py, b_G[t]], writes=[b_acc[t]])
                else:
                    P.dve(lambda e, t=t, ex=ex: e.scalar_tensor_tensor(out=acc[:, t, :], in0=PY[:], scalar=G[:, t, ex:ex + 1], in1=acc[:, t, :],
                                                                       op0=ALU.mult, op1=ALU.add),
                          reads=[b_py, b_acc[t], b_G[t]], writes=[b_acc[t]])
        if ex + 1 < NEXP:
            load_w2(ex + 1)

    outs = []
    for t in range(18):
        s = 0 if t < 16 else 1
        if t == 0 or t == 16:
            P.dma("sp", lambda e, s=s: e.dma_start(out=rows[:, 0, :], in_=rowsd[s, 2]), writes=[b_rows])
        P.dma("sp", lambda e, t=t: e.dma_start(out=xs[:], in_=xt[t]), writes=[b_xs])
        P.dve(lambda e, t=t: e.tensor_tensor(out=acc[:, t, :], in0=acc[:, t, :], in1=rows[:, 0, :], op=ALU.mult),
              reads=[b_acc[t], b_rows], writes=[b_acc[t]])
        P.pool(lambda e, t=t: e.tensor_tensor(out=acc[:, t, :], in0=acc[:, t, :], in1=xs[:], op=ALU.add),
               reads=[b_acc[t], b_xs], writes=[b_acc[t]])
        outs.append(P.dma("sp", lambda e, t=t: e.dma_start(out=xo[t], in_=acc[:, t, :]), reads=[b_acc[t]]))
    P.emit(final_wait_ops=outs)
    return nc


def build_att():
    nc = bass.Bass("TRN2", target_bir_lowering=False)
    P = Prog(nc)
    xt = P.din("xt", [20, 128, 1024], F32)
    mcols = P.din("mcols", [128, 2, 2, 8], F32)
    wqkv = P.din("wqkv", [1024, 1536], F32)
    gqd = P.din("gq", [128, 64], F32)
    gkd = P.din("gk", [128, 64], F32)
    roped = P.din("rope", [18, 128, 2, 32], F32)
    sinkd = P.din("sink", [128, 16], F32)
    maskd = P.din("masks", [128, 4, 128], F32)
    OT = P.dout("OT", [1024, 2304], BF16)

    wb = P.sb("wb", [128, 8, 1536], BF16)
    mc = P.sb("mc", [128, 2, 2, 8], F32)
    gq = P.sb("gqt", [128, 64], F32)
    gk = P.sb("gkt", [128, 64], F32)
    rope = P.sb("ropet", [128, 18, 2, 32], F32)
    esink = P.sb("esink", [128, 16], F32)
    masks = P.sb("maskst", [128, 4, 128], BF16)
    epsc = P.sb("epsc", [128, 1], F32)
    xs = P.sb("xs", [128, 1024], F32)
    junk = P.sb("junk", [128, 1024], BF16)
    xn = P.sb("xn", [128, 1024], BF16)
    ss = P.sb("ss", [128, 20], F32)
    rstd = P.sb("rstd", [128, 20], F32)
    hT = P.sb("hT", [128, 8, 128], BF16)
    qf = P.sb("qf", [128, 1024], F32)
    kf = P.sb("kf", [128, 256], F32)
    tmp = P.sb("tmp", [128, 1024], F32)
    tmp2 = P.sb("tmp2", [128, 512], F32)
    tmp3 = P.sb("tmp3", [128, 512], F32)
    sq = P.sb("sq", [128, 20], F32)
    qr = P.sb("qr", [128, 1024], BF16)
    kr = P.sb("kr", [128, 256], BF16)
    QT = P.sb("QT", [64, 18, 16, 128], BF16)
    KT = P.sb("KT", [64, 20, 4, 128], BF16)
    VX = P.sb("VX", [128, 20, 4, 65], BF16)
    PTS = [P.sb(f"pts{i}", [128, 5, 512], BF16) for i in range(2)]
    den = P.sb("den", [128, 8], F32)
    osb = P.sb("osb", [128, 1024], BF16)
    ots = P.sb("ots", [128, 8, 128], BF16)
    ident, b_ident = make_ident(P)

    b_w, b_mc, b_gq, b_gk, b_rope, b_esink, b_masks, b_eps = P.bufs(8)
    b_xs, b_junk, b_xn, b_hT, b_qf, b_kf, b_tmp, b_tmp2, b_tmp3, b_sq, b_qr, b_kr = P.bufs(12)
    b_ss = P.bufs(20); b_rstd = P.bufs(20)
    b_QT = P.bufs(18); b_KT = P.bufs(20); b_VX = P.bufs(20); b_vone = P.buf()
    b_pts = P.bufs(2); b_den = P.buf(); b_osb = P.buf(); b_ots = P.buf()
    b_phase = P.buf()

    P.dma("pool", lambda e: e.dma_start(out=wb[:], in_=wqkv.rearrange("(k p) n -> p k n", p=128)), writes=[b_w])
    P.dma("sp", lambda e: e.dma_start(out=mc[:], in_=mcols), writes=[b_mc])
    P.dma("sp", lambda e: e.dma_start(out=gq[:], in_=gqd), writes=[b_gq])
    P.dma("sp", lambda e: e.dma_start(out=gk[:], in_=gkd), writes=[b_gk])
    P.dma("sp", lambda e: e.dma_start(out=rope[:], in_=roped.rearrange("t p a i -> p t a i")), writes=[b_rope])
    P.dma("sp", lambda e: e.dma_start(out=esink[:], in_=sinkd), writes=[b_esink])
    P.dma("pool", lambda e: e.dma_start(out=masks[:], in_=maskd), writes=[b_masks])
    P.pool(lambda e: e.memset(epsc[:], EPS), writes=[b_eps])
    P.act(lambda e: e.activation(out=esink[:], in_=esink[:], func=AF.Exp), reads=[b_esink], writes=[b_esink])
    P.dve(lambda e: e.tensor_scalar(out=gq[:], in0=gq[:], scalar1=0.125, scalar2=None, op0=ALU.mult), reads=[b_gq], writes=[b_gq])
    P.pool(lambda e: e.memset(VX[:], 1.0), writes=[b_vone])

    ps1 = ExitStack()
    PTx = ps1.enter_context(nc.psum_tensor("ptx", [128, 8, 128], BF16))
    PQ = ps1.enter_context(nc.psum_tensor("pq", [128, 1024], F32))
    PKV = ps1.enter_context(nc.psum_tensor("pkv", [128, 512], F32))
    PTq = ps1.enter_context(nc.psum_tensor("ptq", [64, 16, 128], BF16))
    PTk = ps1.enter_context(nc.psum_tensor("ptk", [64, 4, 128], BF16))
    b_ptx, b_pq, b_pkv, b_ptq, b_ptk = P.bufs(5)

    def qknorm_rope(src, H, gain, bgain, dst, bdst, t, do_rope, bsrc):
        W = H * 64
        P.dve(lambda e: e.tensor_tensor(out=tmp[:, 0:W], in0=src[:, 0:W], in1=src[:, 0:W], op=ALU.mult), reads=[bsrc], writes=[b_tmp])
        P.dve(lambda e: e.tensor_reduce(out=sq[:, 0:H], in_=tmp[:, 0:W].rearrange("p (h d) -> p h d", d=64), axis=AX.X, op=ALU.add),
              reads=[b_tmp], writes=[b_sq])
        P.act(lambda e: e.activation(out=sq[:, 0:H], in_=sq[:, 0:H], func=AF.Sqrt, scale=1.0 / 64, bias=epsc[:, 0:1]),
              reads=[b_sq, b_eps], writes=[b_sq])
        P.dve(lambda e: e.reciprocal(out=sq[:, 0:H], in_=sq[:, 0:H]), reads=[b_sq], writes=[b_sq])
        s3 = src[:, 0:W].rearrange("p (h d) -> p h d", d=64)
        t3 = tmp[:, 0:W].rearrange("p (h d) -> p h d", d=64)
        P.dve(lambda e: e.tensor_tensor(out=t3, in0=s3, in1=sq[:, 0:H].unsqueeze(2).to_broadcast([128, H, 64]), op=ALU.mult),
              reads=[bsrc, b_sq], writes=[b_tmp])
        if not do_rope:
            d3 = dst[:, 0:W].rearrange("p (h d) -> p h d", d=64)
            P.dve(lambda e: e.tensor_tensor(out=d3, in0=t3, in1=gain[:, :].unsqueeze(1).to_broadcast([128, H, 64]), op=ALU.mult),
                  reads=[b_tmp, bgain], writes=[bdst])
            return
        P.dve(lambda e: e.tensor_tensor(out=t3, in0=t3, in1=gain[:, :].unsqueeze(1).to_broadcast([128, H, 64]), op=ALU.mult),
              reads=[b_tmp, bgain], writes=[b_tmp])
        x5 = tmp[:, 0:W].rearrange("p (h a f i) -> p h a f i", a=2, f=2, i=16)
        d5 = dst[:, 0:W].rearrange("p (h a f i) -> p h a f i", a=2, f=2, i=16)
        x1 = x5[:, :, :, 0, :]; x2 = x5[:, :, :, 1, :]
        cs = rope[:, t, 0, :].rearrange("p (a i) -> p a i", a=2).unsqueeze(1).to_broadcast([128, H, 2, 16])
        sn = rope[:, t, 1, :].rearrange("p (a i) -> p a i", a=2).unsqueeze(1).to_broadcast([128, H, 2, 16])
        n = H * 32
        a4 = tmp2[:, 0:n].rearrange("p (h a i) -> p h a i", a=2, i=16)
        b4 = tmp3[:, 0:n].rearrange("p (h a i) -> p h a i", a=2, i=16)
        P.dve(lambda e: e.tensor_tensor(out=a4, in0=x1, in1=cs, op=ALU.mult), reads=[b_tmp, b_rope], writes=[b_tmp2])
        P.dve(lambda e: e.tensor_tensor(out=b4, in0=x2, in1=sn, op=ALU.mult), reads=[b_tmp, b_rope], writes=[b_tmp3])
        P.dve(lambda e: e.tensor_tensor(out=d5[:, :, :, 0, :], in0=a4, in1=b4, op=ALU.subtract), reads=[b_tmp2, b_tmp3], writes=[bdst])
        P.dve(lambda e: e.tensor_tensor(out=a4, in0=x1, in1=sn, op=ALU.mult), reads=[b_tmp, b_rope, bdst], writes=[b_tmp2])
        P.dve(lambda e: e.tensor_tensor(out=b4, in0=x2, in1=cs, op=ALU.mult), reads=[b_tmp, b_rope, bdst], writes=[b_tmp3])
        P.dve(lambda e: e.tensor_tensor(out=d5[:, :, :, 1, :], in0=a4, in1=b4, op=ALU.add), reads=[b_tmp2, b_tmp3], writes=[bdst])

    qidx = {}
    for t in range(20):
        s = 0 if t < 18 else 1
        is_q = (1 <= t <= 16) or t >= 18
        do_rope = t < 18
        P.dma("sp", lambda e, t=t: e.dma_start(out=xs[:], in_=xt[t]), writes=[b_xs])
        P.act(lambda e, t=t: e.activation(out=junk[:], in_=xs[:], func=AF.Square, accum_out=ss[:, t:t + 1]),
              reads=[b_xs], writes=[b_junk, b_ss[t]])
        P.act(lambda e, t=t: e.activation(out=rstd[:, t:t + 1], in_=ss[:, t:t + 1], func=AF.Sqrt, scale=1.0 / 1024, bias=epsc[:, 0:1]),
              reads=[b_ss[t], b_eps], writes=[b_rstd[t]])
        P.dve(lambda e, t=t: e.reciprocal(out=rstd[:, t:t + 1], in_=rstd[:, t:t + 1]), reads=[b_rstd[t]], writes=[b_rstd[t]])
        P.dve(lambda e, t=t: e.tensor_scalar(out=xn[:], in0=xs[:], scalar1=rstd[:, t:t + 1], scalar2=None, op0=ALU.mult),
              reads=[b_xs, b_rstd[t]], writes=[b_xn])
        for k in range(8):
            P.pe(lambda e, k=k: e.transpose(out=PTx[:, k, :], in_=xn[:, k * 128:(k + 1) * 128], identity=ident[:]),
                 reads=[b_xn, b_ident], writes=[b_ptx])
        for k in range(8):
            P.act(lambda e, k=k, s=s: e.activation(out=hT[:, k, :], in_=PTx[:, k, :], func=AF.Identity,
                                                   scale=mc[:, s, 1, k:k + 1], bias=mc[:, s, 0, k:k + 1]),
                  reads=[b_ptx, b_mc], writes=[b_hT, b_phase])
        if is_q:
            for nb in range(2):
                for k in range(8):
                    P.pe(lambda e, nb=nb, k=k: e.matmul(PQ[:, nb * 512:(nb + 1) * 512], lhsT=hT[:, k, :], rhs=wb[:, k, nb * 512:(nb + 1) * 512],
                                                        start=(k == 0), stop=(k == 7)), reads=[b_hT, b_w], writes=[b_pq])
        for k in range(8):
            P.pe(lambda e, k=k: e.matmul(PKV[:, :], lhsT=hT[:, k, :], rhs=wb[:, k, 1024:1536], start=(k == 0), stop=(k == 7)),
                 reads=[b_hT, b_w], writes=[b_pkv])
        P.act(lambda e: e.activation(out=kf[:], in_=PKV[:, 0:256], func=AF.Copy), reads=[b_pkv], writes=[b_kf, b_phase])
        P.act(lambda e, t=t: e.activation(out=VX[:, t, :, 0:64], in_=PKV[:, 256:512].rearrange("p (j d) -> p j d", d=64), func=AF.Copy),
              reads=[b_pkv, b_vone], writes=[b_VX[t], b_phase])
        qknorm_rope(kf, 4, gk, b_gk, kr, b_kr, t, do_rope, b_kf)
        for j in range(4):
            P.pe(lambda e, j=j: e.transpose(out=PTk[:, j, :], in_=kr[:, j * 64:(j + 1) * 64], identity=ident[:]),
                 reads=[b_kr, b_ident], writes=[b_ptk])
        P.act(lambda e, t=t: e.activation(out=KT[:, t, :, :], in_=PTk[:, :, :], func=AF.Copy), reads=[b_ptk], writes=[b_KT[t], b_phase])
        if is_q:
            qi = len(qidx); qidx[t] = qi
            P.act(lambda e: e.activation(out=qf[:], in_=PQ[:], func=AF.Copy), reads=[b_pq], writes=[b_qf, b_phase])
            qknorm_rope(qf, 16, gq, b_gq, qr, b_qr, t, do_rope, b_qf)
            for h in range(16):
                P.pe(lambda e, h=h: e.transpose(out=PTq[:, h, :], in_=qr[:, h * 64:(h + 1) * 64], identity=ident[:]),
                     reads=[b_qr, b_ident], writes=[b_ptq])
            P.act(lambda e, qi=qi: e.activation(out=QT[:, qi, :, :], in_=PTq[:, :, :], func=AF.Copy), reads=[b_ptq], writes=[b_QT[qi], b_phase])
    ps1.close()

    PS = [P.ps(f"ps{i}", [128, 512], F32) for i in range(5)]
    PO = P.ps("po", [128, 4, 65], F32)
    POT = P.ps("pot", [128, 8, 128], BF16)
    b_ps = P.bufs(5); b_po = P.buf(); b_pot = P.buf()
    outs = []
    first = True
    pi = 0
    for t in list(range(1, 17)) + [18, 19]:
        qi = qidx[t]
        if t < 18:
            chunks = [(t - 1, 0 if t == 1 else 1), (t, None), (t + 1, 3 if t == 16 else 2), (18, None), (19, None)]
            col0 = (t - 1) * 128
        else:
            chunks = [(18, None), (19, None)]
            col0 = 2048 + (t - 18) * 128
        nch = len(chunks)
        for j in range(4):
            pts = PTS[pi % 2]; bpts = b_pts[pi % 2]; pi += 1
            for ci, (kt, m) in enumerate(chunks):
                wr = [b_ps[ci]] + ([b_phase] if first else [])
                first = False
                P.pe(lambda e, ci=ci, kt=kt, j=j, qi=qi: e.matmul(PS[ci][:, :], lhsT=KT[:, kt, j, :],
                                                                  rhs=QT[:, qi, 4 * j:4 * j + 4, :].rearrange("p h q -> p (h q)"),
                                                                  start=True, stop=True),
                     reads=[b_KT[kt], b_QT[qi]], writes=wr)
                P.act(lambda e, ci=ci, pts=pts: e.activation(out=pts[:, ci, :], in_=PS[ci][:, :], func=AF.Exp), reads=[b_ps[ci]], writes=[bpts])
                if m is not None:
                    P.dve(lambda e, ci=ci, pts=pts, m=m: e.tensor_tensor(out=pts[:, ci, :].rearrange("p (h q) -> p h q", h=4),
                                                                         in0=pts[:, ci, :].rearrange("p (h q) -> p h q", h=4),
                                                                         in1=masks[:, m, :].unsqueeze(1).to_broadcast([128, 4, 128]), op=ALU.mult),
                          reads=[bpts, b_masks], writes=[bpts])
            for g in range(4):
                for ci, (kt, m) in enumerate(chunks):
                    P.pe(lambda e, g=g, ci=ci, kt=kt, j=j, pts=pts, nch=nch: e.matmul(PO[:, g, :], lhsT=pts[:, ci, g * 128:(g + 1) * 128],
                                                                                      rhs=VX[:, kt, j, :], start=(ci == 0), stop=(ci == nch - 1)),
                         reads=[bpts, b_VX[kt]], writes=[b_po])
            P.dve(lambda e, j=j: e.tensor_tensor(out=den[:, 0:4], in0=PO[:, :, 64], in1=esink[:, 4 * j:4 * j + 4], op=ALU.add),
                  reads=[b_po, b_esink], writes=[b_den])
            P.dve(lambda e: e.reciprocal(out=den[:, 0:4], in_=den[:, 0:4]), reads=[b_den], writes=[b_den])
            P.dve(lambda e, j=j: e.tensor_tensor(out=osb[:, 256 * j:256 * j + 256].rearrange("p (g d) -> p g d", d=64), in0=PO[:, :, 0:64],
                                                 in1=den[:, 0:4].unsqueeze(2).to_broadcast([128, 4, 64]), op=ALU.mult),
                  reads=[b_po, b_den], writes=[b_osb])
        for k in range(8):
            P.pe(lambda e, k=k: e.transpose(out=POT[:, k, :], in_=osb[:, k * 128:(k + 1) * 128], identity=ident[:]),
                 reads=[b_osb, b_ident], writes=[b_pot])
        P.act(lambda e: e.activation(out=ots[:], in_=POT[:], func=AF.Copy), reads=[b_pot], writes=[b_ots])
        outs.append(P.dma("sp", lambda e, col0=col0: e.dma_start(out=OT.rearrange("(k p) t -> p k t", p=128)[:, :, col0:col0 + 128], in_=ots[:]),
                          reads=[b_ots]))
    P.emit(final_wait_ops=outs)
    return nc


def rope_tables(core):
    t0 = core * 2048 - 128
    t = np.arange(t0, t0 + 18 * 128)
    t = np.clip(t, 0, 16383)
    pos = np.stack([t // 64, t % 64], -1).astype(np.float32)
    freqs = (10000.0 ** (-np.arange(16, dtype=np.float32) / 16)).astype(np.float32)
    ang = pos[:, :, None] * freqs
    cs = np.cos(ang).reshape(-1, 32); sn = np.sin(ang).reshape(-1, 32)
    return np.stack([cs, sn], 1).reshape(18, 128, 2, 32).astype(np.float32)


def att_masks(core):
    j = np.arange(128)[:, None]; i = np.arange(128)[None, :]
    prev = (j >= i).astype(np.float32); nxt = (j <= i).astype(np.float32)
    m = np.stack([prev if core > 0 else np.zeros_like(prev), prev, nxt, nxt if core < 7 else np.zeros_like(nxt)], 1)
    return np.ascontiguousarray(m).astype(np.float32)


def build_mod():
    nc = bass.Bass("TRN2", target_bir_lowering=False)
    P = Prog(nc)
    ccd = P.din("cc", [128, 8, 2], F32)
    wmd = P.din("wm", [4, 1024, 768], F32)
    bmd = P.din("bm", [1, 4, 768], F32)
    gmd = P.din("gm", [1, 4, 256], F32)
    mo = P.dout("mo", [4, 2, 768], F32)
    cc = P.sb("cct", [128, 8, 2], F32)
    S = P.sb("S", [128, 8, 33], F32)
    Sh = P.sb("Sh", [128, 8, 33], BF16)
    Sl = P.sb("Sl", [128, 8, 33], BF16)
    brow = P.sb("brow", [33, 4, 768], F32)
    grow = P.sb("grow", [33, 4, 256], F32)
    WT = [P.sb(f"wt{i}", [128, 8, 768], F32) for i in range(2)]
    Wh = P.sb("Wh", [128, 8, 768], BF16)
    Wl = P.sb("Wl", [128, 8, 768], BF16)
    RR = [P.sb(f"r{i}", [33, 768], F32) for i in range(2)]
    PM = P.ps("pm", [128, 1024], F32)
    b_cc, b_S, b_Sh, b_Sl, b_brow, b_grow, b_Wh, b_Wl, b_pm = P.bufs(9)
    b_wt = P.bufs(2); b_r = P.bufs(2)
    P.dma("sp", lambda e: e.dma_start(out=cc[:], in_=ccd), writes=[b_cc])
    P.pool(lambda e: e.memset(S[:], 0.0), writes=[b_S])
    P.pool(lambda e: e.memset(brow[:], 0.0), writes=[b_brow])
    P.pool(lambda e: e.memset(grow[:], 0.0), writes=[b_grow])
    for prt in (0, 32):
        P.dma("sp", lambda e, prt=prt: e.dma_start(out=brow[prt:prt + 1], in_=bmd), reads=[], writes=[b_brow])
        P.dma("sp", lambda e, prt=prt: e.dma_start(out=grow[prt:prt + 1], in_=gmd), reads=[], writes=[b_grow])
    P.act(lambda e: e.activation(out=S[:, :, 0], in_=cc[:, :, 0], func=AF.Silu), reads=[b_cc], writes=[b_S])
    P.act(lambda e: e.activation(out=S[:, :, 32], in_=cc[:, :, 1], func=AF.Silu), reads=[b_cc], writes=[b_S])
    P.dve(lambda e: e.tensor_copy(out=Sh[:], in_=S[:]), reads=[b_S], writes=[b_Sh])
    P.dve(lambda e: e.tensor_tensor(out=Sl[:], in0=S[:], in1=Sh[:], op=ALU.subtract), reads=[b_S, b_Sh], writes=[b_Sl])
    outs = []
    for l in range(4):
        wt = WT[l % 2]; bwt = b_wt[l % 2]; r = RR[l % 2]; br_ = b_r[l % 2]
        P.dma("sp", lambda e, wt=wt, l=l: e.dma_start(out=wt[:], in_=wmd[l].rearrange("(k p) n -> p k n", p=128)), writes=[bwt])
        P.dve(lambda e, wt=wt: e.tensor_copy(out=Wh[:], in_=wt[:]), reads=[bwt], writes=[b_Wh])
        P.dve(lambda e, wt=wt: e.tensor_tensor(out=Wl[:], in0=wt[:], in1=Wh[:], op=ALU.subtract), reads=[bwt, b_Wh], writes=[b_Wl])
        for half in range(2):
            n = 0
            for k in range(8):
                for (sa, bsa, wa, bwa) in ((Sh, b_Sh, Wh, b_Wh), (Sh, b_Sh, Wl, b_Wl), (Sl, b_Sl, Wh, b_Wh)):
                    P.pe(lambda e, sa=sa, wa=wa, k=k, half=half, n=n: e.matmul(PM[0:33, half * 512:half * 512 + 384], lhsT=sa[:, k, :],
                                                                             rhs=wa[:, k, half * 384:(half + 1) * 384],
                                                                             start=(n == 0), stop=(n == 23)),
                         reads=[bsa, bwa], writes=[b_pm])
                    n += 1
        for half in range(2):
            P.dve(lambda e, r=r, half=half, l=l: e.tensor_tensor(out=r[:, half * 384:(half + 1) * 384], in0=PM[0:33, half * 512:half * 512 + 384],
                                                                 in1=brow[:, l, half * 384:(half + 1) * 384], op=ALU.add),
                  reads=[b_pm, b_brow], writes=[br_])
        P.dve(lambda e, r=r, l=l: e.scalar_tensor_tensor(out=r[:, 128:256], in0=r[:, 128:256], scalar=1.0, in1=grow[:, l, 0:128],
                                                         op0=ALU.add, op1=ALU.mult), reads=[br_, b_grow], writes=[br_])
        P.dve(lambda e, r=r, l=l: e.scalar_tensor_tensor(out=r[:, 512:640], in0=r[:, 512:640], scalar=1.0, in1=grow[:, l, 128:256],
                                                         op0=ALU.add, op1=ALU.mult), reads=[br_, b_grow], writes=[br_])
        outs.append(P.dma("sp", lambda e, r=r, l=l: e.dma_start(out=mo[l, 0:1, :], in_=r[0:1, :]), reads=[br_]))
        outs.append(P.dma("sp", lambda e, r=r, l=l: e.dma_start(out=mo[l, 1:2, :], in_=r[32:33, :]), reads=[br_]))
    P.emit(final_wait_ops=outs)
    return nc


def run_mod(inp, progs):
    c = np.asarray(inp['c'], np.float32).reshape(1024)
    cx = np.asarray(inp['c_ctx'], np.float32).reshape(1024)
    cc = np.ascontiguousarray(np.stack([c.reshape(8, 128).T, cx.reshape(8, 128).T], -1))
    wm6 = np.asarray(inp['w_mod'], np.float32).reshape(4, 1024, 6, 1024)
    bm6 = np.asarray(inp['b_mod'], np.float32).reshape(4, 6, 1024)
    gmix = np.asarray(inp['norm_mix_g'], np.float32); gffn = np.asarray(inp['norm_ffn_g'], np.float32)
    ins = []
    for k in range(8):
        sl = slice(128 * k, 128 * k + 128)
        ins.append(dict(cc=cc, wm=np.ascontiguousarray(wm6[:, :, :, sl]).reshape(4, 1024, 768),
                        bm=np.ascontiguousarray(bm6[:, :, sl]).reshape(1, 4, 768),
                        gm=np.ascontiguousarray(np.stack([gmix[:, sl], gffn[:, sl]], 1)).reshape(1, 4, 256)))
    res = run_bass_kernel_spmd(progs['mod'], ins, core_ids=list(range(8)))
    modx = np.zeros((4, 2, 6, 1024), np.float32)
    for k in range(8):
        modx[:, :, :, 128 * k:128 * k + 128] = np.asarray(res.results[k]['mo']).reshape(4, 2, 6, 128)
    return modx


NCORES = 8
CORES = list(range(NCORES))


def _cols(v):
    return v.reshape(8, 128).T


def _rep(v):
    return np.ascontiguousarray(np.tile(np.asarray(v, np.float32).reshape(1, -1), (128, 1)))


def _run(nc, ins):
    res = run_bass_kernel_spmd(nc, ins, core_ids=CORES)
    return res.results


def kernel(**inp):
    inp = {k: np.asarray(v) for k, v in inp.items()}
    progs = dict(mod=build_mod(), e1=build_e1(), e2=build_e2(), e3=build_e3(), att=build_att(), moe=build_moe())
    xl = np.ascontiguousarray(inp['x'][0], dtype=np.float32)
    xc = np.ascontiguousarray(inp['ctx'][0], dtype=np.float32)
    modx = run_mod(inp, progs)
    K1 = dft_consts()
    K2 = fft_consts()
    for layer in range(4):
        j = layer // 2
        mcols = np.zeros((128, 2, 2, 8), np.float32)
        for s in range(2):
            mcols[:, s, 0] = _cols(modx[layer, s, 0]); mcols[:, s, 1] = _cols(modx[layer, s, 1])
        g1 = np.ascontiguousarray(np.stack([_rep(modx[layer, 0, 2]), _rep(modx[layer, 1, 2])], 1))
        if layer % 2 == 0:
            cwh = np.ascontiguousarray(inp['conv_w'][j].reshape(3, 4, 128).transpose(2, 0, 1))
            ins = []
            for c in CORES:
                xt = np.zeros((19, 128, 1024), np.float32)
                xt[:16] = xl[2048 * c:2048 * (c + 1)].reshape(16, 128, 1024)
                xt[16:18] = xc.reshape(2, 128, 1024)
                if c > 0:
                    xt[18, 0] = xl[2048 * c - 1]
                if c < 7:
                    xt[18, 1] = xl[2048 * (c + 1)]
                flags = np.zeros((128, 2), np.float32); flags[:, 0] = float(c > 0); flags[:, 1] = float(c < 7)
                ins.append(dict(xt=xt, mcols=mcols, win=inp['w_in_even'][j], cw=cwh, cs128=K1['cs128'], c256=K1['c256'],
                                ns256=K1['ns256'], flags=flags))
            r1 = _run(progs['e1'], ins)
            Bfull = np.concatenate([np.asarray(r1[c]['bout']) for c in CORES], 0)
            B5 = Bfull.reshape(128, 128, 4, 2, 128)
            ins = []
            for c in CORES:
                g = c // 2; m0 = 64 * (c % 2)
                zin = np.ascontiguousarray(B5[:, :, g, :, m0:m0 + 64].transpose(0, 2, 3, 1))
                ins.append(dict(zin=zin, m1=K2['m1'], m2=K2['m2'], tw=K2['tw'], c2=K2['c2']))
            r2 = _run(progs['e2'], ins)
            fmT = np.zeros((512, 16384), dtype=Bfull.dtype)
            for c in CORES:
                g = c // 2; m0 = 64 * (c % 2)
                fmT[g * 128 + m0:g * 128 + m0 + 64] = np.asarray(r2[c]['FT']).reshape(64, 16384)
            yTs = []
            for c in CORES:
                top = np.concatenate([fmT[:, 2048 * c:2048 * (c + 1)], np.asarray(r1[c]['fcT'])], 1)
                yTs.append(np.ascontiguousarray(np.concatenate([top, np.asarray(r1[c]['ycT'])], 0)))
            wproj = inp['w_out_even'][j]
        else:
            ins = []
            for c in CORES:
                xt = np.zeros((20, 128, 1024), np.float32)
                if c > 0:
                    xt[0] = xl[2048 * c - 128:2048 * c]
                xt[1:17] = xl[2048 * c:2048 * (c + 1)].reshape(16, 128, 1024)
                if c < 7:
                    xt[17] = xl[2048 * (c + 1):2048 * (c + 1) + 128]
                xt[18:20] = xc.reshape(2, 128, 1024)
                ins.append(dict(xt=xt, mcols=mcols, wqkv=inp['w_qkv'][j], gq=_rep(inp['q_norm_g'][j]), gk=_rep(inp['k_norm_g'][j]),
                                rope=rope_tables(c), sink=_rep(inp['sink_logit'][j]), masks=att_masks(c)))
            ra = _run(progs['att'], ins)
            yTs = [np.asarray(ra[c]['OT']) for c in CORES]
            wproj = inp['w_o'][j]
        ins = []
        for c in CORES:
            xt = np.concatenate([xl[2048 * c:2048 * (c + 1)], xc], 0).reshape(18, 128, 1024)
            ins.append(dict(yT=yTs[c], xt=np.ascontiguousarray(xt), g1=g1, w=wproj))
        r3 = _run(progs['e3'], ins)
        rows = np.zeros((2, 3, 128, 1024), np.float32)
        for s in range(2):
            rows[s, 0] = _rep(modx[layer, s, 4]); rows[s, 1] = _rep(modx[layer, s, 3]); rows[s, 2] = _rep(modx[layer, s, 5])
        wr = np.ascontiguousarray(np.concatenate([inp['w_router_g'][layer], inp['w_router_e'][layer]], 1))
        br = _rep(np.concatenate([inp['b_router_g'][layer], inp['b_router_e'][layer]]))
        ins = []
        for c in CORES:
            ins.append(dict(xt=np.asarray(r3[c]['xo']), rows=rows, wr=wr, br=br, w1=inp['w1'][layer], w3=inp['w3'][layer],
                            w2=inp['w2'][layer]))
        r4 = _run(progs['moe'], ins)
        xl = np.concatenate([np.asarray(r4[c]['xo']).reshape(2304, 1024)[:2048] for c in CORES], 0)
        xc = np.asarray(r4[0]['xo']).reshape(2304, 1024)[2048:]
    return np.ascontiguousarray(xl, dtype=np.float32)[None]
```

```python
import numpy as np
import ml_dtypes
from contextlib import ExitStack
import concourse.bass as bass
import concourse.mybir as mybir
from concourse.bass_utils import run_bass_kernel_spmd

F32 = mybir.dt.float32
BF16 = mybir.dt.bfloat16
I32 = mybir.dt.int32
AF = mybir.ActivationFunctionType
ALU = mybir.AluOpType
AX = mybir.AxisListType
NPBF = ml_dtypes.bfloat16

COMPUTE = ("pe", "act", "dve", "pool")
QUEUES = ("sp", "act", "pool")


class Buf:
    __slots__ = ("name", "last_w", "readers")

    def __init__(self, name):
        self.name = name
        self.last_w = None
        self.readers = []


class Op:
    __slots__ = ("eng", "fn", "deps", "is_dma", "idx", "signal", "sigval", "dsem", "dval", "dprev")

    def __init__(self, eng, fn, is_dma):
        self.eng = eng
        self.fn = fn
        self.is_dma = is_dma
        self.deps = set()
        self.signal = False
        self.sigval = 0
        self.dsem = None
        self.dval = 0
        self.dprev = None


class Prog:
    def __init__(self, nc, n_dma_sems=6):
        self.nc = nc
        self.ops = []
        self.n_dma_sems = n_dma_sems
        self.es = ExitStack()
        self._nb = 0

    def buf(self, name=None):
        self._nb += 1
        return Buf(name or f"b{self._nb}")

    def bufs(self, n, name="b"):
        return [self.buf(f"{name}{i}") for i in range(n)]

    def sb(self, name, shape, dt):
        return self.es.enter_context(self.nc.sbuf_tensor(name, shape, dt))

    def ps(self, name, shape, dt):
        return self.es.enter_context(self.nc.psum_tensor(name, shape, dt))

    def din(self, name, shape, dt):
        return self.nc.dram_tensor(name, list(shape), dt, kind="ExternalInput").ap()

    def dout(self, name, shape, dt):
        return self.nc.dram_tensor(name, list(shape), dt, kind="ExternalOutput").ap()

    def op(self, eng, fn, reads=(), writes=(), dma=False):
        o = Op(eng, fn, dma)
        o.idx = len(self.ops)
        for b in reads:
            if b.last_w is not None:
                o.deps.add(b.last_w)
        for b in writes:
            if b.last_w is not None:
                o.deps.add(b.last_w)
            for r in b.readers:
                o.deps.add(r)
        for b in reads:
            b.readers.append(o.idx)
        for b in writes:
            b.last_w = o.idx
            b.readers = []
        o.deps.discard(o.idx)
        self.ops.append(o)
        return o

    def pe(self, fn, reads=(), writes=()):
        return self.op("pe", fn, reads, writes)

    def act(self, fn, reads=(), writes=()):
        return self.op("act", fn, reads, writes)

    def dve(self, fn, reads=(), writes=()):
        return self.op("dve", fn, reads, writes)

    def pool(self, fn, reads=(), writes=()):
        return self.op("pool", fn, reads, writes)

    def dma(self, q, fn, reads=(), writes=()):
        return self.op(q, fn, reads, writes, dma=True)

    def emit(self, final_wait_ops=None):
        nc = self.nc
        ops = self.ops
        for o in ops:
            if o.eng == "pe" and not o.is_dma:
                o.deps = {d for d in o.deps if not (ops[d].eng == "pe" and not ops[d].is_dma)}
        qcount = {q: 0 for q in QUEUES}
        qlast = {}
        for o in ops:
            if o.is_dma:
                slot = qcount[o.eng] % self.n_dma_sems
                qcount[o.eng] += 1
                key = (o.eng, slot)
                if key in qlast:
                    p = ops[qlast[key]]
                    o.dprev = p.idx
                    o.dval = p.dval + 16
                else:
                    o.dval = 16
                o.dsem = key
                qlast[key] = o.idx
        for o in ops:
            for d in o.deps:
                if not ops[d].is_dma:
                    ops[d].signal = True
        final = list(final_wait_ops or [])
        for d in final:
            if not d.is_dma:
                d.signal = True
        sigc = {e: 0 for e in COMPUTE}
        for o in ops:
            if not o.is_dma and o.signal:
                sigc[o.eng] += 1
                o.sigval = sigc[o.eng]
        sems = {}
        es = self.es
        for e in COMPUTE:
            sems[e] = es.enter_context(nc.semaphore("s_" + e))
        for q in QUEUES:
            for s in range(min(self.n_dma_sems, qcount[q])):
                sems[(q, s)] = es.enter_context(nc.semaphore(f"d_{q}{s}"))
        by_eng = {e: [] for e in ("pe", "act", "dve", "pool", "sp")}
        for o in ops:
            by_eng[o.eng].append(o)
        self.stats = {e: len(v) for e, v in by_eng.items()}

        def semkey_val(d):
            p = ops[d]
            if p.is_dma:
                return p.dsem, p.dval
            return p.eng, p.sigval

        def run_engine(ename, handle):
            waited = {}
            for o in by_eng[ename]:
                need = {}
                for d in o.deps:
                    k, v = semkey_val(d)
                    if need.get(k, 0) < v:
                        need[k] = v
                if o.is_dma and o.dprev is not None:
                    k, v = semkey_val(o.dprev)
                    if need.get(k, 0) < v:
                        need[k] = v
                for k, v in need.items():
                    if waited.get(k, 0) < v:
                        handle.wait_ge(sems[k], v)
                        waited[k] = v
                ins = o.fn(handle)
                if o.is_dma:
                    ins.then_inc(sems[o.dsem], 16)
                elif o.signal:
                    ins.then_inc(sems[o.eng], 1)
            if ename == "sp":
                for d in final:
                    k, v = semkey_val(d.idx)
                    if waited.get(k, 0) < v:
                        handle.wait_ge(sems[k], v)
                        waited[k] = v

        with nc.Block() as block:
            block.sync(lambda e: run_engine("sp", e))
            if by_eng["pe"]:
                block.tensor(lambda e: run_engine("pe", e))
            if by_eng["act"]:
                block.scalar(lambda e: run_engine("act", e))
            if by_eng["dve"]:
                block.vector(lambda e: run_engine("dve", e))
            if by_eng["pool"]:
                block.gpsimd(lambda e: run_engine("pool", e))
        self.es.close()


def make_ident(P, dt=BF16, name="ident"):
    idf = P.sb(name + "_f", [128, 128], F32)
    bf = P.buf()
    P.pool(lambda e: e.memset(idf[:], 0.0), writes=[bf])
    P.pool(lambda e: e.affine_select(out=idf[:], in_=idf[:], pattern=[[-1, 128]], compare_op=ALU.not_equal,
                                     fill=1.0, base=0, channel_multiplier=1), reads=[bf], writes=[bf])
    if dt == F32:
        return idf, bf
    idb = P.sb(name, [128, 128], dt)
    bb = P.buf()
    P.dve(lambda e: e.tensor_copy(out=idb[:], in_=idf[:]), reads=[bf], writes=[bb])
    return idb, bb

EPS = 1e-6


def build_e1():
    nc = bass.Bass("TRN2", target_bir_lowering=False)
    P = Prog(nc)
    xt = P.din("xt", [19, 128, 1024], F32)
    mcols = P.din("mcols", [128, 2, 2, 8], F32)
    win = P.din("win", [1024, 2048], F32)
    cwd = P.din("cw", [128, 3, 4], F32)
    cs128d = P.din("cs128", [128, 256], F32)
    c256d = P.din("c256", [128, 2, 256], F32)
    ns256d = P.din("ns256", [128, 2, 256], F32)
    flagsd = P.din("flags", [128, 2], F32)
    bout = P.dout("bout", [2048, 1024], BF16)
    ycT = P.dout("ycT", [512, 2304], BF16)
    fcT = P.dout("fcT", [512, 256], BF16)

    winb = P.sb("winb", [128, 8, 2048], BF16)
    mc = P.sb("mc", [128, 2, 2, 8], F32)
    cw = P.sb("cwt", [128, 3, 4], F32)
    cs128 = P.sb("cs128b", [128, 256], BF16)
    c256 = P.sb("c256b", [128, 2, 256], BF16)
    ns256 = P.sb("ns256b", [128, 2, 256], BF16)
    flags = P.sb("flagst", [128, 2], F32)
    XS = [P.sb(f"xs{i}", [128, 1024], F32) for i in range(2)]
    junk = P.sb("junk", [128, 1024], BF16)
    XN = [P.sb(f"xn{i}", [128, 1024], BF16) for i in range(2)]
    ss = P.sb("ss", [128, 19], F32)
    rstd = P.sb("rstd", [128, 19], F32)
    HT = [P.sb(f"hT{i}", [128, 8, 512], BF16) for i in range(2)]
    AT = [P.sb(f"aT{i}", [128, 4, 512], BF16) for i in range(2)]
    cgt = P.sb("cgt", [128, 4, 512], F32)
    uh = P.sb("uh", [128, 4, 2], F32)
    BgT = P.sb("BgT", [128, 4, 2304], BF16)
    uT = P.sb("uT", [128, 4, 2050], BF16)
    uTc = P.sb("uTc", [128, 4, 258], BF16)
    BT = [P.sb(f"bt{i}", [128, 1024], BF16) for i in range(2)]
    bctx = P.sb("bctx", [128, 2, 1024], BF16)
    fct = P.sb("fct", [128, 4, 256], BF16)
    yct = P.sb("yct", [128, 4, 2304], BF16)
    T1 = [P.sb(f"t1_{i}", [128, 1024], F32) for i in range(2)]
    T2 = [P.sb(f"t2_{i}", [128, 1024], F32) for i in range(2)]
    ident, b_ident = make_ident(P)
    epsc = P.sb("epsc", [128, 1], F32)
    b_eps = P.buf()
    P.pool(lambda e: e.memset(epsc[:], EPS), writes=[b_eps])

    PT = [P.ps(f"pT{i}", [128, 8, 128], BF16) for i in range(2)]
    PJ = [P.ps(f"pj{i}", [128, 512], F32) for i in range(3)]
    BS = P.ps("bs", [128, 4, 256], F32)
    PF = P.ps("pf", [128, 512], F32)

    b_win, b_mc, b_cw, b_cs, b_c256, b_ns256, b_flags = P.bufs(7, "c")
    b_xs = P.bufs(2, "xs"); b_junk = P.buf(); b_xn = P.bufs(2, "xn")
    b_ss = P.bufs(19, "ss"); b_rstd = P.bufs(19, "rs")
    b_ht = P.bufs(2, "ht"); b_at = P.bufs(2, "at"); b_cgt = P.bufs(4, "cgt"); b_uh = P.buf()
    b_bg = P.bufs(6, "bg"); b_u = P.bufs(7, "u"); b_bt = P.bufs(2, "bt"); b_bctx = P.bufs(2, "bctx")
    b_fct = P.bufs(4, "fct"); b_yct = P.bufs(8, "yct"); b_t1 = P.bufs(2, "t1"); b_t2 = P.bufs(2, "t2")
    b_pt = P.bufs(2, "pt"); b_pj = P.bufs(3, "pj"); b_bs = P.buf(); b_pf = P.buf()
    b_uz = P.buf()

    P.dma("pool", lambda e: e.dma_start(out=winb[:], in_=win.rearrange("(k p) n -> p k n", p=128)), writes=[b_win])
    P.dma("sp", lambda e: e.dma_start(out=mc[:], in_=mcols), writes=[b_mc])
    P.dma("sp", lambda e: e.dma_start(out=cw[:], in_=cwd), writes=[b_cw])
    P.dma("sp", lambda e: e.dma_start(out=flags[:], in_=flagsd), writes=[b_flags])
    P.dma("pool", lambda e: e.dma_start(out=cs128[:], in_=cs128d), writes=[b_cs])
    P.dma("pool", lambda e: e.dma_start(out=c256[:], in_=c256d), writes=[b_c256])
    P.dma("pool", lambda e: e.dma_start(out=ns256[:], in_=ns256d), writes=[b_ns256])
    P.pool(lambda e: e.memset(uTc[:], 0.0), writes=[b_uz])

    groups = [([18], 0, "halo")] + [([4 * g + i for i in range(4)], 0, "lat") for g in range(4)] + [([16, 17], 1, "ctx")]
    outs = []
    ti = 0
    pj_i = 0
    for gi, (tiles, s, kind) in enumerate(groups):
        ncols = 128 * len(tiles)
        hT = HT[gi % 2]; bht = b_ht[gi % 2]
        aT = AT[gi % 2]; bat = b_at[gi % 2]
        for tt, t in enumerate(tiles):
            xs = XS[ti % 2]; bxs = b_xs[ti % 2]
            xn = XN[ti % 2]; bxn = b_xn[ti % 2]
            pT = PT[ti % 2]; bpt = b_pt[ti % 2]
            P.dma("sp", lambda e, xs=xs, t=t: e.dma_start(out=xs[:], in_=xt[t]), writes=[bxs])
            P.act(lambda e, xs=xs, t=t: e.activation(out=junk[:], in_=xs[:], func=AF.Square, accum_out=ss[:, t:t + 1]),
                  reads=[bxs], writes=[b_junk, b_ss[t]])
            P.act(lambda e, t=t: e.activation(out=rstd[:, t:t + 1], in_=ss[:, t:t + 1], func=AF.Sqrt, scale=1.0 / 1024, bias=epsc[:, 0:1]),
                  reads=[b_ss[t], b_eps], writes=[b_rstd[t]])
            P.dve(lambda e, t=t: e.reciprocal(out=rstd[:, t:t + 1], in_=rstd[:, t:t + 1]),
                  reads=[b_rstd[t]], writes=[b_rstd[t]])
            P.dve(lambda e, xs=xs, xn=xn, t=t: e.tensor_scalar(out=xn[:], in0=xs[:], scalar1=rstd[:, t:t + 1], scalar2=None,
                                                               op0=ALU.mult), reads=[bxs, b_rstd[t]], writes=[bxn])
            for k in range(8):
                P.pe(lambda e, k=k, xn=xn, pT=pT: e.transpose(out=pT[:, k, :], in_=xn[:, k * 128:(k + 1) * 128], identity=ident[:]),
                     reads=[bxn, b_ident], writes=[bpt])
            for k in range(8):
                dst = hT[:, k, tt * 128:(tt + 1) * 128]
                if k % 2 == 0:
                    P.act(lambda e, dst=dst, pT=pT, k=k, s=s: e.activation(out=dst, in_=pT[:, k, :], func=AF.Identity,
                                                                          scale=mc[:, s, 1, k:k + 1], bias=mc[:, s, 0, k:k + 1]),
                          reads=[bpt, b_mc], writes=[bht])
                else:
                    P.dve(lambda e, dst=dst, pT=pT, k=k, s=s: e.tensor_scalar(out=dst, in0=pT[:, k, :], scalar1=mc[:, s, 1, k:k + 1],
                                                                             scalar2=mc[:, s, 0, k:k + 1], op0=ALU.mult, op1=ALU.add),
                          reads=[bpt, b_mc], writes=[bht])
            ti += 1
        if kind == "lat":
            tok0 = tiles[0] * 128
            bbg = b_bg[gi - 1]; bu = b_u[gi - 1]
        elif kind == "ctx":
            tok0 = 0
            bbg = b_bg[4]; bu = b_u[4]
        nlist = range(16) if kind != "halo" else range(8, 16)
        for n in nlist:
            pj = PJ[pj_i % 3]; bpj = b_pj[pj_i % 3]; pj_i += 1
            for k in range(8):
                P.pe(lambda e, pj=pj, k=k, n=n, hT=hT, ncols=ncols: e.matmul(pj[:, 0:ncols], lhsT=winb[:, k, n * 128:(n + 1) * 128],
                                                                             rhs=hT[:, k, 0:ncols], start=(k == 0), stop=(k == 7)),
                     reads=[bht, b_win], writes=[bpj])
            if n < 4:
                P.act(lambda e, pj=pj, n=n, aT=aT, ncols=ncols: e.activation(out=aT[:, n, 0:ncols], in_=pj[:, 0:ncols], func=AF.Copy),
                      reads=[bpj], writes=[bat])
            elif n < 8:
                j = n - 4
                if kind == "lat":
                    dst = BgT[:, j, tok0:tok0 + ncols]
                else:
                    dst = BgT[:, j, 2048:2304]
                P.dve(lambda e, pj=pj, dst=dst, ncols=ncols: e.tensor_copy(out=dst, in_=pj[:, 0:ncols]), reads=[bpj], writes=[bbg])
            elif n < 12:
                j = n - 8
                P.act(lambda e, pj=pj, j=j, ncols=ncols: e.activation(out=cgt[:, j, 0:ncols], in_=pj[:, 0:ncols], func=AF.Copy),
                      reads=[bpj], writes=[b_cgt[j]])
            else:
                j = n - 12
                if kind == "halo":
                    P.dve(lambda e, pj=pj, j=j: e.tensor_tensor(out=uh[:, j, :], in0=pj[:, 0:2], in1=cgt[:, j, 0:2], op=ALU.mult),
                          reads=[bpj, b_cgt[j]], writes=[b_uh])
                    P.dve(lambda e, j=j: e.tensor_scalar(out=uT[:, j, 0:1], in0=uh[:, j, 0:1], scalar1=flags[:, 0:1], scalar2=None,
                                                         op0=ALU.mult), reads=[b_uh, b_flags], writes=[b_u[5]])
                    P.dve(lambda e, j=j: e.tensor_scalar(out=uT[:, j, 2049:2050], in0=uh[:, j, 1:2], scalar1=flags[:, 1:2], scalar2=None,
                                                         op0=ALU.mult), reads=[b_uh, b_flags], writes=[b_u[6]])
                else:
                    if kind == "lat":
                        dst = uT[:, j, 1 + tok0:1 + tok0 + ncols]
                        wr = [bu]
                    else:
                        dst = uTc[:, j, 1:257]
                        wr = [bu]

                    rd = [bpj, b_cgt[j]] + ([b_uz] if kind == "ctx" else [])
                    P.dve(lambda e, pj=pj, j=j, dst=dst, ncols=ncols: e.tensor_tensor(out=dst, in0=pj[:, 0:ncols], in1=cgt[:, j, 0:ncols],
                                                                                      op=ALU.mult), reads=rd, writes=wr)
        if kind == "halo":
            continue
        for tt, t in enumerate(tiles):
            for g in range(4):
                P.pe(lambda e, g=g, aT=aT, tt=tt: e.matmul(BS[:, g, :], lhsT=aT[:, g, tt * 128:(tt + 1) * 128], rhs=cs128[:, :],
                                                           start=True, stop=True), reads=[bat, b_cs], writes=[b_bs])
            if kind == "lat":
                bt = BT[t % 2]; bbt = b_bt[t % 2]
                P.act(lambda e, bt=bt: e.activation(out=bt[:], in_=BS[:].rearrange("p g c -> p (g c)"), func=AF.Copy),
                      reads=[b_bs], writes=[bbt])
                outs.append(P.dma("sp", lambda e, bt=bt, t=t: e.dma_start(out=bout[t * 128:(t + 1) * 128, :], in_=bt[:]), reads=[bbt]))
            else:
                P.act(lambda e, tt=tt: e.activation(out=bctx[:, tt, :], in_=BS[:].rearrange("p g c -> p (g c)"), func=AF.Copy),
                      reads=[b_bs], writes=[b_bctx[tt]])
        if kind == "ctx":
            for g in range(4):
                for tt in range(2):
                    P.pe(lambda e, g=g, tt=tt: e.matmul(PF[:, 0:256], lhsT=bctx[:, tt, g * 256:g * 256 + 128], rhs=c256[:, tt, :],
                                                        start=(tt == 0), stop=False), reads=[b_bctx[tt], b_c256], writes=[b_pf])
                    P.pe(lambda e, g=g, tt=tt: e.matmul(PF[:, 0:256], lhsT=bctx[:, tt, g * 256 + 128:g * 256 + 256], rhs=ns256[:, tt, :],
                                                        start=False, stop=(tt == 1)), reads=[b_bctx[tt], b_ns256], writes=[b_pf])
                P.act(lambda e, g=g: e.activation(out=fct[:, g, :], in_=PF[:, 0:256], func=AF.Copy), reads=[b_pf], writes=[b_fct[g]])
                outs.append(P.dma("sp", lambda e, g=g: e.dma_start(out=fcT[g * 128:(g + 1) * 128, :], in_=fct[:, g, :]), reads=[b_fct[g]]))

    allu = b_u[0:4] + [b_u[5], b_u[6]]
    allbg = b_bg[0:4]
    segs = [("lat", 0, 1024), ("lat", 1024, 1024), ("ctx", 0, 256)]
    ci = 0
    for j in range(4):
        for (kind, c0, w) in segs:
            en = "dve"
            t1 = T1[ci % 2]; t2 = T2[ci % 2]; bt1 = b_t1[ci % 2]; bt2 = b_t2[ci % 2]
            ci += 1
            if kind == "lat":
                src = uT; off = c0; ub = allu; bgs = BgT[:, j, c0:c0 + w]; bgb = allbg; ydst = yct[:, j, c0:c0 + w]
                byc = b_yct[j * 2 + (c0 // 1024)]
            else:
                src = uTc; off = 0; ub = [b_u[4], b_uz]; bgs = BgT[:, j, 2048:2304]; bgb = [b_bg[4]]; ydst = yct[:, j, 2048:2304]
                byc = b_yct[j * 2]
                byc = P.buf()
            P.op(en, lambda e, t1=t1, src=src, off=off, w=w, j=j: e.tensor_scalar(out=t1[:, 0:w], in0=src[:, j, off + 1:off + 1 + w],
                                                                                  scalar1=cw[:, 1, j:j + 1], scalar2=None, op0=ALU.mult),
                 reads=ub + [b_cw], writes=[bt1])
            P.op(en, lambda e, t1=t1, t2=t2, src=src, off=off, w=w, j=j: e.scalar_tensor_tensor(
                out=t2[:, 0:w], in0=src[:, j, off:off + w], scalar=cw[:, 0, j:j + 1], in1=t1[:, 0:w], op0=ALU.mult, op1=ALU.add),
                 reads=ub + [b_cw, bt1], writes=[bt2])
            P.op(en, lambda e, t1=t1, t2=t2, src=src, off=off, w=w, j=j: e.scalar_tensor_tensor(
                out=t1[:, 0:w], in0=src[:, j, off + 2:off + 2 + w], scalar=cw[:, 2, j:j + 1], in1=t2[:, 0:w], op0=ALU.mult, op1=ALU.add),
                 reads=ub + [b_cw, bt2], writes=[bt1])
            P.op(en, lambda e, t1=t1, w=w, bgs=bgs, ydst=ydst: e.tensor_tensor(out=ydst, in0=t1[:, 0:w], in1=bgs, op=ALU.mult),
                 reads=[bt1] + bgb, writes=[byc])
            if kind == "lat":
                outs.append(P.dma("sp", lambda e, j=j, c0=c0, w=w: e.dma_start(out=ycT[j * 128:(j + 1) * 128, c0:c0 + w],
                                                                            in_=yct[:, j, c0:c0 + w]), reads=[byc]))
            else:
                outs.append(P.dma("sp", lambda e, j=j: e.dma_start(out=ycT[j * 128:(j + 1) * 128, 2048:2304], in_=yct[:, j, 2048:2304]),
                                  reads=[byc]))
    P.emit(final_wait_ops=outs)
    return nc


def dft_consts():
    n = np.arange(128)
    ang = 2 * np.pi * np.outer(n, n) / 128
    C = np.cos(ang); S = np.sin(ang)
    cs128 = np.concatenate([C, S], 1) / np.sqrt(128)
    t = np.arange(256)
    a256 = 2 * np.pi * np.outer(t, t) / 256
    c256 = (np.cos(a256) / 16).reshape(2, 128, 256).transpose(1, 0, 2)
    ns256 = (-np.sin(a256) / 16).reshape(2, 128, 256).transpose(1, 0, 2)
    return dict(cs128=cs128.astype(np.float32), c256=np.ascontiguousarray(c256).astype(np.float32),
                ns256=np.ascontiguousarray(ns256).astype(np.float32), C=C, S=S)


def build_e2():
    nc = bass.Bass("TRN2", target_bir_lowering=False)
    P = Prog(nc)
    zin = P.din("zin", [128, 2, 64, 128], BF16)
    m1d = P.din("m1", [128, 256], F32)
    m2d = P.din("m2", [128, 256], F32)
    twd = P.din("tw", [128, 2, 128], F32)
    c2d = P.din("c2", [128, 2, 128], F32)
    FT = P.dout("FT", [64, 128, 128], BF16)

    z = P.sb("z", [128, 2, 64, 128], BF16)
    m1 = P.sb("m1b", [128, 256], BF16)
    m2 = P.sb("m2b", [128, 256], BF16)
    tw = P.sb("twt", [128, 2, 128], F32)
    c2 = P.sb("c2b", [128, 2, 128], BF16)
    YS = [P.sb(f"ys{i}", [128, 4, 256], F32) for i in range(2)]
    TA = [P.sb(f"ta{i}", [128, 4, 128], F32) for i in range(2)]
    TB = [P.sb(f"tb{i}", [128, 4, 128], F32) for i in range(2)]
    TC = [P.sb(f"tc{i}", [128, 4, 128], F32) for i in range(2)]
    TD = [P.sb(f"td{i}", [128, 4, 128], F32) for i in range(2)]
    ZC = [P.sb(f"zc{i}", [128, 4, 128], BF16) for i in range(2)]
    ZS = [P.sb(f"zs{i}", [128, 4, 128], BF16) for i in range(2)]
    GS = [P.sb(f"gs{i}", [128, 4, 128], BF16) for i in range(2)]
    PY = [P.ps(f"py{i}", [128, 4, 256], F32) for i in range(2)]
    PG = [P.ps(f"pg{i}", [128, 4, 128], F32) for i in range(2)]

    b_z, b_m1, b_m2, b_tw, b_c2 = P.bufs(5, "c")
    b_ys = P.bufs(2); b_ta = P.bufs(2); b_tb = P.bufs(2); b_tc = P.bufs(2); b_td = P.bufs(2)
    b_zc = P.bufs(2); b_zs = P.bufs(2); b_gs = P.bufs(2); b_py = P.bufs(2); b_pg = P.bufs(2)

    P.dma("sp", lambda e: e.dma_start(out=z[:], in_=zin), writes=[b_z])
    P.dma("pool", lambda e: e.dma_start(out=m1[:], in_=m1d), writes=[b_m1])
    P.dma("pool", lambda e: e.dma_start(out=m2[:], in_=m2d), writes=[b_m2])
    P.dma("pool", lambda e: e.dma_start(out=c2[:], in_=c2d), writes=[b_c2])
    P.dma("sp", lambda e: e.dma_start(out=tw[:], in_=twd), writes=[b_tw])
    outs = []
    for st in range(16):
        i = st % 2
        py = PY[i]; ys = YS[i]; ta = TA[i]; tb = TB[i]; tc_ = TC[i]; td = TD[i]; zc = ZC[i]; zs = ZS[i]; gs = GS[i]; pg = PG[i]
        for q in range(4):
            n = st * 4 + q
            P.pe(lambda e, py=py, q=q, n=n: e.matmul(py[:, q, :], lhsT=z[:, 0, n, :], rhs=m1[:, :], start=True, stop=False),
                 reads=[b_z, b_m1], writes=[b_py[i]])
            P.pe(lambda e, py=py, q=q, n=n: e.matmul(py[:, q, :], lhsT=z[:, 1, n, :], rhs=m2[:, :], start=False, stop=True),
                 reads=[b_z, b_m2], writes=[b_py[i]])
        P.act(lambda e, py=py, ys=ys: e.activation(out=ys[:], in_=py[:], func=AF.Copy), reads=[b_py[i]], writes=[b_ys[i]])
        tcos = tw[:, 0:1, :].to_broadcast([128, 4, 128])
        tsin = tw[:, 1:2, :].to_broadcast([128, 4, 128])
        P.dve(lambda e, ys=ys, ta=ta, tcos=tcos: e.tensor_tensor(out=ta[:], in0=ys[:, :, 0:128], in1=tcos, op=ALU.mult),
              reads=[b_ys[i], b_tw], writes=[b_ta[i]])
        P.dve(lambda e, ys=ys, tb=tb, tsin=tsin: e.tensor_tensor(out=tb[:], in0=ys[:, :, 128:256], in1=tsin, op=ALU.mult),
              reads=[b_ys[i], b_tw], writes=[b_tb[i]])
        P.dve(lambda e, ta=ta, tb=tb, zc=zc: e.tensor_tensor(out=zc[:], in0=ta[:], in1=tb[:], op=ALU.subtract),
              reads=[b_ta[i], b_tb[i]], writes=[b_zc[i]])
        P.pool(lambda e, ys=ys, tc_=tc_, tsin=tsin: e.tensor_tensor(out=tc_[:], in0=ys[:, :, 0:128], in1=tsin, op=ALU.mult),
               reads=[b_ys[i], b_tw], writes=[b_tc[i]])
        P.pool(lambda e, ys=ys, td=td, tcos=tcos: e.tensor_tensor(out=td[:], in0=ys[:, :, 128:256], in1=tcos, op=ALU.mult),
               reads=[b_ys[i], b_tw], writes=[b_td[i]])
        P.pool(lambda e, tc_=tc_, td=td, zs=zs: e.tensor_tensor(out=zs[:], in0=tc_[:], in1=td[:], op=ALU.add),
               reads=[b_tc[i], b_td[i]], writes=[b_zs[i]])
        P.pe(lambda e, pg=pg, zc=zc: e.matmul(pg[:].rearrange("p a b -> p (a b)"), lhsT=c2[:, 0, :], rhs=zc[:].rearrange("p a b -> p (a b)"),
                                              start=True, stop=False), reads=[b_zc[i], b_c2], writes=[b_pg[i]])
        P.pe(lambda e, pg=pg, zs=zs: e.matmul(pg[:].rearrange("p a b -> p (a b)"), lhsT=c2[:, 1, :], rhs=zs[:].rearrange("p a b -> p (a b)"),
                                              start=False, stop=True), reads=[b_zs[i], b_c2], writes=[b_pg[i]])
        P.act(lambda e, pg=pg, gs=gs: e.activation(out=gs[:], in_=pg[:], func=AF.Copy), reads=[b_pg[i]], writes=[b_gs[i]])
        outs.append(P.dma("sp", lambda e, gs=gs, st=st: e.dma_start(out=FT[st * 4:(st + 1) * 4].rearrange("n a b -> a n b"), in_=gs[:]),
                          reads=[b_gs[i]]))
    P.emit(final_wait_ops=outs)
    return nc


def fft_consts():
    n = np.arange(128)
    ang = 2 * np.pi * np.outer(n, n) / 128
    C = np.cos(ang) / np.sqrt(128); S = np.sin(ang) / np.sqrt(128)
    m1 = np.concatenate([C, S], 1); m2 = np.concatenate([-S, C], 1)
    ta = 2 * np.pi * np.outer(n, n) / 16384
    tw = np.stack([np.cos(ta), np.sin(ta)], 1)
    c2 = np.stack([C, -S], 1)
    f = lambda a: np.ascontiguousarray(a).astype(np.float32)
    return dict(m1=f(m1), m2=f(m2), tw=f(tw), c2=f(c2))


def build_e3():
    nc = bass.Bass("TRN2", target_bir_lowering=False)
    P = Prog(nc)
    yT = P.din("yT", [1024, 2304], BF16)
    xt = P.din("xt", [18, 128, 1024], F32)
    g1d = P.din("g1", [128, 2, 1024], F32)
    wd = P.din("w", [1024, 1024], F32)
    xo = P.dout("xo", [18, 128, 1024], F32)
    ysb = P.sb("ysb", [128, 8, 2304], BF16)
    wb = P.sb("wb", [128, 8, 1024], BF16)
    g1 = P.sb("g1t", [128, 2, 1024], F32)
    XS = [P.sb(f"xs{i}", [128, 1024], F32) for i in range(2)]
    TM = [P.sb(f"tm{i}", [128, 1024], F32) for i in range(2)]
    PY = [P.ps(f"py{i}", [128, 1024], F32) for i in range(2)]
    b_y, b_w, b_g = P.bufs(3); b_xs = P.bufs(2); b_tm = P.bufs(2); b_py = P.bufs(2)
    P.dma("sp", lambda e: e.dma_start(out=ysb[:], in_=yT.rearrange("(k p) t -> p k t", p=128)), writes=[b_y])
    P.dma("pool", lambda e: e.dma_start(out=wb[:], in_=wd.rearrange("(k p) n -> p k n", p=128)), writes=[b_w])
    P.dma("sp", lambda e: e.dma_start(out=g1[:], in_=g1d), writes=[b_g])
    outs = []
    for t in range(18):
        i = t % 2
        s = 0 if t < 16 else 1
        xs = XS[i]; tm = TM[i]; py = PY[i]
        P.dma("sp", lambda e, xs=xs, t=t: e.dma_start(out=xs[:], in_=xt[t]), writes=[b_xs[i]])
        for h in range(2):
            for k in range(8):
                P.pe(lambda e, py=py, h=h, k=k, t=t: e.matmul(py[:, h * 512:(h + 1) * 512], lhsT=ysb[:, k, t * 128:(t + 1) * 128],
                                                             rhs=wb[:, k, h * 512:(h + 1) * 512], start=(k == 0), stop=(k == 7)),
                     reads=[b_y, b_w], writes=[b_py[i]])
        P.dve(lambda e, py=py, tm=tm, s=s: e.tensor_tensor(out=tm[:], in0=py[:], in1=g1[:, s, :], op=ALU.mult),
              reads=[b_py[i], b_g], writes=[b_tm[i]])
        P.pool(lambda e, tm=tm, xs=xs: e.tensor_tensor(out=tm[:], in0=tm[:], in1=xs[:], op=ALU.add),
               reads=[b_tm[i], b_xs[i]], writes=[b_tm[i]])
        outs.append(P.dma("sp", lambda e, tm=tm, t=t: e.dma_start(out=xo[t], in_=tm[:]), reads=[b_tm[i]]))
    P.emit(final_wait_ops=outs)
    return nc


NB = 68


def build_moe2():
    nc = bass.Bass("TRN2", target_bir_lowering=False)
    P = Prog(nc)
    xt = P.din("xt", [18, 128, 1024], F32)
    rowsd = P.din("rows", [2, 3, 128, 1024], F32)
    wrd = P.din("wr", [1024, 36], F32)
    brd = P.din("br", [128, 36], F32)
    w1L = P.din("w1L", [4096, 4096], F32)
    w3L = P.din("w3L", [4096, 4096], F32)
    w2L = P.din("w2L", [4096, 4096], F32)
    bvd = P.din("bvals", [128, NB * 32], F32)
    thrd = P.din("thr", [128, 32 * 18], F32)
    pcd = P.din("pcol", [128, 1], F32)
    tokd = P.din("tokid", [128, 18, 16], I32)
    xo = P.dout("xo", [18, 128, 1024], F32)
    Hd = nc.dram_tensor("Hd", [2305, 1024], BF16).ap()
    table = nc.dram_tensor("table", [(NB + 1) * 128, 16], I32).ap()
    Yd = nc.dram_tensor("Yd", [(NB + 1) * 128, 1024], BF16).ap()

    WG1 = [P.sb("wg1", [128, 4096], F32)] * 2
    WG3 = [P.sb("wg3", [128, 4096], F32)] * 2
    WG2 = [P.sb("wg2", [128, 4096], F32)] * 2
    W1 = [P.sb(f"w1b{i}", [128, 8, 512], BF16) for i in range(2)]
    W3 = [P.sb(f"w3b{i}", [128, 8, 512], BF16) for i in range(2)]
    W2 = [P.sb(f"w2b{i}", [128, 4, 1024], BF16) for i in range(2)]
    rows = P.sb("rowst", [128, 2, 1024], F32)
    xs = P.sb("xs", [128, 1024], F32)
    h32 = P.sb("h32", [128, 1024], F32)
    wrt = P.sb("wrt", [128, 8, 36], F32)
    brt = P.sb("brt", [128, 36], F32)
    wrh = P.sb("wrh", [128, 8, 36], BF16)
    wrl = P.sb("wrl", [128, 8, 36], BF16)
    hhi = P.sb("hhi", [128, 1024], BF16)
    hlo = P.sb("hlo", [128, 1024], BF16)
    junk = hlo
    hThi = P.sb("hThi", [128, 8, 128], BF16)
    hTlo = P.sb("hTlo", [128, 8, 128], BF16)
    identb = P.sb("identb", [128, 128], BF16)
    onesb = P.sb("onesb", [128, 128], BF16)
    onesf = P.sb("onesf", [128, 128], F32)
    Ub = P.sb("Ub", [128, 128], BF16)
    sm = P.sb("sm", [128, 64], F32)
    lg = P.sb("lg", [128, 36], F32)
    ss = P.sb("ss", [128, 18], F32)
    rstd = P.sb("rstd", [128, 18], F32)
    epsc = P.sb("epsc", [128, 1], F32)
    M12 = P.sb("M12", [128, 18, 2, 32], BF16)
    Msum = P.sb("Msum", [128, 18, 32], BF16)
    g12 = P.sb("g12", [128, 18, 2], F32)
    bvals = P.sb("bvalst", [128, NB, 32], F32)
    thr = P.sb("thrt", [128, 32, 18], F32)
    pcol = P.sb("pcolt", [128, 1], F32)
    tokid = P.sb("tokidt", [128, 18, 16], I32)
    tinit = P.sb("tinit", [128, NB + 1, 16], I32)
    zrow = P.sb("zrow", [1, 1024], BF16)
    cnt = P.sb("cnt", [128, 32], F32)
    big = P.sb("big", [128, NB * 32], F32)
    nblk = P.sb("nblk", [128, 32], F32)
    padded = P.sb("padded", [128, 32], F32)
    ca = P.sb("ca", [128, 32], F32)
    cb = P.sb("cb", [128, 32], F32)
    pstart = P.sb("pstart", [128, 32], F32)
    bexp = P.sb("bexp", [128, NB], F32)
    same = P.sb("same", [128, NB], F32)
    widx = P.sb("widx", [128, NB], I32)
    carry = P.sb("carry", [128, 32], F32)
    tA = P.sb("tA", [128, 32], F32)
    tB = P.sb("tB", [128, 32], F32)
    destf = P.sb("destf", [128, 18, 2], F32)
    desti = P.sb("desti", [128, 18, 2], I32)
    IDX = [P.sb(f"idx{i}", [128, 16], I32) for i in range(2)]
    XG = [P.sb(f"xg{i}", [128, 1024], BF16) for i in range(2)]
    XgT = P.sb("XgT", [128, 8, 128], BF16)
    ssb = P.sb("ssb", [128, 512], F32)
    hb = P.sb("hb", [128, 512], BF16)
    hT = P.sb("hT", [128, 4, 128], BF16)
    YS = [P.sb(f"ys{i}", [128, 1024], BF16) for i in range(2)]
    Y1 = P.sb("Y1", [128, 1024], BF16)
    Y2 = P.sb("Y2", [128, 1024], BF16)
    identf, b_identf = make_ident(P, F32, "identf")

    PTB = P.ps("ptb", [128, 8, 128], BF16)
    PRG = P.ps("prg", [128, 512], F32)
    PRK = P.ps("prk", [128, 512], F32)
    PH1 = P.ps("ph1", [128, 512], F32)
    PH3 = P.ps("ph3", [128, 512], F32)
    PTH = P.ps("pth", [128, 4, 128], BF16)
    PY = P.ps("py", [128, 1024], F32)

    b_wg1 = [P.buf()] * 2; b_wg3 = [P.buf()] * 2; b_wg2 = [P.buf()] * 2; b_w1 = P.bufs(2); b_w3 = P.bufs(2); b_w2 = P.bufs(2)
    b_rows = P.buf(); b_xs = P.buf(); b_h32 = P.buf(); b_junk = None
    b_wr = P.buf(); b_br = P.buf(); b_sm = P.buf(); b_lg = P.buf()
    b_ss = P.bufs(18); b_rstd = P.bufs(18); b_eps = P.buf()
    b_ptb = P.buf(); b_hhi = P.buf(); b_hlo = P.buf(); b_junk = b_hlo; b_hthi = P.buf(); b_htlo = P.buf(); b_wrh = P.buf(); b_wrl = P.buf(); b_idb = P.buf()
    b_prg = P.buf(); b_prk = P.buf(); b_ph1 = P.buf(); b_ph3 = P.buf(); b_pth = P.buf(); b_py = P.buf()
    b_ones = P.buf(); b_U = P.buf(); b_M = P.bufs(18); b_ms = P.bufs(18); b_g12 = P.bufs(18)
    b_bv, b_thr, b_pc, b_tok, b_tinit, b_zrow, b_tab0, b_hdz = P.bufs(8)
    b_hd = P.bufs(18); b_cnt = P.buf(); b_big = P.buf(); b_nblk = P.buf(); b_pad = P.buf(); b_ca = P.buf(); b_cb = P.buf()
    b_ps = P.buf(); b_same = P.buf(); b_bexp = P.buf(); b_widx = P.buf(); b_carry = P.buf(); b_tA = P.buf(); b_tB = P.buf()
    b_destf = P.bufs(18); b_desti = P.buf(); b_sc = P.bufs(36)
    b_idx = P.bufs(2); b_xg = P.bufs(2); b_xgt = P.buf(); b_ssb = P.buf(); b_hb = P.buf(); b_hT = P.buf(); b_ys = P.bufs(2)
    b_yd = P.bufs(NB); b_y1 = P.buf(); b_y2 = P.buf()

    regs = {}

    def _mkreg(e):
        regs['b'] = e.alloc_register("bnd")
        return e.reg_mov(regs['b'], 4095)
    P.pool(_mkreg)
    P.pool(lambda e: e.memset(epsc[:], EPS), writes=[b_eps])
    P.pool(lambda e: e.memset(onesf[:], 1.0), writes=[b_ones])
    P.pool(lambda e: e.affine_select(out=onesf[:], in_=onesf[:], pattern=[[1, 128]], compare_op=ALU.is_gt, fill=0.0, base=0,
                                     channel_multiplier=-1), reads=[b_ones], writes=[b_ones])
    P.dve(lambda e: e.tensor_copy(out=Ub[:], in_=onesf[:]), reads=[b_ones], writes=[b_U])
    P.pool(lambda e: e.memset(onesb[:], 1.0), writes=[b_U])
    P.pool(lambda e: e.memset(tinit[:], 2304), writes=[b_tinit])
    P.pool(lambda e: e.memset(zrow[:], 0.0), writes=[b_zrow])
    P.pool(lambda e: e.memset(carry[:], 0.0), writes=[b_carry])
    P.dma("sp", lambda e: e.dma_start(out=wrt[:], in_=wrd.rearrange("(k p) n -> p k n", p=128)), writes=[b_wr])
    P.dma("sp", lambda e: e.dma_start(out=brt[:], in_=brd), writes=[b_br])
    P.dma("sp", lambda e: e.dma_start(out=bvals[:].rearrange("p b e -> p (b e)"), in_=bvd), writes=[b_bv])
    P.dma("sp", lambda e: e.dma_start(out=thr[:].rearrange("p e j -> p (e j)"), in_=thrd), writes=[b_thr])
    P.dma("sp", lambda e: e.dma_start(out=pcol[:], in_=pcd), writes=[b_pc])
    P.dma("sp", lambda e: e.dma_start(out=tokid[:], in_=tokd), writes=[b_tok])
    P.dma("sp", lambda e: e.dma_start(out=table.rearrange("(b p) c -> p b c", p=128), in_=tinit[:]), reads=[b_tinit], writes=[b_tab0])
    P.dma("sp", lambda e: e.dma_start(out=Hd[2304:2305, :], in_=zrow[:]), reads=[b_zrow], writes=[b_hdz])
    P.dve(lambda e: e.tensor_copy(out=identb[:], in_=identf[:]), reads=[b_identf], writes=[b_idb])
    P.dve(lambda e: e.tensor_copy(out=wrh[:], in_=wrt[:]), reads=[b_wr], writes=[b_wrh])
    P.dve(lambda e: e.tensor_tensor(out=wrl[:], in0=wrt[:], in1=wrh[:], op=ALU.subtract), reads=[b_wr, b_wrh], writes=[b_wrl])

    def sc(i):
        return sm[:, i:i + 1]

    for t in range(18):
        s = 0 if t < 16 else 1
        if t == 0 or t == 16:
            P.dma("sp", lambda e, s=s: e.dma_start(out=rows[:, 0, :], in_=rowsd[s, 0]), writes=[b_rows])
            P.dma("sp", lambda e, s=s: e.dma_start(out=rows[:, 1, :], in_=rowsd[s, 1]), writes=[b_rows])
        P.dma("sp", lambda e, t=t: e.dma_start(out=xs[:], in_=xt[t]), writes=[b_xs])
        P.act(lambda e, t=t: e.activation(out=junk[:], in_=xs[:], func=AF.Square, accum_out=ss[:, t:t + 1]),
              reads=[b_xs], writes=[b_junk, b_ss[t]])
        P.act(lambda e, t=t: e.activation(out=rstd[:, t:t + 1], in_=ss[:, t:t + 1], func=AF.Sqrt, scale=1.0 / 1024, bias=epsc[:, 0:1]),
              reads=[b_ss[t], b_eps], writes=[b_rstd[t]])
        P.dve(lambda e, t=t: e.reciprocal(out=rstd[:, t:t + 1], in_=rstd[:, t:t + 1]), reads=[b_rstd[t]], writes=[b_rstd[t]])
        P.dve(lambda e, t=t: e.scalar_tensor_tensor(out=h32[:], in0=xs[:], scalar=rstd[:, t:t + 1], in1=rows[:, 0, :],
                                                    op0=ALU.mult, op1=ALU.mult), reads=[b_xs, b_rstd[t], b_rows], writes=[b_h32])
        P.pool(lambda e: e.tensor_tensor(out=h32[:], in0=h32[:], in1=rows[:, 1, :], op=ALU.add), reads=[b_h32, b_rows], writes=[b_h32])
        P.dve(lambda e: e.tensor_copy(out=hhi[:], in_=h32[:]), reads=[b_h32], writes=[b_hhi])
        P.dve(lambda e: e.tensor_tensor(out=hlo[:], in0=h32[:], in1=hhi[:], op=ALU.subtract), reads=[b_h32, b_hhi], writes=[b_hlo])
        for k in range(8):
            P.pe(lambda e, k=k: e.transpose(out=PTB[:, k, :], in_=hhi[:, k * 128:(k + 1) * 128], identity=identb[:]),
                 reads=[b_hhi, b_idb], writes=[b_ptb])
        P.dma("sp", lambda e, t=t: e.dma_start(out=Hd[t * 128:(t + 1) * 128, :], in_=hhi[:]), reads=[b_hhi], writes=[b_hd[t]])
        P.dve(lambda e: e.tensor_copy(out=hThi[:], in_=PTB[:]), reads=[b_ptb], writes=[b_hthi])
        for k in range(8):
            P.pe(lambda e, k=k: e.transpose(out=PTB[:, k, :], in_=hlo[:, k * 128:(k + 1) * 128], identity=identb[:]),
                 reads=[b_hlo, b_idb], writes=[b_ptb])
        P.act(lambda e: e.activation(out=hTlo[:], in_=PTB[:], func=AF.Copy), reads=[b_ptb], writes=[b_htlo])
        first = True
        for k in range(8):
            for (lh, bl, rw, bw) in ((0, b_hthi, wrh, b_wrh), (0, b_hthi, wrl, b_wrl), (1, b_htlo, wrh, b_wrh)):
                lhs = hThi[:, k, :] if lh == 0 else hTlo[:, k, :]
                last = (k == 7 and lh == 1)
                P.pe(lambda e, lhs=lhs, rw=rw, k=k, first=first, last=last: e.matmul(PRG[:, 0:36], lhsT=lhs, rhs=rw[:, k, :], start=first, stop=last),
                     reads=[bl, bw], writes=[b_prg])
                first = False
        R = [b_sm, b_lg]
        P.dve(lambda e: e.tensor_tensor(out=lg[:], in0=PRG[:, 0:36], in1=brt[:], op=ALU.add), reads=[b_prg, b_br], writes=[b_lg])
        P.dve(lambda e: e.tensor_reduce(out=sc(0), in_=lg[:, 0:4], axis=AX.X, op=ALU.max), reads=R, writes=[b_sm])
        P.dve(lambda e: e.tensor_scalar(out=sc(1), in0=sc(0), scalar1=-1.0, scalar2=None, op0=ALU.mult), reads=R, writes=[b_sm])
        P.act(lambda e: e.activation(out=sm[:, 8:12], in_=lg[:, 0:4], func=AF.Exp, bias=sc(1), scale=1.0, accum_out=sc(2)),
              reads=R, writes=[b_sm])
        P.dve(lambda e: e.reciprocal(out=sc(3), in_=sc(2)), reads=R, writes=[b_sm])
        P.dve(lambda e: e.tensor_scalar(out=sm[:, 12:16], in0=lg[:, 0:4], scalar1=sc(0), scalar2=None, op0=ALU.is_equal),
              reads=R, writes=[b_sm])
        P.dve(lambda e: e.tensor_scalar(out=sm[:, 16:24], in0=lg[:, 4:12], scalar1=sm[:, 12:13], scalar2=None, op0=ALU.mult),
              reads=R, writes=[b_sm])
        for g in range(1, 4):
            P.dve(lambda e, g=g: e.scalar_tensor_tensor(out=sm[:, 16:24], in0=lg[:, 4 + 8 * g:12 + 8 * g], scalar=sm[:, 12 + g:13 + g],
                                                        in1=sm[:, 16:24], op0=ALU.mult, op1=ALU.add), reads=R, writes=[b_sm])
        P.dve(lambda e: e.tensor_reduce(out=sc(4), in_=sm[:, 16:24], axis=AX.X, op=ALU.max), reads=R, writes=[b_sm])
        P.dve(lambda e: e.tensor_scalar(out=sm[:, 24:32], in0=sm[:, 16:24], scalar1=sc(4), scalar2=None, op0=ALU.is_equal),
              reads=R, writes=[b_sm])
        P.dve(lambda e: e.scalar_tensor_tensor(out=sm[:, 32:40], in0=sm[:, 24:32], scalar=-1e30, in1=sm[:, 16:24],
                                               op0=ALU.mult, op1=ALU.add), reads=R, writes=[b_sm])
        P.dve(lambda e: e.tensor_reduce(out=sc(5), in_=sm[:, 32:40], axis=AX.X, op=ALU.max), reads=R, writes=[b_sm])
        P.dve(lambda e: e.tensor_scalar(out=sm[:, 40:48], in0=sm[:, 32:40], scalar1=sc(5), scalar2=None, op0=ALU.is_equal),
              reads=R, writes=[b_sm])
        P.dve(lambda e: e.tensor_tensor(out=sc(6), in0=sc(5), in1=sc(4), op=ALU.subtract), reads=R, writes=[b_sm])
        P.act(lambda e: e.activation(out=sc(7), in_=sc(6), func=AF.Exp), reads=R, writes=[b_sm])
        P.dve(lambda e: e.tensor_scalar(out=sc(7), in0=sc(7), scalar1=1.0, scalar2=None, op0=ALU.add), reads=R, writes=[b_sm])
        P.dve(lambda e: e.reciprocal(out=sc(7), in_=sc(7)), reads=R, writes=[b_sm])
        P.dve(lambda e: e.tensor_tensor(out=sc(48), in0=sc(7), in1=sc(3), op=ALU.mult), reads=R, writes=[b_sm])
        P.dve(lambda e: e.tensor_tensor(out=sc(49), in0=sc(3), in1=sc(48), op=ALU.subtract), reads=R, writes=[b_sm])
        P.dve(lambda e: e.tensor_scalar(out=sm[:, 50:58], in0=sm[:, 24:32], scalar1=sc(48), scalar2=None, op0=ALU.mult),
              reads=R, writes=[b_sm])
        P.dve(lambda e: e.scalar_tensor_tensor(out=sm[:, 50:58], in0=sm[:, 40:48], scalar=sc(49), in1=sm[:, 50:58],
                                               op0=ALU.mult, op1=ALU.add), reads=R, writes=[b_sm])
        for g in range(4):
            P.dve(lambda e, g=g, t=t: e.tensor_scalar(out=M12[:, t, 0, 8 * g:8 * g + 8], in0=sm[:, 24:32], scalar1=sm[:, 12 + g:13 + g],
                                                      scalar2=None, op0=ALU.mult), reads=R, writes=[b_M[t]])
            P.dve(lambda e, g=g, t=t: e.tensor_scalar(out=M12[:, t, 1, 8 * g:8 * g + 8], in0=sm[:, 40:48], scalar1=sm[:, 12 + g:13 + g],
                                                      scalar2=None, op0=ALU.mult), reads=R, writes=[b_M[t]])
        P.dve(lambda e, t=t: e.tensor_copy(out=g12[:, t, :], in_=sm[:, 48:50]), reads=R, writes=[b_g12[t]])
        P.dve(lambda e, t=t: e.tensor_tensor(out=Msum[:, t, :], in0=M12[:, t, 0, :], in1=M12[:, t, 1, :], op=ALU.add),
              reads=[b_M[t]], writes=[b_ms[t]])

    for t in range(18):
        P.pe(lambda e, t=t: e.matmul(PRG[:, 0:32], lhsT=onesb[:, :], rhs=Msum[:, t, :], start=(t == 0), stop=(t == 17)),
             reads=[b_U, b_ms[t]], writes=[b_prg])
    P.dve(lambda e: e.tensor_copy(out=cnt[:], in_=PRG[:, 0:32]), reads=[b_prg], writes=[b_cnt])
    big3 = big[:, 0:32 * 18].rearrange("p (e j) -> p e j", j=18)
    P.dve(lambda e: e.tensor_tensor(out=big3, in0=cnt[:, :].unsqueeze(2).to_broadcast([128, 32, 18]), in1=thr[:], op=ALU.is_gt),
          reads=[b_cnt, b_thr], writes=[b_big])
    P.dve(lambda e: e.tensor_reduce(out=nblk[:], in_=big3, axis=AX.X, op=ALU.add), reads=[b_big], writes=[b_nblk])
    P.dve(lambda e: e.tensor_scalar(out=padded[:], in0=nblk[:], scalar1=128.0, scalar2=None, op0=ALU.mult), reads=[b_nblk], writes=[b_pad])
    P.dve(lambda e: e.tensor_copy(out=ca[:], in_=padded[:]), reads=[b_pad], writes=[b_ca])
    src, bsrc, dst, bdst = ca, b_ca, cb, b_cb
    for sft in (1, 2, 4, 8, 16):
        P.dve(lambda e, src=src, dst=dst, sft=sft: e.tensor_tensor(out=dst[:, sft:32], in0=src[:, sft:32], in1=src[:, 0:32 - sft], op=ALU.add),
              reads=[bsrc], writes=[bdst])
        P.dve(lambda e, src=src, dst=dst, sft=sft: e.tensor_copy(out=dst[:, 0:sft], in_=src[:, 0:sft]), reads=[bsrc], writes=[bdst])
        src, bsrc, dst, bdst = dst, bdst, src, bsrc
    pend, b_pend = src, bsrc
    P.dve(lambda e: e.tensor_tensor(out=pstart[:], in0=pend[:], in1=padded[:], op=ALU.subtract), reads=[b_pend, b_pad], writes=[b_ps])
    big4 = big[:].rearrange("p (b e) -> p b e", e=32)
    P.dve(lambda e: e.tensor_tensor(out=big4, in0=pend[:, :].unsqueeze(1).to_broadcast([128, NB, 32]), in1=bvals[:], op=ALU.is_le),
          reads=[b_pend, b_bv, b_nblk], writes=[b_big])
    P.dve(lambda e: e.tensor_reduce(out=bexp[:], in_=big4, axis=AX.X, op=ALU.add), reads=[b_big], writes=[b_bexp])
    P.dve(lambda e: e.tensor_scalar(out=bexp[:], in0=bexp[:], scalar1=31.0, scalar2=None, op0=ALU.min), reads=[b_bexp], writes=[b_bexp])
    P.dve(lambda e: e.memset(same[:], 0.0), writes=[b_same])
    P.dve(lambda e: e.tensor_tensor(out=same[:, 1:NB], in0=bexp[:, 1:NB], in1=bexp[:, 0:NB - 1], op=ALU.is_equal),
          reads=[b_bexp], writes=[b_same])
    P.dve(lambda e: e.tensor_scalar(out=bexp[:], in0=bexp[:], scalar1=128.0, scalar2=pcol[:, 0:1], op0=ALU.mult, op1=ALU.add),
          reads=[b_bexp, b_pc, b_same], writes=[b_bexp])
    P.dve(lambda e: e.scalar_tensor_tensor(out=bexp[:], in0=same[:], scalar=8192.0, in1=bexp[:], op0=ALU.mult, op1=ALU.add),
          reads=[b_bexp, b_same], writes=[b_bexp])
    P.dve(lambda e: e.tensor_copy(out=widx[:], in_=bexp[:]), reads=[b_bexp], writes=[b_widx])
    for t in range(18):
        P.pe(lambda e, t=t: e.matmul(PRK[:, 0:32], lhsT=Ub[:, :], rhs=Msum[:, t, :], start=True, stop=True), reads=[b_U, b_ms[t]], writes=[b_prk])
        P.pe(lambda e, t=t: e.matmul(PRK[:, 32:64], lhsT=onesb[:, :], rhs=Msum[:, t, :], start=True, stop=True), reads=[b_U, b_ms[t]], writes=[b_prk])
        P.dve(lambda e: e.tensor_tensor(out=tA[:], in0=PRK[:, 0:32], in1=carry[:], op=ALU.add), reads=[b_prk, b_carry], writes=[b_tA])
        P.dve(lambda e: e.tensor_tensor(out=tA[:], in0=tA[:], in1=pstart[:], op=ALU.add), reads=[b_tA, b_ps], writes=[b_tA])
        for k in range(2):
            P.dve(lambda e, t=t, k=k: e.tensor_tensor(out=tB[:], in0=tA[:], in1=M12[:, t, k, :], op=ALU.mult), reads=[b_tA, b_M[t]], writes=[b_tB])
            P.dve(lambda e, t=t, k=k: e.tensor_reduce(out=destf[:, t, k:k + 1], in_=tB[:], axis=AX.X, op=ALU.add), reads=[b_tB], writes=[b_destf[t]])
        P.dve(lambda e: e.tensor_tensor(out=carry[:], in0=carry[:], in1=PRK[:, 32:64], op=ALU.add), reads=[b_prk, b_carry], writes=[b_carry])
    P.dve(lambda e: e.tensor_copy(out=desti[:], in_=destf[:]), reads=b_destf, writes=[b_desti])
    for t in range(18):
        for k in range(2):
            P.dma("pool", lambda e, t=t, k=k: e.indirect_dma_start(out=table, out_offset=bass.IndirectOffsetOnAxis(ap=desti[:, t, k:k + 1], axis=0),
                                                                   in_=tokid[:, t, :], in_offset=None),
                  reads=[b_desti, b_tok, b_tab0], writes=[b_sc[2 * t + k]])

    def gather_x(b):
        i = b % 2
        P.dma("sp", lambda e, b=b, i=i: e.dma_start(out=IDX[i][:], in_=table[b * 128:(b + 1) * 128, :]), reads=b_sc + [b_tab0], writes=[b_idx[i]])
        P.dma("pool", lambda e, i=i: e.indirect_dma_start(out=XG[i][:], out_offset=None, in_=Hd,
                                                          in_offset=bass.IndirectOffsetOnAxis(ap=IDX[i][:, 0:1], axis=0)),
              reads=[b_idx[i], b_hdz] + b_hd, writes=[b_xg[i]])

    gather_x(0)
    for b in range(NB):
        i = b % 2
        xg = XG[i]; ys = YS[i]
        for (wg, bwg, wl) in ((WG1[i], b_wg1[i], w1L), (WG3[i], b_wg3[i], w3L), (WG2[i], b_wg2[i], w2L)):
            P.dma("pool", lambda e, wg=wg, wl=wl, b=b: e.indirect_dma_start(out=wg[:], out_offset=None, in_=wl,
                                                                            in_offset=bass.IndirectOffsetOnAxis(ap=widx[:, b:b + 1], axis=0),
                                                                            bounds_check=regs['b'], oob_is_err=False),
                  reads=[b_widx], writes=[bwg])
        if b + 1 < NB:
            gather_x(b + 1)
        P.act(lambda e, i=i: e.activation(out=W1[i][:].rearrange("p k f -> p (k f)"), in_=WG1[i][:], func=AF.Copy), reads=[b_wg1[i]], writes=[b_w1[i]])
        P.act(lambda e, i=i: e.activation(out=W3[i][:].rearrange("p k f -> p (k f)")[:, 0:2048], in_=WG3[i][:, 0:2048], func=AF.Copy), reads=[b_wg3[i]], writes=[b_w3[i]])
        P.dve(lambda e, i=i: e.tensor_copy(out=W3[i][:].rearrange("p k f -> p (k f)")[:, 2048:4096], in_=WG3[i][:, 2048:4096]), reads=[b_wg3[i]], writes=[b_w3[i]])
        P.dve(lambda e, i=i: e.tensor_copy(out=W2[i][:].rearrange("p k f -> p (k f)"), in_=WG2[i][:]), reads=[b_wg2[i]], writes=[b_w2[i]])
        for k in range(8):
            P.pe(lambda e, k=k, xg=xg: e.transpose(out=PTB[:, k, :], in_=xg[:, k * 128:(k + 1) * 128], identity=identb[:]),
                 reads=[b_xg[i], b_idb], writes=[b_ptb])
        P.act(lambda e: e.activation(out=XgT[:], in_=PTB[:], func=AF.Copy), reads=[b_ptb], writes=[b_xgt])
        for k in range(8):
            P.pe(lambda e, k=k, i=i: e.matmul(PH1[:, :], lhsT=XgT[:, k, :], rhs=W1[i][:, k, :], start=(k == 0), stop=(k == 7)),
                 reads=[b_xgt, b_w1[i]], writes=[b_ph1])
        for k in range(8):
            P.pe(lambda e, k=k, i=i: e.matmul(PH3[:, :], lhsT=XgT[:, k, :], rhs=W3[i][:, k, :], start=(k == 0), stop=(k == 7)),
                 reads=[b_xgt, b_w3[i]], writes=[b_ph3])
        P.act(lambda e: e.activation(out=ssb[:], in_=PH1[:, :], func=AF.Silu), reads=[b_ph1], writes=[b_ssb])
        P.dve(lambda e: e.tensor_tensor(out=hb[:], in0=PH3[:, :], in1=ssb[:], op=ALU.mult), reads=[b_ph3, b_ssb], writes=[b_hb])
        for f in range(4):
            P.pe(lambda e, f=f: e.transpose(out=PTH[:, f, :], in_=hb[:, f * 128:(f + 1) * 128], identity=identb[:]),
                 reads=[b_hb, b_idb], writes=[b_pth])
        P.act(lambda e: e.activation(out=hT[:], in_=PTH[:], func=AF.Copy), reads=[b_pth], writes=[b_hT])
        for h in range(2):
            for f in range(4):
                P.pe(lambda e, h=h, f=f, i=i: e.matmul(PY[:, h * 512:(h + 1) * 512], lhsT=hT[:, f, :], rhs=W2[i][:, f, h * 512:(h + 1) * 512],
                                                      start=(f == 0), stop=(f == 3)), reads=[b_hT, b_w2[i]], writes=[b_py])
        P.dve(lambda e, ys=ys: e.tensor_copy(out=ys[:], in_=PY[:]), reads=[b_py], writes=[b_ys[i]])
        P.dma("sp", lambda e, ys=ys, b=b: e.dma_start(out=Yd[b * 128:(b + 1) * 128, :], in_=ys[:]), reads=[b_ys[i]], writes=[b_yd[b]])

    outs = []
    for t in range(18):
        s = 0 if t < 16 else 1
        if t == 0 or t == 16:
            P.dma("sp", lambda e, s=s: e.dma_start(out=rows[:, 0, :], in_=rowsd[s, 2]), writes=[b_rows])
        P.dma("sp", lambda e, t=t: e.dma_start(out=xs[:], in_=xt[t]), writes=[b_xs])
        P.dma("pool", lambda e, t=t: e.indirect_dma_start(out=Y1[:], out_offset=None, in_=Yd,
                                                          in_offset=bass.IndirectOffsetOnAxis(ap=desti[:, t, 0:1], axis=0)),
              reads=[b_desti] + b_yd, writes=[b_y1])
        P.dma("pool", lambda e, t=t: e.indirect_dma_start(out=Y2[:], out_offset=None, in_=Yd,
                                                          in_offset=bass.IndirectOffsetOnAxis(ap=desti[:, t, 1:2], axis=0)),
              reads=[b_desti] + b_yd, writes=[b_y2])
        P.dve(lambda e, t=t: e.tensor_scalar(out=h32[:], in0=Y1[:], scalar1=g12[:, t, 0:1], scalar2=None, op0=ALU.mult),
              reads=[b_y1, b_g12[t]], writes=[b_h32])
        P.dve(lambda e, t=t: e.scalar_tensor_tensor(out=h32[:], in0=Y2[:], scalar=g12[:, t, 1:2], in1=h32[:], op0=ALU.mult, op1=ALU.add),
              reads=[b_y2, b_g12[t], b_h32], writes=[b_h32])
        P.dve(lambda e: e.tensor_tensor(out=h32[:], in0=h32[:], in1=rows[:, 0, :], op=ALU.mult), reads=[b_h32, b_rows], writes=[b_h32])
        P.pool(lambda e: e.tensor_tensor(out=xs[:], in0=xs[:], in1=h32[:], op=ALU.add), reads=[b_h32, b_xs], writes=[b_xs])
        outs.append(P.dma("sp", lambda e, t=t: e.dma_start(out=xo[t], in_=xs[:]), reads=[b_xs]))
    P.emit(final_wait_ops=outs)
    return nc


def build_att():
    nc = bass.Bass("TRN2", target_bir_lowering=False)
    P = Prog(nc)
    xt = P.din("xt", [20, 128, 1024], F32)
    mcols = P.din("mcols", [128, 2, 2, 8], F32)
    wqkv = P.din("wqkv", [1024, 1536], F32)
    gqd = P.din("gq", [128, 64], F32)
    gkd = P.din("gk", [128, 64], F32)
    roped = P.din("rope", [18, 128, 2, 32], F32)
    sinkd = P.din("sink", [128, 16], F32)
    maskd = P.din("masks", [128, 4, 128], F32)
    OT = P.dout("OT", [1024, 2304], BF16)

    wb = P.sb("wb", [128, 8, 1536], BF16)
    mc = P.sb("mc", [128, 2, 2, 8], F32)
    gq = P.sb("gqt", [128, 64], F32)
    gk = P.sb("gkt", [128, 64], F32)
    rope = P.sb("ropet", [128, 18, 2, 32], F32)
    esink = P.sb("esink", [128, 16], F32)
    masks = P.sb("maskst", [128, 4, 128], BF16)
    epsc = P.sb("epsc", [128, 1], F32)
    xs = P.sb("xs", [128, 1024], F32)
    junk = P.sb("junk", [128, 1024], BF16)
    xn = P.sb("xn", [128, 1024], BF16)
    ss = P.sb("ss", [128, 20], F32)
    rstd = P.sb("rstd", [128, 20], F32)
    hT = P.sb("hT", [128, 8, 128], BF16)
    qf = P.sb("qf", [128, 1024], F32)
    kf = P.sb("kf", [128, 256], F32)
    tmp = P.sb("tmp", [128, 1024], F32)
    tmp2 = P.sb("tmp2", [128, 512], F32)
    tmp3 = P.sb("tmp3", [128, 512], F32)
    sq = P.sb("sq", [128, 20], F32)
    qr = P.sb("qr", [128, 1024], BF16)
    kr = P.sb("kr", [128, 256], BF16)
    QT = P.sb("QT", [64, 18, 16, 128], BF16)
    KT = P.sb("KT", [64, 20, 4, 128], BF16)
    VX = P.sb("VX", [128, 20, 4, 65], BF16)
    PTS = [P.sb(f"pts{i}", [128, 5, 512], BF16) for i in range(2)]
    den = P.sb("den", [128, 8], F32)
    osb = P.sb("osb", [128, 1024], BF16)
    ots = P.sb("ots", [128, 8, 128], BF16)
    ident, b_ident = make_ident(P)

    b_w, b_mc, b_gq, b_gk, b_rope, b_esink, b_masks, b_eps = P.bufs(8)
    b_xs, b_junk, b_xn, b_hT, b_qf, b_kf, b_tmp, b_tmp2, b_tmp3, b_sq, b_qr, b_kr = P.bufs(12)
    b_ss = P.bufs(20); b_rstd = P.bufs(20)
    b_QT = P.bufs(18); b_KT = P.bufs(20); b_VX = P.bufs(20); b_vone = P.buf()
    b_pts = P.bufs(2); b_den = P.buf(); b_osb = P.buf(); b_ots = P.buf()
    b_phase = P.buf()

    P.dma("pool", lambda e: e.dma_start(out=wb[:], in_=wqkv.rearrange("(k p) n -> p k n", p=128)), writes=[b_w])
    P.dma("sp", lambda e: e.dma_start(out=mc[:], in_=mcols), writes=[b_mc])
    P.dma("sp", lambda e: e.dma_start(out=gq[:], in_=gqd), writes=[b_gq])
    P.dma("sp", lambda e: e.dma_start(out=gk[:], in_=gkd), writes=[b_gk])
    P.dma("sp", lambda e: e.dma_start(out=rope[:], in_=roped.rearrange("t p a i -> p t a i")), writes=[b_rope])
    P.dma("sp", lambda e: e.dma_start(out=esink[:], in_=sinkd), writes=[b_esink])
    P.dma("pool", lambda e: e.dma_start(out=masks[:], in_=maskd), writes=[b_masks])
    P.pool(lambda e: e.memset(epsc[:], EPS), writes=[b_eps])
    P.act(lambda e: e.activation(out=esink[:], in_=esink[:], func=AF.Exp), reads=[b_esink], writes=[b_esink])
    P.dve(lambda e: e.tensor_scalar(out=gq[:], in0=gq[:], scalar1=0.125, scalar2=None, op0=ALU.mult), reads=[b_gq], writes=[b_gq])
    P.pool(lambda e: e.memset(VX[:], 1.0), writes=[b_vone])

    ps1 = ExitStack()
    PTx = ps1.enter_context(nc.psum_tensor("ptx", [128, 8, 128], BF16))
    PQ = ps1.enter_context(nc.psum_tensor("pq", [128, 1024], F32))
    PKV = ps1.enter_context(nc.psum_tensor("pkv", [128, 512], F32))
    PTq = ps1.enter_context(nc.psum_tensor("ptq", [64, 16, 128], BF16))
    PTk = ps1.enter_context(nc.psum_tensor("ptk", [64, 4, 128], BF16))
    b_ptx, b_pq, b_pkv, b_ptq, b_ptk = P.bufs(5)

    def qknorm_rope(src, H, gain, bgain, dst, bdst, t, do_rope, bsrc):
        W = H * 64
        P.dve(lambda e: e.tensor_tensor(out=tmp[:, 0:W], in0=src[:, 0:W], in1=src[:, 0:W], op=ALU.mult), reads=[bsrc], writes=[b_tmp])
        P.dve(lambda e: e.tensor_reduce(out=sq[:, 0:H], in_=tmp[:, 0:W].rearrange("p (h d) -> p h d", d=64), axis=AX.X, op=ALU.add),
              reads=[b_tmp], writes=[b_sq])
        P.act(lambda e: e.activation(out=sq[:, 0:H], in_=sq[:, 0:H], func=AF.Sqrt, scale=1.0 / 64, bias=epsc[:, 0:1]),
              reads=[b_sq, b_eps], writes=[b_sq])
        P.dve(lambda e: e.reciprocal(out=sq[:, 0:H], in_=sq[:, 0:H]), reads=[b_sq], writes=[b_sq])
        s3 = src[:, 0:W].rearrange("p (h d) -> p h d", d=64)
        t3 = tmp[:, 0:W].rearrange("p (h d) -> p h d", d=64)
        P.dve(lambda e: e.tensor_tensor(out=t3, in0=s3, in1=sq[:, 0:H].unsqueeze(2).to_broadcast([128, H, 64]), op=ALU.mult),
              reads=[bsrc, b_sq], writes=[b_tmp])
        if not do_rope:
            d3 = dst[:, 0:W].rearrange("p (h d) -> p h d", d=64)
            P.dve(lambda e: e.tensor_tensor(out=d3, in0=t3, in1=gain[:, :].unsqueeze(1).to_broadcast([128, H, 64]), op=ALU.mult),
                  reads=[b_tmp, bgain], writes=[bdst])
            return
        P.dve(lambda e: e.tensor_tensor(out=t3, in0=t3, in1=gain[:, :].unsqueeze(1).to_broadcast([128, H, 64]), op=ALU.mult),
              reads=[b_tmp, bgain], writes=[b_tmp])
        x5 = tmp[:, 0:W].rearrange("p (h a f i) -> p h a f i", a=2, f=2, i=16)
        d5 = dst[:, 0:W].rearrange("p (h a f i) -> p h a f i", a=2, f=2, i=16)
        x1 = x5[:, :, :, 0, :]; x2 = x5[:, :, :, 1, :]
        cs = rope[:, t, 0, :].rearrange("p (a i) -> p a i", a=2).unsqueeze(1).to_broadcast([128, H, 2, 16])
        sn = rope[:, t, 1, :].rearrange("p (a i) -> p a i", a=2).unsqueeze(1).to_broadcast([128, H, 2, 16])
        n = H * 32
        a4 = tmp2[:, 0:n].rearrange("p (h a i) -> p h a i", a=2, i=16)
        b4 = tmp3[:, 0:n].rearrange("p (h a i) -> p h a i", a=2, i=16)
        P.dve(lambda e: e.tensor_tensor(out=a4, in0=x1, in1=cs, op=ALU.mult), reads=[b_tmp, b_rope], writes=[b_tmp2])
        P.dve(lambda e: e.tensor_tensor(out=b4, in0=x2, in1=sn, op=ALU.mult), reads=[b_tmp, b_rope], writes=[b_tmp3])
        P.dve(lambda e: e.tensor_tensor(out=d5[:, :, :, 0, :], in0=a4, in1=b4, op=ALU.subtract), reads=[b_tmp2, b_tmp3], writes=[bdst])
        P.dve(lambda e: e.tensor_tensor(out=a4, in0=x1, in1=sn, op=ALU.mult), reads=[b_tmp, b_rope, bdst], writes=[b_tmp2])
        P.dve(lambda e: e.tensor_tensor(out=b4, in0=x2, in1=cs, op=ALU.mult), reads=[b_tmp, b_rope, bdst], writes=[b_tmp3])
        P.dve(lambda e: e.tensor_tensor(out=d5[:, :, :, 1, :], in0=a4, in1=b4, op=ALU.add), reads=[b_tmp2, b_tmp3], writes=[bdst])

    qidx = {}
    for t in range(20):
        s = 0 if t < 18 else 1
        is_q = (1 <= t <= 16) or t >= 18
        do_rope = t < 18
        P.dma("sp", lambda e, t=t: e.dma_start(out=xs[:], in_=xt[t]), writes=[b_xs])
        P.act(lambda e, t=t: e.activation(out=junk[:], in_=xs[:], func=AF.Square, accum_out=ss[:, t:t + 1]),
              reads=[b_xs], writes=[b_junk, b_ss[t]])
        P.act(lambda e, t=t: e.activation(out=rstd[:, t:t + 1], in_=ss[:, t:t + 1], func=AF.Sqrt, scale=1.0 / 1024, bias=epsc[:, 0:1]),
              reads=[b_ss[t], b_eps], writes=[b_rstd[t]])
        P.dve(lambda e, t=t: e.reciprocal(out=rstd[:, t:t + 1], in_=rstd[:, t:t + 1]), reads=[b_rstd[t]], writes=[b_rstd[t]])
        P.dve(lambda e, t=t: e.tensor_scalar(out=xn[:], in0=xs[:], scalar1=rstd[:, t:t + 1], scalar2=None, op0=ALU.mult),
              reads=[b_xs, b_rstd[t]], writes=[b_xn])
        for k in range(8):
            P.pe(lambda e, k=k: e.transpose(out=PTx[:, k, :], in_=xn[:, k * 128:(k + 1) * 128], identity=ident[:]),
                 reads=[b_xn, b_ident], writes=[b_ptx])
        for k in range(8):
            P.act(lambda e, k=k, s=s: e.activation(out=hT[:, k, :], in_=PTx[:, k, :], func=AF.Identity,
                                                   scale=mc[:, s, 1, k:k + 1], bias=mc[:, s, 0, k:k + 1]),
                  reads=[b_ptx, b_mc], writes=[b_hT, b_phase])
        if is_q:
            for nb in range(2):
                for k in range(8):
                    P.pe(lambda e, nb=nb, k=k: e.matmul(PQ[:, nb * 512:(nb + 1) * 512], lhsT=hT[:, k, :], rhs=wb[:, k, nb * 512:(nb + 1) * 512],
                                                        start=(k == 0), stop=(k == 7)), reads=[b_hT, b_w], writes=[b_pq])
        for k in range(8):
            P.pe(lambda e, k=k: e.matmul(PKV[:, :], lhsT=hT[:, k, :], rhs=wb[:, k, 1024:1536], start=(k == 0), stop=(k == 7)),
                 reads=[b_hT, b_w], writes=[b_pkv])
        P.act(lambda e: e.activation(out=kf[:], in_=PKV[:, 0:256], func=AF.Copy), reads=[b_pkv], writes=[b_kf, b_phase])
        P.act(lambda e, t=t: e.activation(out=VX[:, t, :, 0:64], in_=PKV[:, 256:512].rearrange("p (j d) -> p j d", d=64), func=AF.Copy),
              reads=[b_pkv, b_vone], writes=[b_VX[t], b_phase])
        qknorm_rope(kf, 4, gk, b_gk, kr, b_kr, t, do_rope, b_kf)
        for j in range(4):
            P.pe(lambda e, j=j: e.transpose(out=PTk[:, j, :], in_=kr[:, j * 64:(j + 1) * 64], identity=ident[:]),
                 reads=[b_kr, b_ident], writes=[b_ptk])
        P.act(lambda e, t=t: e.activation(out=KT[:, t, :, :], in_=PTk[:, :, :], func=AF.Copy), reads=[b_ptk], writes=[b_KT[t], b_phase])
        if is_q:
            qi = len(qidx); qidx[t] = qi
            P.act(lambda e: e.activation(out=qf[:], in_=PQ[:], func=AF.Copy), reads=[b_pq], writes=[b_qf, b_phase])
            qknorm_rope(qf, 16, gq, b_gq, qr, b_qr, t, do_rope, b_qf)
            for h in range(16):
                P.pe(lambda e, h=h: e.transpose(out=PTq[:, h, :], in_=qr[:, h * 64:(h + 1) * 64], identity=ident[:]),
                     reads=[b_qr, b_ident], writes=[b_ptq])
            P.act(lambda e, qi=qi: e.activation(out=QT[:, qi, :, :], in_=PTq[:, :, :], func=AF.Copy), reads=[b_ptq], writes=[b_QT[qi], b_phase])
    ps1.close()

    PS = [P.ps(f"ps{i}", [128, 512], F32) for i in range(5)]
    PO = P.ps("po", [128, 4, 65], F32)
    POT = P.ps("pot", [128, 8, 128], BF16)
    b_ps = P.bufs(5); b_po = P.buf(); b_pot = P.buf()
    outs = []
    first = True
    pi = 0
    for t in list(range(1, 17)) + [18, 19]:
        qi = qidx[t]
        if t < 18:
            chunks = [(t - 1, 0 if t == 1 else 1), (t, None), (t + 1, 3 if t == 16 else 2), (18, None), (19, None)]
            col0 = (t - 1) * 128
        else:
            chunks = [(18, None), (19, None)]
            col0 = 2048 + (t - 18) * 128
        nch = len(chunks)
        for j in range(4):
            pts = PTS[pi % 2]; bpts = b_pts[pi % 2]; pi += 1
            for ci, (kt, m) in enumerate(chunks):
                wr = [b_ps[ci]] + ([b_phase] if first else [])
                first = False
                P.pe(lambda e, ci=ci, kt=kt, j=j, qi=qi: e.matmul(PS[ci][:, :], lhsT=KT[:, kt, j, :],
                                                                  rhs=QT[:, qi, 4 * j:4 * j + 4, :].rearrange("p h q -> p (h q)"),
                                                                  start=True, stop=True),
                     reads=[b_KT[kt], b_QT[qi]], writes=wr)
                P.act(lambda e, ci=ci, pts=pts: e.activation(out=pts[:, ci, :], in_=PS[ci][:, :], func=AF.Exp), reads=[b_ps[ci]], writes=[bpts])
                if m is not None:
                    P.dve(lambda e, ci=ci, pts=pts, m=m: e.tensor_tensor(out=pts[:, ci, :].rearrange("p (h q) -> p h q", h=4),
                                                                         in0=pts[:, ci, :].rearrange("p (h q) -> p h q", h=4),
                                                                         in1=masks[:, m, :].unsqueeze(1).to_broadcast([128, 4, 128]), op=ALU.mult),
                          reads=[bpts, b_masks], writes=[bpts])
            for g in range(4):
                for ci, (kt, m) in enumerate(chunks):
                    P.pe(lambda e, g=g, ci=ci, kt=kt, j=j, pts=pts, nch=nch: e.matmul(PO[:, g, :], lhsT=pts[:, ci, g * 128:(g + 1) * 128],
                                                                                      rhs=VX[:, kt, j, :], start=(ci == 0), stop=(ci == nch - 1)),
                         reads=[bpts, b_VX[kt]], writes=[b_po])
            P.dve(lambda e, j=j: e.tensor_tensor(out=den[:, 0:4], in0=PO[:, :, 64], in1=esink[:, 4 * j:4 * j + 4], op=ALU.add),
                  reads=[b_po, b_esink], writes=[b_den])
            P.dve(lambda e: e.reciprocal(out=den[:, 0:4], in_=den[:, 0:4]), reads=[b_den], writes=[b_den])
            P.dve(lambda e, j=j: e.tensor_tensor(out=osb[:, 256 * j:256 * j + 256].rearrange("p (g d) -> p g d", d=64), in0=PO[:, :, 0:64],
                                                 in1=den[:, 0:4].unsqueeze(2).to_broadcast([128, 4, 64]), op=ALU.mult),
                  reads=[b_po, b_den], writes=[b_osb])
        for k in range(8):
            P.pe(lambda e, k=k: e.transpose(out=POT[:, k, :], in_=osb[:, k * 128:(k + 1) * 128], identity=ident[:]),
                 reads=[b_osb, b_ident], writes=[b_pot])
        P.act(lambda e: e.activation(out=ots[:], in_=POT[:], func=AF.Copy), reads=[b_pot], writes=[b_ots])
        outs.append(P.dma("sp", lambda e, col0=col0: e.dma_start(out=OT.rearrange("(k p) t -> p k t", p=128)[:, :, col0:col0 + 128], in_=ots[:]),
                          reads=[b_ots]))
    P.emit(final_wait_ops=outs)
    return nc


def rope_tables(core):
    t0 = core * 2048 - 128
    t = np.arange(t0, t0 + 18 * 128)
    t = np.clip(t, 0, 16383)
    pos = np.stack([t // 64, t % 64], -1).astype(np.float32)
    freqs = (10000.0 ** (-np.arange(16, dtype=np.float32) / 16)).astype(np.float32)
    ang = pos[:, :, None] * freqs
    cs = np.cos(ang).reshape(-1, 32); sn = np.sin(ang).reshape(-1, 32)
    return np.stack([cs, sn], 1).reshape(18, 128, 2, 32).astype(np.float32)


def att_masks(core):
    j = np.arange(128)[:, None]; i = np.arange(128)[None, :]
    prev = (j >= i).astype(np.float32); nxt = (j <= i).astype(np.float32)
    m = np.stack([prev if core > 0 else np.zeros_like(prev), prev, nxt, nxt if core < 7 else np.zeros_like(nxt)], 1)
    return np.ascontiguousarray(m).astype(np.float32)


def build_mod():
    nc = bass.Bass("TRN2", target_bir_lowering=False)
    P = Prog(nc)
    ccd = P.din("cc", [128, 8, 2], F32)
    wmd = P.din("wm", [4, 1024, 768], F32)
    bmd = P.din("bm", [1, 4, 768], F32)
    gmd = P.din("gm", [1, 4, 256], F32)
    mo = P.dout("mo", [4, 2, 768], F32)
    cc = P.sb("cct", [128, 8, 2], F32)
    S = P.sb("S", [128, 8, 33], F32)
    Sh = P.sb("Sh", [128, 8, 33], BF16)
    Sl = P.sb("Sl", [128, 8, 33], BF16)
    brow = P.sb("brow", [33, 4, 768], F32)
    grow = P.sb("grow", [33, 4, 256], F32)
    WT = [P.sb(f"wt{i}", [128, 8, 768], F32) for i in range(2)]
    Wh = P.sb("Wh", [128, 8, 768], BF16)
    Wl = P.sb("Wl", [128, 8, 768], BF16)
    RR = [P.sb(f"r{i}", [33, 768], F32) for i in range(2)]
    PM = P.ps("pm", [128, 1024], F32)
    b_cc, b_S, b_Sh, b_Sl, b_brow, b_grow, b_Wh, b_Wl, b_pm = P.bufs(9)
    b_wt = P.bufs(2); b_r = P.bufs(2)
    P.dma("sp", lambda e: e.dma_start(out=cc[:], in_=ccd), writes=[b_cc])
    P.pool(lambda e: e.memset(S[:], 0.0), writes=[b_S])
    P.pool(lambda e: e.memset(brow[:], 0.0), writes=[b_brow])
    P.pool(lambda e: e.memset(grow[:], 0.0), writes=[b_grow])
    for prt in (0, 32):
        P.dma("sp", lambda e, prt=prt: e.dma_start(out=brow[prt:prt + 1], in_=bmd), reads=[], writes=[b_brow])
        P.dma("sp", lambda e, prt=prt: e.dma_start(out=grow[prt:prt + 1], in_=gmd), reads=[], writes=[b_grow])
    P.act(lambda e: e.activation(out=S[:, :, 0], in_=cc[:, :, 0], func=AF.Silu), reads=[b_cc], writes=[b_S])
    P.act(lambda e: e.activation(out=S[:, :, 32], in_=cc[:, :, 1], func=AF.Silu), reads=[b_cc], writes=[b_S])
    P.dve(lambda e: e.tensor_copy(out=Sh[:], in_=S[:]), reads=[b_S], writes=[b_Sh])
    P.dve(lambda e: e.tensor_tensor(out=Sl[:], in0=S[:], in1=Sh[:], op=ALU.subtract), reads=[b_S, b_Sh], writes=[b_Sl])
    outs = []
    for l in range(4):
        wt = WT[l % 2]; bwt = b_wt[l % 2]; r = RR[l % 2]; br_ = b_r[l % 2]
        P.dma("sp", lambda e, wt=wt, l=l: e.dma_start(out=wt[:], in_=wmd[l].rearrange("(k p) n -> p k n", p=128)), writes=[bwt])
        P.dve(lambda e, wt=wt: e.tensor_copy(out=Wh[:], in_=wt[:]), reads=[bwt], writes=[b_Wh])
        P.dve(lambda e, wt=wt: e.tensor_tensor(out=Wl[:], in0=wt[:], in1=Wh[:], op=ALU.subtract), reads=[bwt, b_Wh], writes=[b_Wl])
        for half in range(2):
            n = 0
            for k in range(8):
                for (sa, bsa, wa, bwa) in ((Sh, b_Sh, Wh, b_Wh), (Sh, b_Sh, Wl, b_Wl), (Sl, b_Sl, Wh, b_Wh)):
                    P.pe(lambda e, sa=sa, wa=wa, k=k, half=half, n=n: e.matmul(PM[0:33, half * 512:half * 512 + 384], lhsT=sa[:, k, :],
                                                                             rhs=wa[:, k, half * 384:(half + 1) * 384],
                                                                             start=(n == 0), stop=(n == 23)),
                         reads=[bsa, bwa], writes=[b_pm])
                    n += 1
        for half in range(2):
            P.dve(lambda e, r=r, half=half, l=l: e.tensor_tensor(out=r[:, half * 384:(half + 1) * 384], in0=PM[0:33, half * 512:half * 512 + 384],
                                                                 in1=brow[:, l, half * 384:(half + 1) * 384], op=ALU.add),
                  reads=[b_pm, b_brow], writes=[br_])
        P.dve(lambda e, r=r, l=l: e.scalar_tensor_tensor(out=r[:, 128:256], in0=r[:, 128:256], scalar=1.0, in1=grow[:, l, 0:128],
                                                         op0=ALU.add, op1=ALU.mult), reads=[br_, b_grow], writes=[br_])
        P.dve(lambda e, r=r, l=l: e.scalar_tensor_tensor(out=r[:, 512:640], in0=r[:, 512:640], scalar=1.0, in1=grow[:, l, 128:256],
                                                         op0=ALU.add, op1=ALU.mult), reads=[br_, b_grow], writes=[br_])
        outs.append(P.dma("sp", lambda e, r=r, l=l: e.dma_start(out=mo[l, 0:1, :], in_=r[0:1, :]), reads=[br_]))
        outs.append(P.dma("sp", lambda e, r=r, l=l: e.dma_start(out=mo[l, 1:2, :], in_=r[32:33, :]), reads=[br_]))
    P.emit(final_wait_ops=outs)
    return nc


def run_mod(inp, progs):
    c = np.asarray(inp['c'], np.float32).reshape(1024)
    cx = np.asarray(inp['c_ctx'], np.float32).reshape(1024)
    cc = np.ascontiguousarray(np.stack([c.reshape(8, 128).T, cx.reshape(8, 128).T], -1))
    wm6 = np.asarray(inp['w_mod'], np.float32).reshape(4, 1024, 6, 1024)
    bm6 = np.asarray(inp['b_mod'], np.float32).reshape(4, 6, 1024)
    gmix = np.asarray(inp['norm_mix_g'], np.float32); gffn = np.asarray(inp['norm_ffn_g'], np.float32)
    ins = []
    for k in range(8):
        sl = slice(128 * k, 128 * k + 128)
        ins.append(dict(cc=cc, wm=np.ascontiguousarray(wm6[:, :, :, sl]).reshape(4, 1024, 768),
                        bm=np.ascontiguousarray(bm6[:, :, sl]).reshape(1, 4, 768),
                        gm=np.ascontiguousarray(np.stack([gmix[:, sl], gffn[:, sl]], 1)).reshape(1, 4, 256)))
    res = run_bass_kernel_spmd(progs['mod'], ins, core_ids=list(range(8)))
    modx = np.zeros((4, 2, 6, 1024), np.float32)
    for k in range(8):
        modx[:, :, :, 128 * k:128 * k + 128] = np.asarray(res.results[k]['mo']).reshape(4, 2, 6, 128)
    return modx


NCORES = 8
CORES = list(range(NCORES))


def _cols(v):
    return v.reshape(8, 128).T


def _rep(v):
    return np.ascontiguousarray(np.tile(np.asarray(v, np.float32).reshape(1, -1), (128, 1)))


def _wlayout(w, kch):
    E, K, N = w.shape
    return np.ascontiguousarray(np.asarray(w, np.float32).reshape(E, kch, 128, N).transpose(0, 2, 1, 3)).reshape(E * 128, kch * N)


def _moe_consts():
    bv = np.tile((128.0 * np.arange(NB, dtype=np.float32))[:, None], (1, 32)).reshape(1, -1)
    thr = np.tile((128.0 * np.arange(18, dtype=np.float32))[None, :], (32, 1)).reshape(1, -1)
    tok = (np.arange(18)[None, :, None] * 128 + np.arange(128)[:, None, None] + np.zeros((1, 1, 16))).astype(np.int32)
    return dict(bvals=np.tile(bv, (128, 1)).astype(np.float32), thr=np.tile(thr, (128, 1)).astype(np.float32),
                pcol=np.arange(128, dtype=np.float32).reshape(128, 1), tokid=np.ascontiguousarray(tok))


def _run(nc, ins):
    res = run_bass_kernel_spmd(nc, ins, core_ids=CORES)
    return res.results


def kernel(**inp):
    inp = {k: np.asarray(v) for k, v in inp.items()}
    progs = dict(mod=build_mod(), e1=build_e1(), e2=build_e2(), e3=build_e3(), att=build_att(), moe=build_moe2())
    xl = np.ascontiguousarray(inp['x'][0], dtype=np.float32)
    xc = np.ascontiguousarray(inp['ctx'][0], dtype=np.float32)
    modx = run_mod(inp, progs)
    K1 = dft_consts()
    K2 = fft_consts()
    MC = _moe_consts()
    for layer in range(4):
        j = layer // 2
        mcols = np.zeros((128, 2, 2, 8), np.float32)
        for s in range(2):
            mcols[:, s, 0] = _cols(modx[layer, s, 0]); mcols[:, s, 1] = _cols(modx[layer, s, 1])
        g1 = np.ascontiguousarray(np.stack([_rep(modx[layer, 0, 2]), _rep(modx[layer, 1, 2])], 1))
        if layer % 2 == 0:
            cwh = np.ascontiguousarray(inp['conv_w'][j].reshape(3, 4, 128).transpose(2, 0, 1))
            ins = []
            for c in CORES:
                xt = np.zeros((19, 128, 1024), np.float32)
                xt[:16] = xl[2048 * c:2048 * (c + 1)].reshape(16, 128, 1024)
                xt[16:18] = xc.reshape(2, 128, 1024)
                if c > 0:
                    xt[18, 0] = xl[2048 * c - 1]
                if c < 7:
                    xt[18, 1] = xl[2048 * (c + 1)]
                flags = np.zeros((128, 2), np.float32); flags[:, 0] = float(c > 0); flags[:, 1] = float(c < 7)
                ins.append(dict(xt=xt, mcols=mcols, win=inp['w_in_even'][j], cw=cwh, cs128=K1['cs128'], c256=K1['c256'],
                                ns256=K1['ns256'], flags=flags))
            r1 = _run(progs['e1'], ins)
            Bfull = np.concatenate([np.asarray(r1[c]['bout']) for c in CORES], 0)
            B5 = Bfull.reshape(128, 128, 4, 2, 128)
            ins = []
            for c in CORES:
                g = c // 2; m0 = 64 * (c % 2)
                zin = np.ascontiguousarray(B5[:, :, g, :, m0:m0 + 64].transpose(0, 2, 3, 1))
                ins.append(dict(zin=zin, m1=K2['m1'], m2=K2['m2'], tw=K2['tw'], c2=K2['c2']))
            r2 = _run(progs['e2'], ins)
            fmT = np.zeros((512, 16384), dtype=Bfull.dtype)
            for c in CORES:
                g = c // 2; m0 = 64 * (c % 2)
                fmT[g * 128 + m0:g * 128 + m0 + 64] = np.asarray(r2[c]['FT']).reshape(64, 16384)
            yTs = []
            for c in CORES:
                top = np.concatenate([fmT[:, 2048 * c:2048 * (c + 1)], np.asarray(r1[c]['fcT'])], 1)
                yTs.append(np.ascontiguousarray(np.concatenate([top, np.asarray(r1[c]['ycT'])], 0)))
            wproj = inp['w_out_even'][j]
        else:
            ins = []
            for c in CORES:
                xt = np.zeros((20, 128, 1024), np.float32)
                if c > 0:
                    xt[0] = xl[2048 * c - 128:2048 * c]
                xt[1:17] = xl[2048 * c:2048 * (c + 1)].reshape(16, 128, 1024)
                if c < 7:
                    xt[17] = xl[2048 * (c + 1):2048 * (c + 1) + 128]
                xt[18:20] = xc.reshape(2, 128, 1024)
                ins.append(dict(xt=xt, mcols=mcols, wqkv=inp['w_qkv'][j], gq=_rep(inp['q_norm_g'][j]), gk=_rep(inp['k_norm_g'][j]),
                                rope=rope_tables(c), sink=_rep(inp['sink_logit'][j]), masks=att_masks(c)))
            ra = _run(progs['att'], ins)
            yTs = [np.asarray(ra[c]['OT']) for c in CORES]
            wproj = inp['w_o'][j]
        ins = []
        for c in CORES:
            xt = np.concatenate([xl[2048 * c:2048 * (c + 1)], xc], 0).reshape(18, 128, 1024)
            ins.append(dict(yT=yTs[c], xt=np.ascontiguousarray(xt), g1=g1, w=wproj))
        r3 = _run(progs['e3'], ins)
        rows = np.zeros((2, 3, 128, 1024), np.float32)
        for s in range(2):
            rows[s, 0] = _rep(modx[layer, s, 4]); rows[s, 1] = _rep(modx[layer, s, 3]); rows[s, 2] = _rep(modx[layer, s, 5])
        wr = np.ascontiguousarray(np.concatenate([inp['w_router_g'][layer], inp['w_router_e'][layer]], 1))
        br = _rep(np.concatenate([inp['b_router_g'][layer], inp['b_router_e'][layer]]))
        w1L = _wlayout(inp['w1'][layer], 8); w3L = _wlayout(inp['w3'][layer], 8); w2L = _wlayout(inp['w2'][layer], 4)
        ins = []
        for c in CORES:
            ins.append(dict(xt=np.asarray(r3[c]['xo']), rows=rows, wr=wr, br=br, w1L=w1L, w3L=w3L, w2L=w2L, **MC))
        r4 = _run(progs['moe'], ins)
        xl = np.concatenate([np.asarray(r4[c]['xo']).reshape(2304, 1024)[:2048] for c in CORES], 0)
        xc = np.asarray(r4[0]['xo']).reshape(2304, 1024)[2048:]
    return np.ascontiguousarray(xl, dtype=np.float32)[None]
```

```python
import numpy as np
import ml_dtypes
from contextlib import ExitStack
import concourse.bass as bass
import concourse.mybir as mybir
from concourse.bass_utils import run_bass_kernel_spmd

F32 = mybir.dt.float32
BF16 = mybir.dt.bfloat16
I32 = mybir.dt.int32
AF = mybir.ActivationFunctionType
ALU = mybir.AluOpType
AX = mybir.AxisListType
NPBF = ml_dtypes.bfloat16

COMPUTE = ("pe", "act", "dve", "pool")
QUEUES = ("sp", "act", "pool")


class Buf:
    __slots__ = ("name", "last_w", "readers")

    def __init__(self, name):
        self.name = name
        self.last_w = None
        self.readers = []


class Op:
    __slots__ = ("eng", "fn", "deps", "is_dma", "idx", "signal", "sigval", "dsem", "dval", "dprev")

    def __init__(self, eng, fn, is_dma):
        self.eng = eng
        self.fn = fn
        self.is_dma = is_dma
        self.deps = set()
        self.signal = False
        self.sigval = 0
        self.dsem = None
        self.dval = 0
        self.dprev = None


class Prog:
    def __init__(self, nc, n_dma_sems=6):
        self.nc = nc
        self.ops = []
        self.n_dma_sems = n_dma_sems
        self.es = ExitStack()
        self._nb = 0

    def buf(self, name=None):
        self._nb += 1
        return Buf(name or f"b{self._nb}")

    def bufs(self, n, name="b"):
        return [self.buf(f"{name}{i}") for i in range(n)]

    def sb(self, name, shape, dt):
        return self.es.enter_context(self.nc.sbuf_tensor(name, shape, dt))

    def ps(self, name, shape, dt):
        return self.es.enter_context(self.nc.psum_tensor(name, shape, dt))

    def din(self, name, shape, dt):
        return self.nc.dram_tensor(name, list(shape), dt, kind="ExternalInput").ap()

    def dout(self, name, shape, dt):
        return self.nc.dram_tensor(name, list(shape), dt, kind="ExternalOutput").ap()

    def op(self, eng, fn, reads=(), writes=(), dma=False):
        o = Op(eng, fn, dma)
        o.idx = len(self.ops)
        for b in reads:
            if b.last_w is not None:
                o.deps.add(b.last_w)
        for b in writes:
            if b.last_w is not None:
                o.deps.add(b.last_w)
            for r in b.readers:
                o.deps.add(r)
        for b in reads:
            b.readers.append(o.idx)
        for b in writes:
            b.last_w = o.idx
            b.readers = []
        o.deps.discard(o.idx)
        self.ops.append(o)
        return o

    def pe(self, fn, reads=(), writes=()):
        return self.op("pe", fn, reads, writes)

    def act(self, fn, reads=(), writes=()):
        return self.op("act", fn, reads, writes)

    def dve(self, fn, reads=(), writes=()):
        return self.op("dve", fn, reads, writes)

    def pool(self, fn, reads=(), writes=()):
        return self.op("pool", fn, reads, writes)

    def dma(self, q, fn, reads=(), writes=()):
        return self.op(q, fn, reads, writes, dma=True)

    def emit(self, final_wait_ops=None):
        nc = self.nc
        ops = self.ops
        for o in ops:
            if o.eng == "pe" and not o.is_dma:
                o.deps = {d for d in o.deps if not (ops[d].eng == "pe" and not ops[d].is_dma)}
        qcount = {q: 0 for q in QUEUES}
        qlast = {}
        for o in ops:
            if o.is_dma:
                slot = qcount[o.eng] % self.n_dma_sems
                qcount[o.eng] += 1
                key = (o.eng, slot)
                if key in qlast:
                    p = ops[qlast[key]]
                    o.dprev = p.idx
                    o.dval = p.dval + 16
                else:
                    o.dval = 16
                o.dsem = key
                qlast[key] = o.idx
        for o in ops:
            for d in o.deps:
                if not ops[d].is_dma:
                    ops[d].signal = True
        final = list(final_wait_ops or [])
        for d in final:
            if not d.is_dma:
                d.signal = True
        sigc = {e: 0 for e in COMPUTE}
        for o in ops:
            if not o.is_dma and o.signal:
                sigc[o.eng] += 1
                o.sigval = sigc[o.eng]
        sems = {}
        es = self.es
        for e in COMPUTE:
            sems[e] = es.enter_context(nc.semaphore("s_" + e))
        for q in QUEUES:
            for s in range(min(self.n_dma_sems, qcount[q])):
                sems[(q, s)] = es.enter_context(nc.semaphore(f"d_{q}{s}"))
        by_eng = {e: [] for e in ("pe", "act", "dve", "pool", "sp")}
        for o in ops:
            by_eng[o.eng].append(o)
        self.stats = {e: len(v) for e, v in by_eng.items()}

        def semkey_val(d):
            p = ops[d]
            if p.is_dma:
                return p.dsem, p.dval
            return p.eng, p.sigval

        def run_engine(ename, handle):
            waited = {}
            for o in by_eng[ename]:
                need = {}
                for d in o.deps:
                    k, v = semkey_val(d)
                    if need.get(k, 0) < v:
                        need[k] = v
                if o.is_dma and o.dprev is not None:
                    k, v = semkey_val(o.dprev)
                    if need.get(k, 0) < v:
                        need[k] = v
                for k, v in need.items():
                    if waited.get(k, 0) < v:
                        handle.wait_ge(sems[k], v)
                        waited[k] = v
                ins = o.fn(handle)
                if o.is_dma:
                    ins.then_inc(sems[o.dsem], 16)
                elif o.signal:
                    ins.then_inc(sems[o.eng], 1)
            if ename == "sp":
                for d in final:
                    k, v = semkey_val(d.idx)
                    if waited.get(k, 0) < v:
                        handle.wait_ge(sems[k], v)
                        waited[k] = v

        with nc.Block() as block:
            block.sync(lambda e: run_engine("sp", e))
            if by_eng["pe"]:
                block.tensor(lambda e: run_engine("pe", e))
            if by_eng["act"]:
                block.scalar(lambda e: run_engine("act", e))
            if by_eng["dve"]:
                block.vector(lambda e: run_engine("dve", e))
            if by_eng["pool"]:
                block.gpsimd(lambda e: run_engine("pool", e))
        self.es.close()


def make_ident(P, dt=BF16, name="ident"):
    idf = P.sb(name + "_f", [128, 128], F32)
    bf = P.buf()
    P.pool(lambda e: e.memset(idf[:], 0.0), writes=[bf])
    P.pool(lambda e: e.affine_select(out=idf[:], in_=idf[:], pattern=[[-1, 128]], compare_op=ALU.not_equal,
                                     fill=1.0, base=0, channel_multiplier=1), reads=[bf], writes=[bf])
    if dt == F32:
        return idf, bf
    idb = P.sb(name, [128, 128], dt)
    bb = P.buf()
    P.dve(lambda e: e.tensor_copy(out=idb[:], in_=idf[:]), reads=[bf], writes=[bb])
    return idb, bb

EPS = 1e-6


def build_e1():
    nc = bass.Bass("TRN2", target_bir_lowering=False)
    P = Prog(nc)
    xt = P.din("xt", [19, 128, 1024], F32)
    mcols = P.din("mcols", [128, 2, 2, 8], F32)
    win = P.din("win", [1024, 2048], F32)
    cwd = P.din("cw", [128, 3, 4], F32)
    cs128d = P.din("cs128", [128, 256], F32)
    c256d = P.din("c256", [128, 2, 256], F32)
    ns256d = P.din("ns256", [128, 2, 256], F32)
    flagsd = P.din("flags", [128, 2], F32)
    bout = P.dout("bout", [2048, 1024], BF16)
    ycT = P.dout("ycT", [512, 2304], BF16)
    fcT = P.dout("fcT", [512, 256], BF16)

    winb = P.sb("winb", [128, 8, 2048], BF16)
    mc = P.sb("mc", [128, 2, 2, 8], F32)
    cw = P.sb("cwt", [128, 3, 4], F32)
    cs128 = P.sb("cs128b", [128, 256], BF16)
    c256 = P.sb("c256b", [128, 2, 256], BF16)
    ns256 = P.sb("ns256b", [128, 2, 256], BF16)
    flags = P.sb("flagst", [128, 2], F32)
    XS = [P.sb(f"xs{i}", [128, 1024], F32) for i in range(2)]
    junk = P.sb("junk", [128, 1024], BF16)
    XN = [P.sb(f"xn{i}", [128, 1024], BF16) for i in range(2)]
    ss = P.sb("ss", [128, 19], F32)
    rstd = P.sb("rstd", [128, 19], F32)
    HT = [P.sb(f"hT{i}", [128, 8, 512], BF16) for i in range(2)]
    AT = [P.sb(f"aT{i}", [128, 4, 512], BF16) for i in range(2)]
    cgt = P.sb("cgt", [128, 4, 512], F32)
    uh = P.sb("uh", [128, 4, 2], F32)
    BgT = P.sb("BgT", [128, 4, 2304], BF16)
    uT = P.sb("uT", [128, 4, 2050], BF16)
    uTc = P.sb("uTc", [128, 4, 258], BF16)
    BT = [P.sb(f"bt{i}", [128, 1024], BF16) for i in range(2)]
    bctx = P.sb("bctx", [128, 2, 1024], BF16)
    fct = P.sb("fct", [128, 4, 256], BF16)
    yct = P.sb("yct", [128, 4, 2304], BF16)
    T1 = [P.sb(f"t1_{i}", [128, 1024], F32) for i in range(2)]
    T2 = [P.sb(f"t2_{i}", [128, 1024], F32) for i in range(2)]
    ident, b_ident = make_ident(P)
    epsc = P.sb("epsc", [128, 1], F32)
    b_eps = P.buf()
    P.pool(lambda e: e.memset(epsc[:], EPS), writes=[b_eps])

    PT = [P.ps(f"pT{i}", [128, 8, 128], BF16) for i in range(2)]
    PJ = [P.ps(f"pj{i}", [128, 512], F32) for i in range(3)]
    BS = P.ps("bs", [128, 4, 256], F32)
    PF = P.ps("pf", [128, 512], F32)

    b_win, b_mc, b_cw, b_cs, b_c256, b_ns256, b_flags = P.bufs(7, "c")
    b_xs = P.bufs(2, "xs"); b_junk = P.buf(); b_xn = P.bufs(2, "xn")
    b_ss = P.bufs(19, "ss"); b_rstd = P.bufs(19, "rs")
    b_ht = P.bufs(2, "ht"); b_at = P.bufs(2, "at"); b_cgt = P.bufs(4, "cgt"); b_uh = P.buf()
    b_bg = P.bufs(6, "bg"); b_u = P.bufs(7, "u"); b_bt = P.bufs(2, "bt"); b_bctx = P.bufs(2, "bctx")
    b_fct = P.bufs(4, "fct"); b_yct = P.bufs(8, "yct"); b_t1 = P.bufs(2, "t1"); b_t2 = P.bufs(2, "t2")
    b_pt = P.bufs(2, "pt"); b_pj = P.bufs(3, "pj"); b_bs = P.buf(); b_pf = P.buf()
    b_uz = P.buf()

    P.dma("pool", lambda e: e.dma_start(out=winb[:], in_=win.rearrange("(k p) n -> p k n", p=128)), writes=[b_win])
    P.dma("sp", lambda e: e.dma_start(out=mc[:], in_=mcols), writes=[b_mc])
    P.dma("sp", lambda e: e.dma_start(out=cw[:], in_=cwd), writes=[b_cw])
    P.dma("sp", lambda e: e.dma_start(out=flags[:], in_=flagsd), writes=[b_flags])
    P.dma("pool", lambda e: e.dma_start(out=cs128[:], in_=cs128d), writes=[b_cs])
    P.dma("pool", lambda e: e.dma_start(out=c256[:], in_=c256d), writes=[b_c256])
    P.dma("pool", lambda e: e.dma_start(out=ns256[:], in_=ns256d), writes=[b_ns256])
    P.pool(lambda e: e.memset(uTc[:], 0.0), writes=[b_uz])

    groups = [([18], 0, "halo")] + [([4 * g + i for i in range(4)], 0, "lat") for g in range(4)] + [([16, 17], 1, "ctx")]
    outs = []
    ti = 0
    pj_i = 0
    for gi, (tiles, s, kind) in enumerate(groups):
        ncols = 128 * len(tiles)
        hT = HT[gi % 2]; bht = b_ht[gi % 2]
        aT = AT[gi % 2]; bat = b_at[gi % 2]
        for tt, t in enumerate(tiles):
            xs = XS[ti % 2]; bxs = b_xs[ti % 2]
            xn = XN[ti % 2]; bxn = b_xn[ti % 2]
            pT = PT[ti % 2]; bpt = b_pt[ti % 2]
            P.dma("sp", lambda e, xs=xs, t=t: e.dma_start(out=xs[:], in_=xt[t]), writes=[bxs])
            P.act(lambda e, xs=xs, t=t: e.activation(out=junk[:], in_=xs[:], func=AF.Square, accum_out=ss[:, t:t + 1]),
                  reads=[bxs], writes=[b_junk, b_ss[t]])
            P.act(lambda e, t=t: e.activation(out=rstd[:, t:t + 1], in_=ss[:, t:t + 1], func=AF.Sqrt, scale=1.0 / 1024, bias=epsc[:, 0:1]),
                  reads=[b_ss[t], b_eps], writes=[b_rstd[t]])
            P.dve(lambda e, t=t: e.reciprocal(out=rstd[:, t:t + 1], in_=rstd[:, t:t + 1]),
                  reads=[b_rstd[t]], writes=[b_rstd[t]])
            P.dve(lambda e, xs=xs, xn=xn, t=t: e.tensor_scalar(out=xn[:], in0=xs[:], scalar1=rstd[:, t:t + 1], scalar2=None,
                                                               op0=ALU.mult), reads=[bxs, b_rstd[t]], writes=[bxn])
            for k in range(8):
                P.pe(lambda e, k=k, xn=xn, pT=pT: e.transpose(out=pT[:, k, :], in_=xn[:, k * 128:(k + 1) * 128], identity=ident[:]),
                     reads=[bxn, b_ident], writes=[bpt])
            for k in range(8):
                dst = hT[:, k, tt * 128:(tt + 1) * 128]
                if k % 2 == 0:
                    P.act(lambda e, dst=dst, pT=pT, k=k, s=s: e.activation(out=dst, in_=pT[:, k, :], func=AF.Identity,
                                                                          scale=mc[:, s, 1, k:k + 1], bias=mc[:, s, 0, k:k + 1]),
                          reads=[bpt, b_mc], writes=[bht])
                else:
                    P.dve(lambda e, dst=dst, pT=pT, k=k, s=s: e.tensor_scalar(out=dst, in0=pT[:, k, :], scalar1=mc[:, s, 1, k:k + 1],
                                                                             scalar2=mc[:, s, 0, k:k + 1], op0=ALU.mult, op1=ALU.add),
                          reads=[bpt, b_mc], writes=[bht])
            ti += 1
        if kind == "lat":
            tok0 = tiles[0] * 128
            bbg = b_bg[gi - 1]; bu = b_u[gi - 1]
        elif kind == "ctx":
            tok0 = 0
            bbg = b_bg[4]; bu = b_u[4]
        nlist = range(16) if kind != "halo" else range(8, 16)
        for n in nlist:
            pj = PJ[pj_i % 3]; bpj = b_pj[pj_i % 3]; pj_i += 1
            for k in range(8):
                P.pe(lambda e, pj=pj, k=k, n=n, hT=hT, ncols=ncols: e.matmul(pj[:, 0:ncols], lhsT=winb[:, k, n * 128:(n + 1) * 128],
                                                                             rhs=hT[:, k, 0:ncols], start=(k == 0), stop=(k == 7)),
                     reads=[bht, b_win], writes=[bpj])
            if n < 4:
                P.act(lambda e, pj=pj, n=n, aT=aT, ncols=ncols: e.activation(out=aT[:, n, 0:ncols], in_=pj[:, 0:ncols], func=AF.Copy),
                      reads=[bpj], writes=[bat])
            elif n < 8:
                j = n - 4
                if kind == "lat":
                    dst = BgT[:, j, tok0:tok0 + ncols]
                else:
                    dst = BgT[:, j, 2048:2304]
                P.dve(lambda e, pj=pj, dst=dst, ncols=ncols: e.tensor_copy(out=dst, in_=pj[:, 0:ncols]), reads=[bpj], writes=[bbg])
            elif n < 12:
                j = n - 8
                P.act(lambda e, pj=pj, j=j, ncols=ncols: e.activation(out=cgt[:, j, 0:ncols], in_=pj[:, 0:ncols], func=AF.Copy),
                      reads=[bpj], writes=[b_cgt[j]])
            else:
                j = n - 12
                if kind == "halo":
                    P.dve(lambda e, pj=pj, j=j: e.tensor_tensor(out=uh[:, j, :], in0=pj[:, 0:2], in1=cgt[:, j, 0:2], op=ALU.mult),
                          reads=[bpj, b_cgt[j]], writes=[b_uh])
                    P.dve(lambda e, j=j: e.tensor_scalar(out=uT[:, j, 0:1], in0=uh[:, j, 0:1], scalar1=flags[:, 0:1], scalar2=None,
                                                         op0=ALU.mult), reads=[b_uh, b_flags], writes=[b_u[5]])
                    P.dve(lambda e, j=j: e.tensor_scalar(out=uT[:, j, 2049:2050], in0=uh[:, j, 1:2], scalar1=flags[:, 1:2], scalar2=None,
                                                         op0=ALU.mult), reads=[b_uh, b_flags], writes=[b_u[6]])
                else:
                    if kind == "lat":
                        dst = uT[:, j, 1 + tok0:1 + tok0 + ncols]
                        wr = [bu]
                    else:
                        dst = uTc[:, j, 1:257]
                        wr = [bu]

                    rd = [bpj, b_cgt[j]] + ([b_uz] if kind == "ctx" else [])
                    P.dve(lambda e, pj=pj, j=j, dst=dst, ncols=ncols: e.tensor_tensor(out=dst, in0=pj[:, 0:ncols], in1=cgt[:, j, 0:ncols],
                                                                                      op=ALU.mult), reads=rd, writes=wr)
        if kind == "halo":
            continue
        for tt, t in enumerate(tiles):
            for g in range(4):
                P.pe(lambda e, g=g, aT=aT, tt=tt: e.matmul(BS[:, g, :], lhsT=aT[:, g, tt * 128:(tt + 1) * 128], rhs=cs128[:, :],
                                                           start=True, stop=True), reads=[bat, b_cs], writes=[b_bs])
            if kind == "lat":
                bt = BT[t % 2]; bbt = b_bt[t % 2]
                P.act(lambda e, bt=bt: e.activation(out=bt[:], in_=BS[:].rearrange("p g c -> p (g c)"), func=AF.Copy),
                      reads=[b_bs], writes=[bbt])
                outs.append(P.dma("sp", lambda e, bt=bt, t=t: e.dma_start(out=bout[t * 128:(t + 1) * 128, :], in_=bt[:]), reads=[bbt]))
            else:
                P.act(lambda e, tt=tt: e.activation(out=bctx[:, tt, :], in_=BS[:].rearrange("p g c -> p (g c)"), func=AF.Copy),
                      reads=[b_bs], writes=[b_bctx[tt]])
        if kind == "ctx":
            for g in range(4):
                for tt in range(2):
                    P.pe(lambda e, g=g, tt=tt: e.matmul(PF[:, 0:256], lhsT=bctx[:, tt, g * 256:g * 256 + 128], rhs=c256[:, tt, :],
                                                        start=(tt == 0), stop=False), reads=[b_bctx[tt], b_c256], writes=[b_pf])
                    P.pe(lambda e, g=g, tt=tt: e.matmul(PF[:, 0:256], lhsT=bctx[:, tt, g * 256 + 128:g * 256 + 256], rhs=ns256[:, tt, :],
                                                        start=False, stop=(tt == 1)), reads=[b_bctx[tt], b_ns256], writes=[b_pf])
                P.act(lambda e, g=g: e.activation(out=fct[:, g, :], in_=PF[:, 0:256], func=AF.Copy), reads=[b_pf], writes=[b_fct[g]])
                outs.append(P.dma("sp", lambda e, g=g: e.dma_start(out=fcT[g * 128:(g + 1) * 128, :], in_=fct[:, g, :]), reads=[b_fct[g]]))

    allu = b_u[0:4] + [b_u[5], b_u[6]]
    allbg = b_bg[0:4]
    segs = [("lat", 0, 1024), ("lat", 1024, 1024), ("ctx", 0, 256)]
    ci = 0
    for j in range(4):
        for (kind, c0, w) in segs:
            en = "dve"
            t1 = T1[ci % 2]; t2 = T2[ci % 2]; bt1 = b_t1[ci % 2]; bt2 = b_t2[ci % 2]
            ci += 1
            if kind == "lat":
                src = uT; off = c0; ub = allu; bgs = BgT[:, j, c0:c0 + w]; bgb = allbg; ydst = yct[:, j, c0:c0 + w]
                byc = b_yct[j * 2 + (c0 // 1024)]
            else:
                src = uTc; off = 0; ub = [b_u[4], b_uz]; bgs = BgT[:, j, 2048:2304]; bgb = [b_bg[4]]; ydst = yct[:, j, 2048:2304]
                byc = b_yct[j * 2]
                byc = P.buf()
            P.op(en, lambda e, t1=t1, src=src, off=off, w=w, j=j: e.tensor_scalar(out=t1[:, 0:w], in0=src[:, j, off + 1:off + 1 + w],
                                                                                  scalar1=cw[:, 1, j:j + 1], scalar2=None, op0=ALU.mult),
                 reads=ub + [b_cw], writes=[bt1])
            P.op(en, lambda e, t1=t1, t2=t2, src=src, off=off, w=w, j=j: e.scalar_tensor_tensor(
                out=t2[:, 0:w], in0=src[:, j, off:off + w], scalar=cw[:, 0, j:j + 1], in1=t1[:, 0:w], op0=ALU.mult, op1=ALU.add),
                 reads=ub + [b_cw, bt1], writes=[bt2])
            P.op(en, lambda e, t1=t1, t2=t2, src=src, off=off, w=w, j=j: e.scalar_tensor_tensor(
                out=t1[:, 0:w], in0=src[:, j, off + 2:off + 2 + w], scalar=cw[:, 2, j:j + 1], in1=t2[:, 0:w], op0=ALU.mult, op1=ALU.add),
                 reads=ub + [b_cw, bt2], writes=[bt1])
            P.op(en, lambda e, t1=t1, w=w, bgs=bgs, ydst=ydst: e.tensor_tensor(out=ydst, in0=t1[:, 0:w], in1=bgs, op=ALU.mult),
                 reads=[bt1] + bgb, writes=[byc])
            if kind == "lat":
                outs.append(P.dma("sp", lambda e, j=j, c0=c0, w=w: e.dma_start(out=ycT[j * 128:(j + 1) * 128, c0:c0 + w],
                                                                            in_=yct[:, j, c0:c0 + w]), reads=[byc]))
            else:
                outs.append(P.dma("sp", lambda e, j=j: e.dma_start(out=ycT[j * 128:(j + 1) * 128, 2048:2304], in_=yct[:, j, 2048:2304]),
                                  reads=[byc]))
    P.emit(final_wait_ops=outs)
    return nc


def dft_consts():
    n = np.arange(128)
    ang = 2 * np.pi * np.outer(n, n) / 128
    C = np.cos(ang); S = np.sin(ang)
    cs128 = np.concatenate([C, S], 1) / np.sqrt(128)
    t = np.arange(256)
    a256 = 2 * np.pi * np.outer(t, t) / 256
    c256 = (np.cos(a256) / 16).reshape(2, 128, 256).transpose(1, 0, 2)
    ns256 = (-np.sin(a256) / 16).reshape(2, 128, 256).transpose(1, 0, 2)
    return dict(cs128=cs128.astype(np.float32), c256=np.ascontiguousarray(c256).astype(np.float32),
                ns256=np.ascontiguousarray(ns256).astype(np.float32), C=C, S=S)


def build_e2():
    nc = bass.Bass("TRN2", target_bir_lowering=False)
    P = Prog(nc)
    zin = P.din("zin", [128, 2, 64, 128], BF16)
    m1d = P.din("m1", [128, 256], F32)
    m2d = P.din("m2", [128, 256], F32)
    twd = P.din("tw", [128, 2, 128], F32)
    c2d = P.din("c2", [128, 2, 128], F32)
    FT = P.dout("FT", [64, 128, 128], BF16)

    z = P.sb("z", [128, 2, 64, 128], BF16)
    m1 = P.sb("m1b", [128, 256], BF16)
    m2 = P.sb("m2b", [128, 256], BF16)
    tw = P.sb("twt", [128, 2, 128], F32)
    c2 = P.sb("c2b", [128, 2, 128], BF16)
    YS = [P.sb(f"ys{i}", [128, 4, 256], F32) for i in range(2)]
    TA = [P.sb(f"ta{i}", [128, 4, 128], F32) for i in range(2)]
    TB = [P.sb(f"tb{i}", [128, 4, 128], F32) for i in range(2)]
    TC = [P.sb(f"tc{i}", [128, 4, 128], F32) for i in range(2)]
    TD = [P.sb(f"td{i}", [128, 4, 128], F32) for i in range(2)]
    ZC = [P.sb(f"zc{i}", [128, 4, 128], BF16) for i in range(2)]
    ZS = [P.sb(f"zs{i}", [128, 4, 128], BF16) for i in range(2)]
    GS = [P.sb(f"gs{i}", [128, 4, 128], BF16) for i in range(2)]
    PY = [P.ps(f"py{i}", [128, 4, 256], F32) for i in range(2)]
    PG = [P.ps(f"pg{i}", [128, 4, 128], F32) for i in range(2)]

    b_z, b_m1, b_m2, b_tw, b_c2 = P.bufs(5, "c")
    b_ys = P.bufs(2); b_ta = P.bufs(2); b_tb = P.bufs(2); b_tc = P.bufs(2); b_td = P.bufs(2)
    b_zc = P.bufs(2); b_zs = P.bufs(2); b_gs = P.bufs(2); b_py = P.bufs(2); b_pg = P.bufs(2)

    P.dma("sp", lambda e: e.dma_start(out=z[:], in_=zin), writes=[b_z])
    P.dma("pool", lambda e: e.dma_start(out=m1[:], in_=m1d), writes=[b_m1])
    P.dma("pool", lambda e: e.dma_start(out=m2[:], in_=m2d), writes=[b_m2])
    P.dma("pool", lambda e: e.dma_start(out=c2[:], in_=c2d), writes=[b_c2])
    P.dma("sp", lambda e: e.dma_start(out=tw[:], in_=twd), writes=[b_tw])
    outs = []
    for st in range(16):
        i = st % 2
        py = PY[i]; ys = YS[i]; ta = TA[i]; tb = TB[i]; tc_ = TC[i]; td = TD[i]; zc = ZC[i]; zs = ZS[i]; gs = GS[i]; pg = PG[i]
        for q in range(4):
            n = st * 4 + q
            P.pe(lambda e, py=py, q=q, n=n: e.matmul(py[:, q, :], lhsT=z[:, 0, n, :], rhs=m1[:, :], start=True, stop=False),
                 reads=[b_z, b_m1], writes=[b_py[i]])
            P.pe(lambda e, py=py, q=q, n=n: e.matmul(py[:, q, :], lhsT=z[:, 1, n, :], rhs=m2[:, :], start=False, stop=True),
                 reads=[b_z, b_m2], writes=[b_py[i]])
        P.act(lambda e, py=py, ys=ys: e.activation(out=ys[:], in_=py[:], func=AF.Copy), reads=[b_py[i]], writes=[b_ys[i]])
        tcos = tw[:, 0:1, :].to_broadcast([128, 4, 128])
        tsin = tw[:, 1:2, :].to_broadcast([128, 4, 128])
        P.dve(lambda e, ys=ys, ta=ta, tcos=tcos: e.tensor_tensor(out=ta[:], in0=ys[:, :, 0:128], in1=tcos, op=ALU.mult),
              reads=[b_ys[i], b_tw], writes=[b_ta[i]])
        P.dve(lambda e, ys=ys, tb=tb, tsin=tsin: e.tensor_tensor(out=tb[:], in0=ys[:, :, 128:256], in1=tsin, op=ALU.mult),
              reads=[b_ys[i], b_tw], writes=[b_tb[i]])
        P.dve(lambda e, ta=ta, tb=tb, zc=zc: e.tensor_tensor(out=zc[:], in0=ta[:], in1=tb[:], op=ALU.subtract),
              reads=[b_ta[i], b_tb[i]], writes=[b_zc[i]])
        P.pool(lambda e, ys=ys, tc_=tc_, tsin=tsin: e.tensor_tensor(out=tc_[:], in0=ys[:, :, 0:128], in1=tsin, op=ALU.mult),
               reads=[b_ys[i], b_tw], writes=[b_tc[i]])
        P.pool(lambda e, ys=ys, td=td, tcos=tcos: e.tensor_tensor(out=td[:], in0=ys[:, :, 128:256], in1=tcos, op=ALU.mult),
               reads=[b_ys[i], b_tw], writes=[b_td[i]])
        P.pool(lambda e, tc_=tc_, td=td, zs=zs: e.tensor_tensor(out=zs[:], in0=tc_[:], in1=td[:], op=ALU.add),
               reads=[b_tc[i], b_td[i]], writes=[b_zs[i]])
        P.pe(lambda e, pg=pg, zc=zc: e.matmul(pg[:].rearrange("p a b -> p (a b)"), lhsT=c2[:, 0, :], rhs=zc[:].rearrange("p a b -> p (a b)"),
                                              start=True, stop=False), reads=[b_zc[i], b_c2], writes=[b_pg[i]])
        P.pe(lambda e, pg=pg, zs=zs: e.matmul(pg[:].rearrange("p a b -> p (a b)"), lhsT=c2[:, 1, :], rhs=zs[:].rearrange("p a b -> p (a b)"),
                                              start=False, stop=True), reads=[b_zs[i], b_c2], writes=[b_pg[i]])
        P.act(lambda e, pg=pg, gs=gs: e.activation(out=gs[:], in_=pg[:], func=AF.Copy), reads=[b_pg[i]], writes=[b_gs[i]])
        outs.append(P.dma("sp", lambda e, gs=gs, st=st: e.dma_start(out=FT[st * 4:(st + 1) * 4].rearrange("n a b -> a n b"), in_=gs[:]),
                          reads=[b_gs[i]]))
    P.emit(final_wait_ops=outs)
    return nc


def fft_consts():
    n = np.arange(128)
    ang = 2 * np.pi * np.outer(n, n) / 128
    C = np.cos(ang) / np.sqrt(128); S = np.sin(ang) / np.sqrt(128)
    m1 = np.concatenate([C, S], 1); m2 = np.concatenate([-S, C], 1)
    ta = 2 * np.pi * np.outer(n, n) / 16384
    tw = np.stack([np.cos(ta), np.sin(ta)], 1)
    c2 = np.stack([C, -S], 1)
    f = lambda a: np.ascontiguousarray(a).astype(np.float32)
    return dict(m1=f(m1), m2=f(m2), tw=f(tw), c2=f(c2))


def build_e3():
    nc = bass.Bass("TRN2", target_bir_lowering=False)
    P = Prog(nc)
    yT = P.din("yT", [1024, 2304], BF16)
    xt = P.din("xt", [18, 128, 1024], F32)
    g1d = P.din("g1", [128, 2, 1024], F32)
    wd = P.din("w", [1024, 1024], F32)
    xo = P.dout("xo", [18, 128, 1024], F32)
    ysb = P.sb("ysb", [128, 8, 2304], BF16)
    wb = P.sb("wb", [128, 8, 1024], BF16)
    g1 = P.sb("g1t", [128, 2, 1024], F32)
    XS = [P.sb(f"xs{i}", [128, 1024], F32) for i in range(2)]
    TM = [P.sb(f"tm{i}", [128, 1024], F32) for i in range(2)]
    PY = [P.ps(f"py{i}", [128, 1024], F32) for i in range(2)]
    b_y, b_w, b_g = P.bufs(3); b_xs = P.bufs(2); b_tm = P.bufs(2); b_py = P.bufs(2)
    P.dma("sp", lambda e: e.dma_start(out=ysb[:], in_=yT.rearrange("(k p) t -> p k t", p=128)), writes=[b_y])
    P.dma("pool", lambda e: e.dma_start(out=wb[:], in_=wd.rearrange("(k p) n -> p k n", p=128)), writes=[b_w])
    P.dma("sp", lambda e: e.dma_start(out=g1[:], in_=g1d), writes=[b_g])
    outs = []
    for t in range(18):
        i = t % 2
        s = 0 if t < 16 else 1
        xs = XS[i]; tm = TM[i]; py = PY[i]
        P.dma("sp", lambda e, xs=xs, t=t: e.dma_start(out=xs[:], in_=xt[t]), writes=[b_xs[i]])
        for h in range(2):
            for k in range(8):
                P.pe(lambda e, py=py, h=h, k=k, t=t: e.matmul(py[:, h * 512:(h + 1) * 512], lhsT=ysb[:, k, t * 128:(t + 1) * 128],
                                                             rhs=wb[:, k, h * 512:(h + 1) * 512], start=(k == 0), stop=(k == 7)),
                     reads=[b_y, b_w], writes=[b_py[i]])
        P.dve(lambda e, py=py, tm=tm, s=s: e.tensor_tensor(out=tm[:], in0=py[:], in1=g1[:, s, :], op=ALU.mult),
              reads=[b_py[i], b_g], writes=[b_tm[i]])
        P.pool(lambda e, tm=tm, xs=xs: e.tensor_tensor(out=tm[:], in0=tm[:], in1=xs[:], op=ALU.add),
               reads=[b_tm[i], b_xs[i]], writes=[b_tm[i]])
        outs.append(P.dma("sp", lambda e, tm=tm, t=t: e.dma_start(out=xo[t], in_=tm[:]), reads=[b_tm[i]]))
    P.emit(final_wait_ops=outs)
    return nc


NB = 68


def build_moe2():
    nc = bass.Bass("TRN2", target_bir_lowering=False)
    P = Prog(nc)
    xt = P.din("xt", [18, 128, 1024], F32)
    rowsd = P.din("rows", [2, 3, 128, 1024], F32)
    wrd = P.din("wr", [1024, 36], F32)
    brd = P.din("br", [128, 36], F32)
    w1L = P.din("w1L", [4096, 4096], BF16)
    w3L = P.din("w3L", [4096, 4096], BF16)
    w2L = P.din("w2L", [4096, 4096], BF16)
    bvd = P.din("bvals", [128, NB * 32], F32)
    thrd = P.din("thr", [128, 32 * 18], F32)
    pcd = P.din("pcol", [128, 1], F32)
    tokd = P.din("tokid", [128, 18, 16], I32)
    xo = P.dout("xo", [18, 128, 1024], F32)
    Hd = nc.dram_tensor("Hd", [2305, 1024], BF16).ap()
    table = nc.dram_tensor("table", [(NB + 1) * 128, 16], I32).ap()
    Yd = nc.dram_tensor("Yd", [(NB + 1) * 128, 1024], BF16).ap()

    WG1 = [P.sb("wg1", [128, 4096], BF16)] * 2
    WG3 = [P.sb("wg3", [128, 4096], BF16)] * 2
    WG2 = [P.sb("wg2", [128, 4096], BF16)] * 2
    W1 = [P.sb(f"w1b{i}", [128, 8, 512], BF16) for i in range(2)]
    W3 = [P.sb(f"w3b{i}", [128, 8, 512], BF16) for i in range(2)]
    W2 = [P.sb(f"w2b{i}", [128, 4, 1024], BF16) for i in range(2)]
    rows = P.sb("rowst", [128, 2, 1024], F32)
    XS = [P.sb(f"xs{i}", [128, 1024], F32) for i in range(2)]
    H32 = [P.sb(f"h32_{i}", [128, 1024], F32) for i in range(2)]
    wrt = P.sb("wrt", [128, 8, 36], F32)
    brt = P.sb("brt", [128, 36], F32)
    wrh = P.sb("wrh", [128, 8, 36], BF16)
    wrl = P.sb("wrl", [128, 8, 36], BF16)
    HHI = [P.sb(f"hhi{i}", [128, 1024], BF16) for i in range(2)]
    HLO = [P.sb(f"hlo{i}", [128, 1024], BF16) for i in range(2)]
    HTHI = [P.sb(f"hThi{i}", [128, 8, 128], BF16) for i in range(2)]
    HTLO = [P.sb(f"hTlo{i}", [128, 8, 128], BF16) for i in range(2)]
    identb = P.sb("identb", [128, 128], BF16)
    onesb = P.sb("onesb", [128, 128], BF16)
    onesf = P.sb("onesf", [128, 128], F32)
    Ub = P.sb("Ub", [128, 128], BF16)
    SM = [P.sb(f"sm{i}", [128, 64], F32) for i in range(2)]
    LG = [P.sb(f"lg{i}", [128, 36], F32) for i in range(2)]
    ss = P.sb("ss", [128, 18], F32)
    rstd = P.sb("rstd", [128, 18], F32)
    epsc = P.sb("epsc", [128, 1], F32)
    M12 = P.sb("M12", [128, 18, 2, 32], BF16)
    Msum = P.sb("Msum", [128, 18, 32], BF16)
    g12 = P.sb("g12", [128, 18, 2], F32)
    LGA = P.sb("LGA", [128, 18, 36], F32)
    bvals = P.sb("bvalst", [128, NB, 32], F32)
    thr = P.sb("thrt", [128, 32, 18], F32)
    pcol = P.sb("pcolt", [128, 1], F32)
    tokid = P.sb("tokidt", [128, 18, 16], I32)
    tinit = P.sb("tinit", [128, NB + 1, 16], I32)
    zrow = P.sb("zrow", [1, 1024], BF16)
    cnt = P.sb("cnt", [128, 32], F32)
    big = P.sb("big", [128, NB * 32], F32)
    nblk = P.sb("nblk", [128, 32], F32)
    padded = P.sb("padded", [128, 32], F32)
    ca = P.sb("ca", [128, 32], F32)
    cb = P.sb("cb", [128, 32], F32)
    pstart = P.sb("pstart", [128, 32], F32)
    bexp = P.sb("bexp", [128, NB], F32)
    same = P.sb("same", [128, NB], F32)
    widx = P.sb("widx", [128, NB], I32)
    carry = P.sb("carry", [128, 32], F32)
    tA = P.sb("tA", [128, 32], F32)
    tB = P.sb("tB", [128, 32], F32)
    destf = P.sb("destf", [128, 18, 2], F32)
    desti = P.sb("desti", [128, 18, 2], I32)
    IDX = [P.sb(f"idx{i}", [128, 16], I32) for i in range(2)]
    XG = [P.sb(f"xg{i}", [128, 1024], BF16) for i in range(2)]
    XGT = [P.sb(f"XgT{i}", [128, 8, 128], BF16) for i in range(2)]
    SSB = [P.sb(f"ssb{i}", [128, 512], F32) for i in range(2)]
    HB = [P.sb(f"hb{i}", [128, 512], BF16) for i in range(2)]
    HTT = [P.sb(f"hT{i}", [128, 4, 128], BF16) for i in range(2)]
    YS = [P.sb(f"ys{i}", [128, 1024], BF16) for i in range(2)]
    YY1 = [P.sb(f"Y1_{i}", [128, 1024], BF16) for i in range(2)]
    YY2 = [P.sb(f"Y2_{i}", [128, 1024], BF16) for i in range(2)]
    identf, b_identf = make_ident(P, F32, "identf")

    PTB = P.ps("ptb", [128, 8, 128], BF16)
    PRG = P.ps("prg", [128, 512], F32)
    PRK = P.ps("prk", [128, 512], F32)
    PH1 = [P.ps("ph1", [128, 512], F32), PRG]
    PH3 = [P.ps("ph3", [128, 512], F32), PRK]
    PTH2 = P.ps("pth", [128, 2, 4, 128], BF16)
    PY = P.ps("py", [128, 1024], F32)

    b_wg1 = [P.buf()] * 2; b_wg3 = [P.buf()] * 2; b_wg2 = [P.buf()] * 2; b_w1 = P.bufs(2); b_w3 = P.bufs(2); b_w2 = P.bufs(2)
    b_rows = P.buf(); B_xs = P.bufs(2); B_h32 = P.bufs(2)
    b_wr = P.buf(); b_br = P.buf(); B_sm = P.bufs(2); B_lg = P.bufs(2); b_lga = P.bufs(18)
    b_ss = P.bufs(18); b_rstd = P.bufs(18); b_eps = P.buf()
    b_ptb = P.buf(); B_hhi = P.bufs(2); B_hlo = P.bufs(2); B_hthi = P.bufs(2); B_htlo = P.bufs(2); b_wrh = P.buf(); b_wrl = P.buf(); b_idb = P.buf()
    b_prg = P.buf(); b_prk = P.buf(); B_ph1 = [P.buf(), b_prg]; B_ph3 = [P.buf(), b_prk]; B_pth = P.bufs(2); b_py = P.buf()
    b_ones = P.buf(); b_U = P.buf(); b_M = P.bufs(18); b_ms = P.bufs(18); b_g12 = P.bufs(18)
    b_bv, b_thr, b_pc, b_tok, b_tinit, b_zrow, b_tab0, b_hdz = P.bufs(8)
    b_hd = P.bufs(18); b_cnt = P.buf(); b_big = P.buf(); b_nblk = P.buf(); b_pad = P.buf(); b_ca = P.buf(); b_cb = P.buf()
    b_ps = P.buf(); b_same = P.buf(); b_bexp = P.buf(); b_widx = P.buf(); b_carry = P.buf(); b_tA = P.buf(); b_tB = P.buf()
    b_destf = P.bufs(18); b_desti = P.buf(); b_sc = P.bufs(36)
    b_idx = P.bufs(2); b_xg = P.bufs(2); B_xgt = P.bufs(2); B_ssb = P.bufs(2); B_hb = P.bufs(2); B_hT = P.bufs(2); b_ys = P.bufs(2)
    b_yd = P.bufs(NB); B_y1 = P.bufs(2); B_y2 = P.bufs(2)

    regs = {}

    def _mkreg(e):
        regs['b'] = e.alloc_register("bnd")
        return e.reg_mov(regs['b'], 4095)
    P.pool(_mkreg)
    P.pool(lambda e: e.memset(epsc[:], EPS), writes=[b_eps])
    P.pool(lambda e: e.memset(onesf[:], 1.0), writes=[b_ones])
    P.pool(lambda e: e.affine_select(out=onesf[:], in_=onesf[:], pattern=[[1, 128]], compare_op=ALU.is_gt, fill=0.0, base=0,
                                     channel_multiplier=-1), reads=[b_ones], writes=[b_ones])
    P.dve(lambda e: e.tensor_copy(out=Ub[:], in_=onesf[:]), reads=[b_ones], writes=[b_U])
    P.pool(lambda e: e.memset(onesb[:], 1.0), writes=[b_U])
    P.pool(lambda e: e.memset(tinit[:], 2304), writes=[b_tinit])
    P.pool(lambda e: e.memset(zrow[:], 0.0), writes=[b_zrow])
    P.pool(lambda e: e.memset(carry[:], 0.0), writes=[b_carry])
    P.dma("sp", lambda e: e.dma_start(out=wrt[:], in_=wrd.rearrange("(k p) n -> p k n", p=128)), writes=[b_wr])
    P.dma("sp", lambda e: e.dma_start(out=brt[:], in_=brd), writes=[b_br])
    P.dma("sp", lambda e: e.dma_start(out=bvals[:].rearrange("p b e -> p (b e)"), in_=bvd), writes=[b_bv])
    P.dma("sp", lambda e: e.dma_start(out=thr[:].rearrange("p e j -> p (e j)"), in_=thrd), writes=[b_thr])
    P.dma("sp", lambda e: e.dma_start(out=pcol[:], in_=pcd), writes=[b_pc])
    P.dma("sp", lambda e: e.dma_start(out=tokid[:], in_=tokd), writes=[b_tok])
    P.dma("sp", lambda e: e.dma_start(out=table.rearrange("(b p) c -> p b c", p=128), in_=tinit[:]), reads=[b_tinit], writes=[b_tab0])
    P.dma("sp", lambda e: e.dma_start(out=Hd[2304:2305, :], in_=zrow[:]), reads=[b_zrow], writes=[b_hdz])
    P.dve(lambda e: e.tensor_copy(out=identb[:], in_=identf[:]), reads=[b_identf], writes=[b_idb])
    P.dve(lambda e: e.tensor_copy(out=wrh[:], in_=wrt[:]), reads=[b_wr], writes=[b_wrh])
    P.dve(lambda e: e.tensor_tensor(out=wrl[:], in0=wrt[:], in1=wrh[:], op=ALU.subtract), reads=[b_wr, b_wrh], writes=[b_wrl])

    def phase_a(t, xs, h32, hhi, hlo, hThi, hTlo, sm, lg, b_xs, b_h32, b_hhi, b_hlo, b_hthi, b_htlo, b_sm, b_lg):
        junk = hlo; b_junk = b_hlo

        def sc(i):
            return sm[:, i:i + 1]
        s = 0 if t < 16 else 1
        if t == 0 or t == 16:
            P.dma("sp", lambda e, s=s: e.dma_start(out=rows[:, 0, :], in_=rowsd[s, 0]), writes=[b_rows])
            P.dma("sp", lambda e, s=s: e.dma_start(out=rows[:, 1, :], in_=rowsd[s, 1]), writes=[b_rows])
        P.dma("sp", lambda e, t=t: e.dma_start(out=xs[:], in_=xt[t]), writes=[b_xs])
        P.act(lambda e, t=t: e.activation(out=junk[:], in_=xs[:], func=AF.Square, accum_out=ss[:, t:t + 1]),
              reads=[b_xs], writes=[b_junk, b_ss[t]])
        P.act(lambda e, t=t: e.activation(out=rstd[:, t:t + 1], in_=ss[:, t:t + 1], func=AF.Sqrt, scale=1.0 / 1024, bias=epsc[:, 0:1]),
              reads=[b_ss[t], b_eps], writes=[b_rstd[t]])
        P.dve(lambda e, t=t: e.reciprocal(out=rstd[:, t:t + 1], in_=rstd[:, t:t + 1]), reads=[b_rstd[t]], writes=[b_rstd[t]])
        P.dve(lambda e, t=t: e.scalar_tensor_tensor(out=h32[:], in0=xs[:], scalar=rstd[:, t:t + 1], in1=rows[:, 0, :],
                                                    op0=ALU.mult, op1=ALU.mult), reads=[b_xs, b_rstd[t], b_rows], writes=[b_h32])
        P.pool(lambda e: e.tensor_tensor(out=h32[:], in0=h32[:], in1=rows[:, 1, :], op=ALU.add), reads=[b_h32, b_rows], writes=[b_h32])
        P.dve(lambda e: e.tensor_copy(out=hhi[:], in_=h32[:]), reads=[b_h32], writes=[b_hhi])
        P.dve(lambda e: e.tensor_tensor(out=hlo[:], in0=h32[:], in1=hhi[:], op=ALU.subtract), reads=[b_h32, b_hhi], writes=[b_hlo])
        for k in range(8):
            P.pe(lambda e, k=k: e.transpose(out=PTB[:, k, :], in_=hhi[:, k * 128:(k + 1) * 128], identity=identb[:]),
                 reads=[b_hhi, b_idb], writes=[b_ptb])
        P.dma("sp", lambda e, t=t: e.dma_start(out=Hd[t * 128:(t + 1) * 128, :], in_=hhi[:]), reads=[b_hhi], writes=[b_hd[t]])
        P.dve(lambda e: e.tensor_copy(out=hThi[:], in_=PTB[:]), reads=[b_ptb], writes=[b_hthi])
        for k in range(8):
            P.pe(lambda e, k=k: e.transpose(out=PTB[:, k, :], in_=hlo[:, k * 128:(k + 1) * 128], identity=identb[:]),
                 reads=[b_hlo, b_idb], writes=[b_ptb])
        P.act(lambda e: e.activation(out=hTlo[:], in_=PTB[:], func=AF.Copy), reads=[b_ptb], writes=[b_htlo])
        first = True
        for k in range(8):
            for (lh, bl, rw, bw) in ((0, b_hthi, wrh, b_wrh), (0, b_hthi, wrl, b_wrl), (1, b_htlo, wrh, b_wrh)):
                lhs = hThi[:, k, :] if lh == 0 else hTlo[:, k, :]
                last = (k == 7 and lh == 1)
                P.pe(lambda e, lhs=lhs, rw=rw, k=k, first=first, last=last: e.matmul(PRG[:, 0:36], lhsT=lhs, rhs=rw[:, k, :], start=first, stop=last),
                     reads=[bl, bw], writes=[b_prg])
                first = False
        P.dve(lambda e: e.tensor_tensor(out=LGA[:, t, :], in0=PRG[:, 0:36], in1=brt[:], op=ALU.add), reads=[b_prg, b_br], writes=[b_lga[t]])

    for t in range(18):
        i = t % 2
        phase_a(t, XS[i], H32[i], HHI[i], HLO[i], HTHI[i], HTLO[i], SM[i], LG[i], B_xs[i], B_h32[i], B_hhi[i], B_hlo[i], B_hthi[i], B_htlo[i], B_sm[i], B_lg[i])

    T = 18
    rp = big[:, 0:T * 32].rearrange("p (t g j) -> p t g j", g=4, j=8)
    rt = big[:, 576:576 + T * 64].rearrange("p (t c) -> p t c", c=64)
    RB = [b_big]

    def rc(c0, c1=None):
        return rt[:, :, c0] if c1 is None else rt[:, :, c0:c1]

    def bc(ap2, n):
        return ap2.unsqueeze(2).to_broadcast([128, T, n])
    Lg = LGA[:, :, 0:4]
    Le = LGA[:, :, 4:36].rearrange("p t (g j) -> p t g j", j=8)
    P.dve(lambda e: e.tensor_reduce(out=rc(0), in_=Lg, axis=AX.X, op=ALU.max), reads=b_lga, writes=RB)
    P.dve(lambda e: e.tensor_tensor(out=rc(8, 12), in0=Lg, in1=bc(rc(0), 4), op=ALU.subtract), reads=b_lga + RB, writes=RB)
    P.act(lambda e: e.activation(out=rc(8, 12), in_=rc(8, 12), func=AF.Exp), reads=RB, writes=RB)
    P.dve(lambda e: e.tensor_reduce(out=rc(2), in_=rc(8, 12), axis=AX.X, op=ALU.add), reads=RB, writes=RB)
    P.dve(lambda e: e.reciprocal(out=rc(3), in_=rc(2)), reads=RB, writes=RB)
    P.dve(lambda e: e.tensor_tensor(out=rc(12, 16), in0=Lg, in1=bc(rc(0), 4), op=ALU.is_equal), reads=b_lga + RB, writes=RB)
    P.dve(lambda e: e.tensor_tensor(out=rp, in0=Le, in1=rc(12, 16).unsqueeze(3).to_broadcast([128, T, 4, 8]), op=ALU.mult),
          reads=b_lga + RB, writes=RB)
    P.dve(lambda e: e.tensor_reduce(out=rc(16, 24), in_=rp.rearrange("p t g j -> p t j g"), axis=AX.X, op=ALU.add), reads=RB, writes=RB)
    P.dve(lambda e: e.tensor_reduce(out=rc(4), in_=rc(16, 24), axis=AX.X, op=ALU.max), reads=RB, writes=RB)
    P.dve(lambda e: e.tensor_tensor(out=rc(24, 32), in0=rc(16, 24), in1=bc(rc(4), 8), op=ALU.is_equal), reads=RB, writes=RB)
    P.dve(lambda e: e.scalar_tensor_tensor(out=rc(32, 40), in0=rc(24, 32), scalar=-1e30, in1=rc(16, 24), op0=ALU.mult, op1=ALU.add),
          reads=RB, writes=RB)
    P.dve(lambda e: e.tensor_reduce(out=rc(5), in_=rc(32, 40), axis=AX.X, op=ALU.max), reads=RB, writes=RB)
    P.dve(lambda e: e.tensor_tensor(out=rc(40, 48), in0=rc(32, 40), in1=bc(rc(5), 8), op=ALU.is_equal), reads=RB, writes=RB)
    P.dve(lambda e: e.tensor_tensor(out=rc(6), in0=rc(5), in1=rc(4), op=ALU.subtract), reads=RB, writes=RB)
    P.act(lambda e: e.activation(out=rc(7), in_=rc(6), func=AF.Exp), reads=RB, writes=RB)
    P.dve(lambda e: e.tensor_scalar(out=rc(7), in0=rc(7), scalar1=1.0, scalar2=None, op0=ALU.add), reads=RB, writes=RB)
    P.dve(lambda e: e.reciprocal(out=rc(7), in_=rc(7)), reads=RB, writes=RB)
    P.dve(lambda e: e.tensor_tensor(out=g12[:, :, 0], in0=rc(7), in1=rc(3), op=ALU.mult), reads=RB, writes=b_g12)
    P.dve(lambda e: e.tensor_tensor(out=g12[:, :, 1], in0=rc(3), in1=g12[:, :, 0], op=ALU.subtract), reads=RB + b_g12, writes=b_g12)
    for k, c0 in ((0, 24), (1, 40)):
        P.dve(lambda e, k=k, c0=c0: e.tensor_tensor(out=M12[:, :, k, :].rearrange("p t (g j) -> p t g j", j=8),
                                                    in0=rc(12, 16).unsqueeze(3).to_broadcast([128, T, 4, 8]),
                                                    in1=rc(c0, c0 + 8).unsqueeze(2).to_broadcast([128, T, 4, 8]), op=ALU.mult),
              reads=RB, writes=b_M)
    P.dve(lambda e: e.tensor_tensor(out=Msum[:, :, :], in0=M12[:, :, 0, :], in1=M12[:, :, 1, :], op=ALU.add), reads=b_M, writes=b_ms)

    for t in range(18):
        P.pe(lambda e, t=t: e.matmul(PRG[:, 0:32], lhsT=onesb[:, :], rhs=Msum[:, t, :], start=(t == 0), stop=(t == 17)),
             reads=[b_U, b_ms[t]], writes=[b_prg])
    P.dve(lambda e: e.tensor_copy(out=cnt[:], in_=PRG[:, 0:32]), reads=[b_prg], writes=[b_cnt])
    big3 = big[:, 0:32 * 18].rearrange("p (e j) -> p e j", j=18)
    P.dve(lambda e: e.tensor_tensor(out=big3, in0=cnt[:, :].unsqueeze(2).to_broadcast([128, 32, 18]), in1=thr[:], op=ALU.is_gt),
          reads=[b_cnt, b_thr], writes=[b_big])
    P.dve(lambda e: e.tensor_reduce(out=nblk[:], in_=big3, axis=AX.X, op=ALU.add), reads=[b_big], writes=[b_nblk])
    P.dve(lambda e: e.tensor_scalar(out=padded[:], in0=nblk[:], scalar1=128.0, scalar2=None, op0=ALU.mult), reads=[b_nblk], writes=[b_pad])
    P.dve(lambda e: e.tensor_copy(out=ca[:], in_=padded[:]), reads=[b_pad], writes=[b_ca])
    src, bsrc, dst, bdst = ca, b_ca, cb, b_cb
    for sft in (1, 2, 4, 8, 16):
        P.dve(lambda e, src=src, dst=dst, sft=sft: e.tensor_tensor(out=dst[:, sft:32], in0=src[:, sft:32], in1=src[:, 0:32 - sft], op=ALU.add),
              reads=[bsrc], writes=[bdst])
        P.dve(lambda e, src=src, dst=dst, sft=sft: e.tensor_copy(out=dst[:, 0:sft], in_=src[:, 0:sft]), reads=[bsrc], writes=[bdst])
        src, bsrc, dst, bdst = dst, bdst, src, bsrc
    pend, b_pend = src, bsrc
    P.dve(lambda e: e.tensor_tensor(out=pstart[:], in0=pend[:], in1=padded[:], op=ALU.subtract), reads=[b_pend, b_pad], writes=[b_ps])
    big4 = big[:].rearrange("p (b e) -> p b e", e=32)
    P.dve(lambda e: e.tensor_tensor(out=big4, in0=pend[:, :].unsqueeze(1).to_broadcast([128, NB, 32]), in1=bvals[:], op=ALU.is_le),
          reads=[b_pend, b_bv, b_nblk], writes=[b_big])
    P.dve(lambda e: e.tensor_reduce(out=bexp[:], in_=big4, axis=AX.X, op=ALU.add), reads=[b_big], writes=[b_bexp])
    P.dve(lambda e: e.tensor_scalar(out=bexp[:], in0=bexp[:], scalar1=31.0, scalar2=None, op0=ALU.min), reads=[b_bexp], writes=[b_bexp])
    P.dve(lambda e: e.memset(same[:], 0.0), writes=[b_same])
    P.dve(lambda e: e.tensor_tensor(out=same[:, 1:NB], in0=bexp[:, 1:NB], in1=bexp[:, 0:NB - 1], op=ALU.is_equal),
          reads=[b_bexp], writes=[b_same])
    P.dve(lambda e: e.tensor_scalar(out=bexp[:], in0=bexp[:], scalar1=128.0, scalar2=pcol[:, 0:1], op0=ALU.mult, op1=ALU.add),
          reads=[b_bexp, b_pc, b_same], writes=[b_bexp])
    P.dve(lambda e: e.scalar_tensor_tensor(out=bexp[:], in0=same[:], scalar=8192.0, in1=bexp[:], op0=ALU.mult, op1=ALU.add),
          reads=[b_bexp, b_same], writes=[b_bexp])
    P.dve(lambda e: e.tensor_copy(out=widx[:], in_=bexp[:]), reads=[b_bexp], writes=[b_widx])
    for t in range(18):
        P.pe(lambda e, t=t: e.matmul(PRK[:, 0:32], lhsT=Ub[:, :], rhs=Msum[:, t, :], start=True, stop=True), reads=[b_U, b_ms[t]], writes=[b_prk])
        P.pe(lambda e, t=t: e.matmul(PRK[:, 32:64], lhsT=onesb[:, :], rhs=Msum[:, t, :], start=True, stop=True), reads=[b_U, b_ms[t]], writes=[b_prk])
        P.dve(lambda e: e.tensor_tensor(out=tA[:], in0=PRK[:, 0:32], in1=carry[:], op=ALU.add), reads=[b_prk, b_carry], writes=[b_tA])
        P.dve(lambda e: e.tensor_tensor(out=tA[:], in0=tA[:], in1=pstart[:], op=ALU.add), reads=[b_tA, b_ps], writes=[b_tA])
        for k in range(2):
            P.dve(lambda e, t=t, k=k: e.tensor_tensor(out=tB[:], in0=tA[:], in1=M12[:, t, k, :], op=ALU.mult), reads=[b_tA, b_M[t]], writes=[b_tB])
            P.dve(lambda e, t=t, k=k: e.tensor_reduce(out=destf[:, t, k:k + 1], in_=tB[:], axis=AX.X, op=ALU.add), reads=[b_tB], writes=[b_destf[t]])
        P.dve(lambda e: e.tensor_tensor(out=carry[:], in0=carry[:], in1=PRK[:, 32:64], op=ALU.add), reads=[b_prk, b_carry], writes=[b_carry])
    P.dve(lambda e: e.tensor_copy(out=desti[:], in_=destf[:]), reads=b_destf, writes=[b_desti])
    for t in range(18):
        for k in range(2):
            P.dma("pool", lambda e, t=t, k=k: e.indirect_dma_start(out=table, out_offset=bass.IndirectOffsetOnAxis(ap=desti[:, t, k:k + 1], axis=0),
                                                                   in_=tokid[:, t, :], in_offset=None),
                  reads=[b_desti, b_tok, b_tab0], writes=[b_sc[2 * t + k]])

    def gather_x(b):
        i = b % 2
        P.dma("sp", lambda e, b=b, i=i: e.dma_start(out=IDX[i][:], in_=table[b * 128:(b + 1) * 128, :]), reads=b_sc + [b_tab0], writes=[b_idx[i]])
        P.dma("pool", lambda e, i=i: e.indirect_dma_start(out=XG[i][:], out_offset=None, in_=Hd,
                                                          in_offset=bass.IndirectOffsetOnAxis(ap=IDX[i][:, 0:1], axis=0)),
              reads=[b_idx[i], b_hdz] + b_hd, writes=[b_xg[i]])

    def gather_w(b):
        for (wg, bwg, wl) in ((WG1[0], b_wg1[0], w1L), (WG3[0], b_wg3[0], w3L), (WG2[0], b_wg2[0], w2L)):
            P.dma("pool", lambda e, wg=wg, wl=wl, b=b: e.indirect_dma_start(out=wg[:], out_offset=None, in_=wl,
                                                                            in_offset=bass.IndirectOffsetOnAxis(ap=widx[:, b:b + 1], axis=0),
                                                                            bounds_check=regs['b'], oob_is_err=False),
                  reads=[b_widx], writes=[bwg])

    def copy_w(b):
        i = b % 2
        w1f = W1[i][:].rearrange("p k f -> p (k f)"); w3f = W3[i][:].rearrange("p k f -> p (k f)"); w2f = W2[i][:].rearrange("p k f -> p (k f)")
        P.act(lambda e: e.activation(out=w1f[:, 0:2048], in_=WG1[0][:, 0:2048], func=AF.Copy), reads=[b_wg1[0]], writes=[b_w1[i]])
        P.dve(lambda e: e.tensor_copy(out=w1f[:, 2048:4096], in_=WG1[0][:, 2048:4096]), reads=[b_wg1[0]], writes=[b_w1[i]])
        P.act(lambda e: e.activation(out=w3f[:, 0:2048], in_=WG3[0][:, 0:2048], func=AF.Copy), reads=[b_wg3[0]], writes=[b_w3[i]])
        P.dve(lambda e: e.tensor_copy(out=w3f[:, 2048:4096], in_=WG3[0][:, 2048:4096]), reads=[b_wg3[0]], writes=[b_w3[i]])
        P.dve(lambda e: e.tensor_copy(out=w2f, in_=WG2[0][:]), reads=[b_wg2[0]], writes=[b_w2[i]])

    def st_t8(b):
        i = b % 2
        for k in range(8):
            P.pe(lambda e, k=k, i=i: e.transpose(out=PTB[:, k, :], in_=XG[i][:, k * 128:(k + 1) * 128], identity=identb[:]),
                 reads=[b_xg[i], b_idb], writes=[b_ptb])
        P.act(lambda e, i=i: e.activation(out=XGT[i][:], in_=PTB[:], func=AF.Copy), reads=[b_ptb], writes=[B_xgt[i]])

    def st_h(b):
        i = b % 2
        for k in range(8):
            P.pe(lambda e, k=k, i=i: e.matmul(PH1[i][:, :], lhsT=XGT[i][:, k, :], rhs=W1[i][:, k, :], start=(k == 0), stop=(k == 7)),
                 reads=[B_xgt[i], b_w1[i]], writes=[B_ph1[i]])
        for k in range(8):
            P.pe(lambda e, k=k, i=i: e.matmul(PH3[i][:, :], lhsT=XGT[i][:, k, :], rhs=W3[i][:, k, :], start=(k == 0), stop=(k == 7)),
                 reads=[B_xgt[i], b_w3[i]], writes=[B_ph3[i]])
        P.act(lambda e, i=i: e.activation(out=SSB[i][:], in_=PH1[i][:, :], func=AF.Silu), reads=[B_ph1[i]], writes=[B_ssb[i]])
        P.dve(lambda e, i=i: e.tensor_tensor(out=HB[i][:], in0=PH3[i][:, :], in1=SSB[i][:], op=ALU.mult), reads=[B_ph3[i], B_ssb[i]], writes=[B_hb[i]])

    def st_t4(b):
        i = b % 2
        for f in range(4):
            P.pe(lambda e, f=f, i=i: e.transpose(out=PTH2[:, i, f, :], in_=HB[i][:, f * 128:(f + 1) * 128], identity=identb[:]),
                 reads=[B_hb[i], b_idb], writes=[B_pth[i]])
        P.act(lambda e, i=i: e.activation(out=HTT[i][:], in_=PTH2[:, i, :, :], func=AF.Copy), reads=[B_pth[i]], writes=[B_hT[i]])

    def st_y(b):
        i = b % 2
        for h in range(2):
            for f in range(4):
                P.pe(lambda e, h=h, f=f, i=i: e.matmul(PY[:, h * 512:(h + 1) * 512], lhsT=HTT[i][:, f, :], rhs=W2[i][:, f, h * 512:(h + 1) * 512],
                                                      start=(f == 0), stop=(f == 3)), reads=[B_hT[i], b_w2[i]], writes=[b_py])
        P.dve(lambda e, i=i: e.tensor_copy(out=YS[i][:], in_=PY[:]), reads=[b_py], writes=[b_ys[i]])
        P.dma("sp", lambda e, i=i, b=b: e.dma_start(out=Yd[b * 128:(b + 1) * 128, :], in_=YS[i][:]), reads=[b_ys[i]], writes=[b_yd[b]])

    gather_x(0); gather_x(1)
    gather_w(0)
    for b0 in range(0, NB, 2):
        b1 = b0 + 1
        copy_w(b0)
        gather_w(b1)
        copy_w(b1)
        if b1 + 1 < NB:
            gather_w(b1 + 1)
        st_t8(b0); st_t8(b1)
        if b0 + 2 < NB:
            gather_x(b0 + 2); gather_x(b0 + 3)
        st_h(b0); st_h(b1)
        st_t4(b0); st_t4(b1)
        st_y(b0); st_y(b1)

    outs = []

    def phase_c(t, xs, h32, Y1, Y2, b_xs, b_h32, b_y1, b_y2):
        s = 0 if t < 16 else 1
        if t == 0 or t == 16:
            P.dma("sp", lambda e, s=s: e.dma_start(out=rows[:, 0, :], in_=rowsd[s, 2]), writes=[b_rows])
        P.dma("sp", lambda e, t=t: e.dma_start(out=xs[:], in_=xt[t]), writes=[b_xs])
        P.dma("pool", lambda e, t=t: e.indirect_dma_start(out=Y1[:], out_offset=None, in_=Yd,
                                                          in_offset=bass.IndirectOffsetOnAxis(ap=desti[:, t, 0:1], axis=0)),
              reads=[b_desti] + b_yd, writes=[b_y1])
        P.dma("pool", lambda e, t=t: e.indirect_dma_start(out=Y2[:], out_offset=None, in_=Yd,
                                                          in_offset=bass.IndirectOffsetOnAxis(ap=desti[:, t, 1:2], axis=0)),
              reads=[b_desti] + b_yd, writes=[b_y2])
        P.dve(lambda e, t=t: e.tensor_scalar(out=h32[:], in0=Y1[:], scalar1=g12[:, t, 0:1], scalar2=None, op0=ALU.mult),
              reads=[b_y1, b_g12[t]], writes=[b_h32])
        P.dve(lambda e, t=t: e.scalar_tensor_tensor(out=h32[:], in0=Y2[:], scalar=g12[:, t, 1:2], in1=h32[:], op0=ALU.mult, op1=ALU.add),
              reads=[b_y2, b_g12[t], b_h32], writes=[b_h32])
        P.dve(lambda e: e.tensor_tensor(out=h32[:], in0=h32[:], in1=rows[:, 0, :], op=ALU.mult), reads=[b_h32, b_rows], writes=[b_h32])
        P.dve(lambda e: e.tensor_tensor(out=xs[:], in0=xs[:], in1=h32[:], op=ALU.add), reads=[b_h32, b_xs], writes=[b_xs])
        outs.append(P.dma("sp", lambda e, t=t: e.dma_start(out=xo[t], in_=xs[:]), reads=[b_xs]))
    for t in range(18):
        i = t % 2
        phase_c(t, XS[i], H32[i], YY1[i], YY2[i], B_xs[i], B_h32[i], B_y1[i], B_y2[i])
    P.emit(final_wait_ops=outs)
    return nc


def build_att():
    nc = bass.Bass("TRN2", target_bir_lowering=False)
    P = Prog(nc)
    xt = P.din("xt", [20, 128, 1024], F32)
    mcols = P.din("mcols", [128, 2, 2, 8], F32)
    wqkv = P.din("wqkv", [1024, 1536], F32)
    gqd = P.din("gq", [128, 64], F32)
    gkd = P.din("gk", [128, 64], F32)
    roped = P.din("rope", [18, 128, 2, 32], F32)
    sinkd = P.din("sink", [128, 16], F32)
    maskd = P.din("masks", [128, 4, 128], F32)
    OT = P.dout("OT", [1024, 2304], BF16)

    wb = P.sb("wb", [128, 8, 1536], BF16)
    mc = P.sb("mc", [128, 2, 2, 8], F32)
    gq = P.sb("gqt", [128, 64], F32)
    gk = P.sb("gkt", [128, 64], F32)
    rope = P.sb("ropet", [128, 18, 2, 32], F32)
    esink = P.sb("esink", [128, 16], F32)
    masks = P.sb("maskst", [128, 4, 128], BF16)
    epsc = P.sb("epsc", [128, 1], F32)
    xs = P.sb("xs", [128, 1024], F32)
    junk = P.sb("junk", [128, 1024], BF16)
    xn = P.sb("xn", [128, 1024], BF16)
    ss = P.sb("ss", [128, 20], F32)
    rstd = P.sb("rstd", [128, 20], F32)
    hT = P.sb("hT", [128, 8, 128], BF16)
    qf = P.sb("qf", [128, 1024], F32)
    kf = P.sb("kf", [128, 256], F32)
    tmp = P.sb("tmp", [128, 1024], F32)
    tmp2 = P.sb("tmp2", [128, 512], F32)
    tmp3 = P.sb("tmp3", [128, 512], F32)
    sq = P.sb("sq", [128, 20], F32)
    qr = P.sb("qr", [128, 1024], BF16)
    kr = P.sb("kr", [128, 256], BF16)
    QT = P.sb("QT", [64, 18, 16, 128], BF16)
    KT = P.sb("KT", [64, 20, 4, 128], BF16)
    VX = P.sb("VX", [128, 20, 4, 65], BF16)
    PTS = [P.sb(f"pts{i}", [128, 5, 512], BF16) for i in range(2)]
    den = P.sb("den", [128, 8], F32)
    osb = P.sb("osb", [128, 1024], BF16)
    ots = P.sb("ots", [128, 8, 128], BF16)
    ident, b_ident = make_ident(P)

    b_w, b_mc, b_gq, b_gk, b_rope, b_esink, b_masks, b_eps = P.bufs(8)
    b_xs, b_junk, b_xn, b_hT, b_qf, b_kf, b_tmp, b_tmp2, b_tmp3, b_sq, b_qr, b_kr = P.bufs(12)
    b_ss = P.bufs(20); b_rstd = P.bufs(20)
    b_QT = P.bufs(18); b_KT = P.bufs(20); b_VX = P.bufs(20); b_vone = P.buf()
    b_pts = P.bufs(2); b_den = P.buf(); b_osb = P.buf(); b_ots = P.buf()
    b_phase = P.buf()

    P.dma("pool", lambda e: e.dma_start(out=wb[:], in_=wqkv.rearrange("(k p) n -> p k n", p=128)), writes=[b_w])
    P.dma("sp", lambda e: e.dma_start(out=mc[:], in_=mcols), writes=[b_mc])
    P.dma("sp", lambda e: e.dma_start(out=gq[:], in_=gqd), writes=[b_gq])
    P.dma("sp", lambda e: e.dma_start(out=gk[:], in_=gkd), writes=[b_gk])
    P.dma("sp", lambda e: e.dma_start(out=rope[:], in_=roped.rearrange("t p a i -> p t a i")), writes=[b_rope])
    P.dma("sp", lambda e: e.dma_start(out=esink[:], in_=sinkd), writes=[b_esink])
    P.dma("pool", lambda e: e.dma_start(out=masks[:], in_=maskd), writes=[b_masks])
    P.pool(lambda e: e.memset(epsc[:], EPS), writes=[b_eps])
    P.act(lambda e: e.activation(out=esink[:], in_=esink[:], func=AF.Exp), reads=[b_esink], writes=[b_esink])
    P.dve(lambda e: e.tensor_scalar(out=gq[:], in0=gq[:], scalar1=0.125, scalar2=None, op0=ALU.mult), reads=[b_gq], writes=[b_gq])
    P.pool(lambda e: e.memset(VX[:], 1.0), writes=[b_vone])

    ps1 = ExitStack()
    PTx = ps1.enter_context(nc.psum_tensor("ptx", [128, 8, 128], BF16))
    PQ = ps1.enter_context(nc.psum_tensor("pq", [128, 1024], F32))
    PKV = ps1.enter_context(nc.psum_tensor("pkv", [128, 512], F32))
    PTq = ps1.enter_context(nc.psum_tensor("ptq", [64, 16, 128], BF16))
    PTk = ps1.enter_context(nc.psum_tensor("ptk", [64, 4, 128], BF16))
    b_ptx, b_pq, b_pkv, b_ptq, b_ptk = P.bufs(5)

    def qknorm_rope(src, H, gain, bgain, dst, bdst, t, do_rope, bsrc):
        W = H * 64
        P.dve(lambda e: e.tensor_tensor(out=tmp[:, 0:W], in0=src[:, 0:W], in1=src[:, 0:W], op=ALU.mult), reads=[bsrc], writes=[b_tmp])
        P.dve(lambda e: e.tensor_reduce(out=sq[:, 0:H], in_=tmp[:, 0:W].rearrange("p (h d) -> p h d", d=64), axis=AX.X, op=ALU.add),
              reads=[b_tmp], writes=[b_sq])
        P.act(lambda e: e.activation(out=sq[:, 0:H], in_=sq[:, 0:H], func=AF.Sqrt, scale=1.0 / 64, bias=epsc[:, 0:1]),
              reads=[b_sq, b_eps], writes=[b_sq])
        P.dve(lambda e: e.reciprocal(out=sq[:, 0:H], in_=sq[:, 0:H]), reads=[b_sq], writes=[b_sq])
        s3 = src[:, 0:W].rearrange("p (h d) -> p h d", d=64)
        t3 = tmp[:, 0:W].rearrange("p (h d) -> p h d", d=64)
        P.dve(lambda e: e.tensor_tensor(out=t3, in0=s3, in1=sq[:, 0:H].unsqueeze(2).to_broadcast([128, H, 64]), op=ALU.mult),
              reads=[bsrc, b_sq], writes=[b_tmp])
        if not do_rope:
            d3 = dst[:, 0:W].rearrange("p (h d) -> p h d", d=64)
            P.dve(lambda e: e.tensor_tensor(out=d3, in0=t3, in1=gain[:, :].unsqueeze(1).to_broadcast([128, H, 64]), op=ALU.mult),
                  reads=[b_tmp, bgain], writes=[bdst])
            return
        P.dve(lambda e: e.tensor_tensor(out=t3, in0=t3, in1=gain[:, :].unsqueeze(1).to_broadcast([128, H, 64]), op=ALU.mult),
              reads=[b_tmp, bgain], writes=[b_tmp])
        x5 = tmp[:, 0:W].rearrange("p (h a f i) -> p h a f i", a=2, f=2, i=16)
        d5 = dst[:, 0:W].rearrange("p (h a f i) -> p h a f i", a=2, f=2, i=16)
        x1 = x5[:, :, :, 0, :]; x2 = x5[:, :, :, 1, :]
        cs = rope[:, t, 0, :].rearrange("p (a i) -> p a i", a=2).unsqueeze(1).to_broadcast([128, H, 2, 16])
        sn = rope[:, t, 1, :].rearrange("p (a i) -> p a i", a=2).unsqueeze(1).to_broadcast([128, H, 2, 16])
        n = H * 32
        a4 = tmp2[:, 0:n].rearrange("p (h a i) -> p h a i", a=2, i=16)
        b4 = tmp3[:, 0:n].rearrange("p (h a i) -> p h a i", a=2, i=16)
        P.dve(lambda e: e.tensor_tensor(out=a4, in0=x1, in1=cs, op=ALU.mult), reads=[b_tmp, b_rope], writes=[b_tmp2])
        P.dve(lambda e: e.tensor_tensor(out=b4, in0=x2, in1=sn, op=ALU.mult), reads=[b_tmp, b_rope], writes=[b_tmp3])
        P.dve(lambda e: e.tensor_tensor(out=d5[:, :, :, 0, :], in0=a4, in1=b4, op=ALU.subtract), reads=[b_tmp2, b_tmp3], writes=[bdst])
        P.dve(lambda e: e.tensor_tensor(out=a4, in0=x1, in1=sn, op=ALU.mult), reads=[b_tmp, b_rope, bdst], writes=[b_tmp2])
        P.dve(lambda e: e.tensor_tensor(out=b4, in0=x2, in1=cs, op=ALU.mult), reads=[b_tmp, b_rope, bdst], writes=[b_tmp3])
        P.dve(lambda e: e.tensor_tensor(out=d5[:, :, :, 1, :], in0=a4, in1=b4, op=ALU.add), reads=[b_tmp2, b_tmp3], writes=[bdst])

    qidx = {}
    for t in range(20):
        s = 0 if t < 18 else 1
        is_q = (1 <= t <= 16) or t >= 18
        do_rope = t < 18
        P.dma("sp", lambda e, t=t: e.dma_start(out=xs[:], in_=xt[t]), writes=[b_xs])
        P.act(lambda e, t=t: e.activation(out=junk[:], in_=xs[:], func=AF.Square, accum_out=ss[:, t:t + 1]),
              reads=[b_xs], writes=[b_junk, b_ss[t]])
        P.act(lambda e, t=t: e.activation(out=rstd[:, t:t + 1], in_=ss[:, t:t + 1], func=AF.Sqrt, scale=1.0 / 1024, bias=epsc[:, 0:1]),
              reads=[b_ss[t], b_eps], writes=[b_rstd[t]])
        P.dve(lambda e, t=t: e.reciprocal(out=rstd[:, t:t + 1], in_=rstd[:, t:t + 1]), reads=[b_rstd[t]], writes=[b_rstd[t]])
        P.dve(lambda e, t=t: e.tensor_scalar(out=xn[:], in0=xs[:], scalar1=rstd[:, t:t + 1], scalar2=None, op0=ALU.mult),
              reads=[b_xs, b_rstd[t]], writes=[b_xn])
        for k in range(8):
            P.pe(lambda e, k=k: e.transpose(out=PTx[:, k, :], in_=xn[:, k * 128:(k + 1) * 128], identity=ident[:]),
                 reads=[b_xn, b_ident], writes=[b_ptx])
        for k in range(8):
            P.act(lambda e, k=k, s=s: e.activation(out=hT[:, k, :], in_=PTx[:, k, :], func=AF.Identity,
                                                   scale=mc[:, s, 1, k:k + 1], bias=mc[:, s, 0, k:k + 1]),
                  reads=[b_ptx, b_mc], writes=[b_hT, b_phase])
        if is_q:
            for nb in range(2):
                for k in range(8):
                    P.pe(lambda e, nb=nb, k=k: e.matmul(PQ[:, nb * 512:(nb + 1) * 512], lhsT=hT[:, k, :], rhs=wb[:, k, nb * 512:(nb + 1) * 512],
                                                        start=(k == 0), stop=(k == 7)), reads=[b_hT, b_w], writes=[b_pq])
        for k in range(8):
            P.pe(lambda e, k=k: e.matmul(PKV[:, :], lhsT=hT[:, k, :], rhs=wb[:, k, 1024:1536], start=(k == 0), stop=(k == 7)),
                 reads=[b_hT, b_w], writes=[b_pkv])
        P.act(lambda e: e.activation(out=kf[:], in_=PKV[:, 0:256], func=AF.Copy), reads=[b_pkv], writes=[b_kf, b_phase])
        P.act(lambda e, t=t: e.activation(out=VX[:, t, :, 0:64], in_=PKV[:, 256:512].rearrange("p (j d) -> p j d", d=64), func=AF.Copy),
              reads=[b_pkv, b_vone], writes=[b_VX[t], b_phase])
        qknorm_rope(kf, 4, gk, b_gk, kr, b_kr, t, do_rope, b_kf)
        for j in range(4):
            P.pe(lambda e, j=j: e.transpose(out=PTk[:, j, :], in_=kr[:, j * 64:(j + 1) * 64], identity=ident[:]),
                 reads=[b_kr, b_ident], writes=[b_ptk])
        P.act(lambda e, t=t: e.activation(out=KT[:, t, :, :], in_=PTk[:, :, :], func=AF.Copy), reads=[b_ptk], writes=[b_KT[t], b_phase])
        if is_q:
            qi = len(qidx); qidx[t] = qi
            P.act(lambda e: e.activation(out=qf[:], in_=PQ[:], func=AF.Copy), reads=[b_pq], writes=[b_qf, b_phase])
            qknorm_rope(qf, 16, gq, b_gq, qr, b_qr, t, do_rope, b_qf)
            for h in range(16):
                P.pe(lambda e, h=h: e.transpose(out=PTq[:, h, :], in_=qr[:, h * 64:(h + 1) * 64], identity=ident[:]),
                     reads=[b_qr, b_ident], writes=[b_ptq])
            P.act(lambda e, qi=qi: e.activation(out=QT[:, qi, :, :], in_=PTq[:, :, :], func=AF.Copy), reads=[b_ptq], writes=[b_QT[qi], b_phase])
    ps1.close()

    PS = [P.ps(f"ps{i}", [128, 512], F32) for i in range(5)]
    PO = P.ps("po", [128, 4, 65], F32)
    POT = P.ps("pot", [128, 8, 128], BF16)
    b_ps = P.bufs(5); b_po = P.buf(); b_pot = P.buf()
    outs = []
    first = True
    pi = 0
    for t in list(range(1, 17)) + [18, 19]:
        qi = qidx[t]
        if t < 18:
            chunks = [(t - 1, 0 if t == 1 else 1), (t, None), (t + 1, 3 if t == 16 else 2), (18, None), (19, None)]
            col0 = (t - 1) * 128
        else:
            chunks = [(18, None), (19, None)]
            col0 = 2048 + (t - 18) * 128
        nch = len(chunks)
        for j in range(4):
            pts = PTS[pi % 2]; bpts = b_pts[pi % 2]; pi += 1
            for ci, (kt, m) in enumerate(chunks):
                wr = [b_ps[ci]] + ([b_phase] if first else [])
                first = False
                P.pe(lambda e, ci=ci, kt=kt, j=j, qi=qi: e.matmul(PS[ci][:, :], lhsT=KT[:, kt, j, :],
                                                                  rhs=QT[:, qi, 4 * j:4 * j + 4, :].rearrange("p h q -> p (h q)"),
                                                                  start=True, stop=True),
                     reads=[b_KT[kt], b_QT[qi]], writes=wr)
                P.act(lambda e, ci=ci, pts=pts: e.activation(out=pts[:, ci, :], in_=PS[ci][:, :], func=AF.Exp), reads=[b_ps[ci]], writes=[bpts])
                if m is not None:
                    P.dve(lambda e, ci=ci, pts=pts, m=m: e.tensor_tensor(out=pts[:, ci, :].rearrange("p (h q) -> p h q", h=4),
                                                                         in0=pts[:, ci, :].rearrange("p (h q) -> p h q", h=4),
                                                                         in1=masks[:, m, :].unsqueeze(1).to_broadcast([128, 4, 128]), op=ALU.mult),
                          reads=[bpts, b_masks], writes=[bpts])
            for g in range(4):
                for ci, (kt, m) in enumerate(chunks):
                    P.pe(lambda e, g=g, ci=ci, kt=kt, j=j, pts=pts, nch=nch: e.matmul(PO[:, g, :], lhsT=pts[:, ci, g * 128:(g + 1) * 128],
                                                                                      rhs=VX[:, kt, j, :], start=(ci == 0), stop=(ci == nch - 1)),
                         reads=[bpts, b_VX[kt]], writes=[b_po])
            P.dve(lambda e, j=j: e.tensor_tensor(out=den[:, 0:4], in0=PO[:, :, 64], in1=esink[:, 4 * j:4 * j + 4], op=ALU.add),
                  reads=[b_po, b_esink], writes=[b_den])
            P.dve(lambda e: e.reciprocal(out=den[:, 0:4], in_=den[:, 0:4]), reads=[b_den], writes=[b_den])
            P.dve(lambda e, j=j: e.tensor_tensor(out=osb[:, 256 * j:256 * j + 256].rearrange("p (g d) -> p g d", d=64), in0=PO[:, :, 0:64],
                                                 in1=den[:, 0:4].unsqueeze(2).to_broadcast([128, 4, 64]), op=ALU.mult),
                  reads=[b_po, b_den], writes=[b_osb])
        for k in range(8):
            P.pe(lambda e, k=k: e.transpose(out=POT[:, k, :], in_=osb[:, k * 128:(k + 1) * 128], identity=ident[:]),
                 reads=[b_osb, b_ident], writes=[b_pot])
        P.act(lambda e: e.activation(out=ots[:], in_=POT[:], func=AF.Copy), reads=[b_pot], writes=[b_ots])
        outs.append(P.dma("sp", lambda e, col0=col0: e.dma_start(out=OT.rearrange("(k p) t -> p k t", p=128)[:, :, col0:col0 + 128], in_=ots[:]),
                          reads=[b_ots]))
    P.emit(final_wait_ops=outs)
    return nc


def rope_tables(core):
    t0 = core * 2048 - 128
    t = np.arange(t0, t0 + 18 * 128)
    t = np.clip(t, 0, 16383)
    pos = np.stack([t // 64, t % 64], -1).astype(np.float32)
    freqs = (10000.0 ** (-np.arange(16, dtype=np.float32) / 16)).astype(np.float32)
    ang = pos[:, :, None] * freqs
    cs = np.cos(ang).reshape(-1, 32); sn = np.sin(ang).reshape(-1, 32)
    return np.stack([cs, sn], 1).reshape(18, 128, 2, 32).astype(np.float32)


def att_masks(core):
    j = np.arange(128)[:, None]; i = np.arange(128)[None, :]
    prev = (j >= i).astype(np.float32); nxt = (j <= i).astype(np.float32)
    m = np.stack([prev if core > 0 else np.zeros_like(prev), prev, nxt, nxt if core < 7 else np.zeros_like(nxt)], 1)
    return np.ascontiguousarray(m).astype(np.float32)


def build_mod():
    nc = bass.Bass("TRN2", target_bir_lowering=False)
    P = Prog(nc)
    ccd = P.din("cc", [128, 8, 2], F32)
    wmd = P.din("wm", [4, 1024, 768], F32)
    bmd = P.din("bm", [1, 4, 768], F32)
    gmd = P.din("gm", [1, 4, 256], F32)
    mo = P.dout("mo", [4, 2, 768], F32)
    cc = P.sb("cct", [128, 8, 2], F32)
    S = P.sb("S", [128, 8, 33], F32)
    Sh = P.sb("Sh", [128, 8, 33], BF16)
    Sl = P.sb("Sl", [128, 8, 33], BF16)
    brow = P.sb("brow", [33, 4, 768], F32)
    grow = P.sb("grow", [33, 4, 256], F32)
    WT = [P.sb(f"wt{i}", [128, 8, 768], F32) for i in range(2)]
    Wh = P.sb("Wh", [128, 8, 768], BF16)
    Wl = P.sb("Wl", [128, 8, 768], BF16)
    RR = [P.sb(f"r{i}", [33, 768], F32) for i in range(2)]
    PM = P.ps("pm", [128, 1024], F32)
    b_cc, b_S, b_Sh, b_Sl, b_brow, b_grow, b_Wh, b_Wl, b_pm = P.bufs(9)
    b_wt = P.bufs(2); b_r = P.bufs(2)
    P.dma("sp", lambda e: e.dma_start(out=cc[:], in_=ccd), writes=[b_cc])
    P.pool(lambda e: e.memset(S[:], 0.0), writes=[b_S])
    P.pool(lambda e: e.memset(brow[:], 0.0), writes=[b_brow])
    P.pool(lambda e: e.memset(grow[:], 0.0), writes=[b_grow])
    for prt in (0, 32):
        P.dma("sp", lambda e, prt=prt: e.dma_start(out=brow[prt:prt + 1], in_=bmd), reads=[], writes=[b_brow])
        P.dma("sp", lambda e, prt=prt: e.dma_start(out=grow[prt:prt + 1], in_=gmd), reads=[], writes=[b_grow])
    P.act(lambda e: e.activation(out=S[:, :, 0], in_=cc[:, :, 0], func=AF.Silu), reads=[b_cc], writes=[b_S])
    P.act(lambda e: e.activation(out=S[:, :, 32], in_=cc[:, :, 1], func=AF.Silu), reads=[b_cc], writes=[b_S])
    P.dve(lambda e: e.tensor_copy(out=Sh[:], in_=S[:]), reads=[b_S], writes=[b_Sh])
    P.dve(lambda e: e.tensor_tensor(out=Sl[:], in0=S[:], in1=Sh[:], op=ALU.subtract), reads=[b_S, b_Sh], writes=[b_Sl])
    outs = []
    for l in range(4):
        wt = WT[l % 2]; bwt = b_wt[l % 2]; r = RR[l % 2]; br_ = b_r[l % 2]
        P.dma("sp", lambda e, wt=wt, l=l: e.dma_start(out=wt[:], in_=wmd[l].rearrange("(k p) n -> p k n", p=128)), writes=[bwt])
        P.dve(lambda e, wt=wt: e.tensor_copy(out=Wh[:], in_=wt[:]), reads=[bwt], writes=[b_Wh])
        P.dve(lambda e, wt=wt: e.tensor_tensor(out=Wl[:], in0=wt[:], in1=Wh[:], op=ALU.subtract), reads=[bwt, b_Wh], writes=[b_Wl])
        for half in range(2):
            n = 0
            for k in range(8):
                for (sa, bsa, wa, bwa) in ((Sh, b_Sh, Wh, b_Wh), (Sh, b_Sh, Wl, b_Wl), (Sl, b_Sl, Wh, b_Wh)):
                    P.pe(lambda e, sa=sa, wa=wa, k=k, half=half, n=n: e.matmul(PM[0:33, half * 512:half * 512 + 384], lhsT=sa[:, k, :],
                                                                             rhs=wa[:, k, half * 384:(half + 1) * 384],
                                                                             start=(n == 0), stop=(n == 23)),
                         reads=[bsa, bwa], writes=[b_pm])
                    n += 1
        for half in range(2):
            P.dve(lambda e, r=r, half=half, l=l: e.tensor_tensor(out=r[:, half * 384:(half + 1) * 384], in0=PM[0:33, half * 512:half * 512 + 384],
                                                                 in1=brow[:, l, half * 384:(half + 1) * 384], op=ALU.add),
                  reads=[b_pm, b_brow], writes=[br_])
        P.dve(lambda e, r=r, l=l: e.scalar_tensor_tensor(out=r[:, 128:256], in0=r[:, 128:256], scalar=1.0, in1=grow[:, l, 0:128],
                                                         op0=ALU.add, op1=ALU.mult), reads=[br_, b_grow], writes=[br_])
        P.dve(lambda e, r=r, l=l: e.scalar_tensor_tensor(out=r[:, 512:640], in0=r[:, 512:640], scalar=1.0, in1=grow[:, l, 128:256],
                                                         op0=ALU.add, op1=ALU.mult), reads=[br_, b_grow], writes=[br_])
        outs.append(P.dma("sp", lambda e, r=r, l=l: e.dma_start(out=mo[l, 0:1, :], in_=r[0:1, :]), reads=[br_]))
        outs.append(P.dma("sp", lambda e, r=r, l=l: e.dma_start(out=mo[l, 1:2, :], in_=r[32:33, :]), reads=[br_]))
    P.emit(final_wait_ops=outs)
    return nc


def run_mod(inp, progs):
    c = np.asarray(inp['c'], np.float32).reshape(1024)
    cx = np.asarray(inp['c_ctx'], np.float32).reshape(1024)
    cc = np.ascontiguousarray(np.stack([c.reshape(8, 128).T, cx.reshape(8, 128).T], -1))
    wm6 = np.asarray(inp['w_mod'], np.float32).reshape(4, 1024, 6, 1024)
    bm6 = np.asarray(inp['b_mod'], np.float32).reshape(4, 6, 1024)
    gmix = np.asarray(inp['norm_mix_g'], np.float32); gffn = np.asarray(inp['norm_ffn_g'], np.float32)
    ins = []
    for k in range(8):
        sl = slice(128 * k, 128 * k + 128)
        ins.append(dict(cc=cc, wm=np.ascontiguousarray(wm6[:, :, :, sl]).reshape(4, 1024, 768),
                        bm=np.ascontiguousarray(bm6[:, :, sl]).reshape(1, 4, 768),
                        gm=np.ascontiguousarray(np.stack([gmix[:, sl], gffn[:, sl]], 1)).reshape(1, 4, 256)))
    res = run_bass_kernel_spmd(progs['mod'], ins, core_ids=list(range(8)))
    modx = np.zeros((4, 2, 6, 1024), np.float32)
    for k in range(8):
        modx[:, :, :, 128 * k:128 * k + 128] = np.asarray(res.results[k]['mo']).reshape(4, 2, 6, 128)
    return modx


NCORES = 8
CORES = list(range(NCORES))


def _cols(v):
    return v.reshape(8, 128).T


def _rep(v):
    return np.ascontiguousarray(np.tile(np.asarray(v, np.float32).reshape(1, -1), (128, 1)))


def _wlayout(w, kch):
    E, K, N = w.shape
    return np.ascontiguousarray(np.asarray(w, np.float32).reshape(E, kch, 128, N).transpose(0, 2, 1, 3)).reshape(E * 128, kch * N)


def _moe_consts():
    bv = np.tile((128.0 * np.arange(NB, dtype=np.float32))[:, None], (1, 32)).reshape(1, -1)
    thr = np.tile((128.0 * np.arange(18, dtype=np.float32))[None, :], (32, 1)).reshape(1, -1)
    tok = (np.arange(18)[None, :, None] * 128 + np.arange(128)[:, None, None] + np.zeros((1, 1, 16))).astype(np.int32)
    return dict(bvals=np.tile(bv, (128, 1)).astype(np.float32), thr=np.tile(thr, (128, 1)).astype(np.float32),
                pcol=np.arange(128, dtype=np.float32).reshape(128, 1), tokid=np.ascontiguousarray(tok))


def _run(nc, ins):
    res = run_bass_kernel_spmd(nc, ins, core_ids=CORES)
    return res.results


def kernel(**inp):
    inp = {k: np.asarray(v) for k, v in inp.items()}
    progs = dict(mod=build_mod(), e1=build_e1(), e2=build_e2(), e3=build_e3(), att=build_att(), moe=build_moe2(), cast=build_cast())
    xl = np.ascontiguousarray(inp['x'][0], dtype=np.float32)
    xc = np.ascontiguousarray(inp['ctx'][0], dtype=np.float32)
    modx = run_mod(inp, progs)
    K1 = dft_consts()
    K2 = fft_consts()
    MC = _moe_consts()
    wbf = run_cast(inp, progs)
    for layer in range(4):
        j = layer // 2
        mcols = np.zeros((128, 2, 2, 8), np.float32)
        for s in range(2):
            mcols[:, s, 0] = _cols(modx[layer, s, 0]); mcols[:, s, 1] = _cols(modx[layer, s, 1])
        g1 = np.ascontiguousarray(np.stack([_rep(modx[layer, 0, 2]), _rep(modx[layer, 1, 2])], 1))
        if layer % 2 == 0:
            cwh = np.ascontiguousarray(inp['conv_w'][j].reshape(3, 4, 128).transpose(2, 0, 1))
            ins = []
            for c in CORES:
                xt = np.zeros((19, 128, 1024), np.float32)
                xt[:16] = xl[2048 * c:2048 * (c + 1)].reshape(16, 128, 1024)
                xt[16:18] = xc.reshape(2, 128, 1024)
                if c > 0:
                    xt[18, 0] = xl[2048 * c - 1]
                if c < 7:
                    xt[18, 1] = xl[2048 * (c + 1)]
                flags = np.zeros((128, 2), np.float32); flags[:, 0] = float(c > 0); flags[:, 1] = float(c < 7)
                ins.append(dict(xt=xt, mcols=mcols, win=inp['w_in_even'][j], cw=cwh, cs128=K1['cs128'], c256=K1['c256'],
                                ns256=K1['ns256'], flags=flags))
            r1 = _run(progs['e1'], ins)
            Bfull = np.concatenate([np.asarray(r1[c]['bout']) for c in CORES], 0)
            B5 = Bfull.reshape(128, 128, 4, 2, 128)
            ins = []
            for c in CORES:
                g = c // 2; m0 = 64 * (c % 2)
                zin = np.ascontiguousarray(B5[:, :, g, :, m0:m0 + 64].transpose(0, 2, 3, 1))
                ins.append(dict(zin=zin, m1=K2['m1'], m2=K2['m2'], tw=K2['tw'], c2=K2['c2']))
            r2 = _run(progs['e2'], ins)
            fmT = np.zeros((512, 16384), dtype=Bfull.dtype)
            for c in CORES:
                g = c // 2; m0 = 64 * (c % 2)
                fmT[g * 128 + m0:g * 128 + m0 + 64] = np.asarray(r2[c]['FT']).reshape(64, 16384)
            yTs = []
            for c in CORES:
                top = np.concatenate([fmT[:, 2048 * c:2048 * (c + 1)], np.asarray(r1[c]['fcT'])], 1)
                yTs.append(np.ascontiguousarray(np.concatenate([top, np.asarray(r1[c]['ycT'])], 0)))
            wproj = inp['w_out_even'][j]
        else:
            ins = []
            for c in CORES:
                xt = np.zeros((20, 128, 1024), np.float32)
                if c > 0:
                    xt[0] = xl[2048 * c - 128:2048 * c]
                xt[1:17] = xl[2048 * c:2048 * (c + 1)].reshape(16, 128, 1024)
                if c < 7:
                    xt[17] = xl[2048 * (c + 1):2048 * (c + 1) + 128]
                xt[18:20] = xc.reshape(2, 128, 1024)
                ins.append(dict(xt=xt, mcols=mcols, wqkv=inp['w_qkv'][j], gq=_rep(inp['q_norm_g'][j]), gk=_rep(inp['k_norm_g'][j]),
                                rope=rope_tables(c), sink=_rep(inp['sink_logit'][j]), masks=att_masks(c)))
            ra = _run(progs['att'], ins)
            yTs = [np.asarray(ra[c]['OT']) for c in CORES]
            wproj = inp['w_o'][j]
        ins = []
        for c in CORES:
            xt = np.concatenate([xl[2048 * c:2048 * (c + 1)], xc], 0).reshape(18, 128, 1024)
            ins.append(dict(yT=yTs[c], xt=np.ascontiguousarray(xt), g1=g1, w=wproj))
        r3 = _run(progs['e3'], ins)
        rows = np.zeros((2, 3, 128, 1024), np.float32)
        for s in range(2):
            rows[s, 0] = _rep(modx[layer, s, 4]); rows[s, 1] = _rep(modx[layer, s, 3]); rows[s, 2] = _rep(modx[layer, s, 5])
        wr = np.ascontiguousarray(np.concatenate([inp['w_router_g'][layer], inp['w_router_e'][layer]], 1))
        br = _rep(np.concatenate([inp['b_router_g'][layer], inp['b_router_e'][layer]]))
        w1L, w3L, w2L = wbf[layer]
        ins = []
        for c in CORES:
            ins.append(dict(xt=np.asarray(r3[c]['xo']), rows=rows, wr=wr, br=br, w1L=w1L, w3L=w3L, w2L=w2L, **MC))
        r4 = _run(progs['moe'], ins)
        xl = np.concatenate([np.asarray(r4[c]['xo']).reshape(2304, 1024)[:2048] for c in CORES], 0)
        xc = np.asarray(r4[0]['xo']).reshape(2304, 1024)[2048:]
    return np.ascontiguousarray(xl, dtype=np.float32)[None]


def build_cast():
    nc = bass.Bass("TRN2", target_bir_lowering=False)
    P = Prog(nc)
    wi = P.din("wi", [12, 512, 4096], F32)
    wo = P.dout("wo", [12, 512, 4096], BF16)
    NBUF = 4
    T = [P.sb(f"t{i}", [128, 4096], BF16) for i in range(NBUF)]
    bt = P.bufs(NBUF)
    outs = []
    n = 0
    for m in range(12):
        for r in range(4):
            i = n % NBUF; n += 1
            P.dma("pool", lambda e, i=i, m=m, r=r: e.dma_start(out=T[i][:], in_=wi[m, r * 128:(r + 1) * 128, :]), writes=[bt[i]])
            outs.append(P.dma("sp", lambda e, i=i, m=m, r=r: e.dma_start(out=wo[m, r * 128:(r + 1) * 128, :], in_=T[i][:]), reads=[bt[i]]))
    P.emit(final_wait_ops=outs)
    return nc


def run_cast(inp, progs):
    mats = []
    for layer in range(4):
        mats += [_wlayout(inp['w1'][layer], 8), _wlayout(inp['w3'][layer], 8), _wlayout(inp['w2'][layer], 4)]
    ins = [dict(wi=np.ascontiguousarray(np.stack([m[512 * c:512 * (c + 1)] for m in mats], 0))) for c in range(8)]
    res = run_bass_kernel_spmd(progs['cast'], ins, core_ids=list(range(8)))
    outs = [np.concatenate([np.asarray(res.results[c]['wo'])[m] for c in range(8)], 0) for m in range(12)]
    return [(outs[3 * l], outs[3 * l + 1], outs[3 * l + 2]) for l in range(4)]
```

```python
import numpy as np
import ml_dtypes
from contextlib import ExitStack
import concourse.bass as bass
import concourse.mybir as mybir
from concourse.bass_utils import run_bass_kernel_spmd

F32 = mybir.dt.float32
BF16 = mybir.dt.bfloat16
I32 = mybir.dt.int32
AF = mybir.ActivationFunctionType
ALU = mybir.AluOpType
AX = mybir.AxisListType
NPBF = ml_dtypes.bfloat16

COMPUTE = ("pe", "act", "dve", "pool")
QUEUES = ("sp", "act", "pool")


class Buf:
    __slots__ = ("name", "last_w", "readers")

    def __init__(self, name):
        self.name = name
        self.last_w = None
        self.readers = []


class Op:
    __slots__ = ("eng", "fn", "deps", "is_dma", "idx", "signal", "sigval", "dsem", "dval", "dprev")

    def __init__(self, eng, fn, is_dma):
        self.eng = eng
        self.fn = fn
        self.is_dma = is_dma
        self.deps = set()
        self.signal = False
        self.sigval = 0
        self.dsem = None
        self.dval = 0
        self.dprev = None


class Prog:
    def __init__(self, nc, n_dma_sems=6):
        self.nc = nc
        self.ops = []
        self.n_dma_sems = n_dma_sems
        self.es = ExitStack()
        self._nb = 0

    def buf(self, name=None):
        self._nb += 1
        return Buf(name or f"b{self._nb}")

    def bufs(self, n, name="b"):
        return [self.buf(f"{name}{i}") for i in range(n)]

    def sb(self, name, shape, dt):
        return self.es.enter_context(self.nc.sbuf_tensor(name, shape, dt))

    def ps(self, name, shape, dt):
        return self.es.enter_context(self.nc.psum_tensor(name, shape, dt))

    def din(self, name, shape, dt):
        return self.nc.dram_tensor(name, list(shape), dt, kind="ExternalInput").ap()

    def dout(self, name, shape, dt):
        return self.nc.dram_tensor(name, list(shape), dt, kind="ExternalOutput").ap()

    def op(self, eng, fn, reads=(), writes=(), dma=False):
        o = Op(eng, fn, dma)
        o.idx = len(self.ops)
        for b in reads:
            if b.last_w is not None:
                o.deps.add(b.last_w)
        for b in writes:
            if b.last_w is not None:
                o.deps.add(b.last_w)
            for r in b.readers:
                o.deps.add(r)
        for b in reads:
            b.readers.append(o.idx)
        for b in writes:
            b.last_w = o.idx
            b.readers = []
        o.deps.discard(o.idx)
        self.ops.append(o)
        return o

    def pe(self, fn, reads=(), writes=()):
        return self.op("pe", fn, reads, writes)

    def act(self, fn, reads=(), writes=()):
        return self.op("act", fn, reads, writes)

    def dve(self, fn, reads=(), writes=()):
        return self.op("dve", fn, reads, writes)

    def pool(self, fn, reads=(), writes=()):
        return self.op("pool", fn, reads, writes)

    def dma(self, q, fn, reads=(), writes=()):
        return self.op(q, fn, reads, writes, dma=True)

    def emit(self, final_wait_ops=None):
        nc = self.nc
        ops = self.ops
        for o in ops:
            if o.eng == "pe" and not o.is_dma:
                o.deps = {d for d in o.deps if not (ops[d].eng == "pe" and not ops[d].is_dma)}
        qcount = {q: 0 for q in QUEUES}
        qlast = {}
        for o in ops:
            if o.is_dma:
                slot = qcount[o.eng] % self.n_dma_sems
                qcount[o.eng] += 1
                key = (o.eng, slot)
                if key in qlast:
                    p = ops[qlast[key]]
                    o.dprev = p.idx
                    o.dval = p.dval + 16
                else:
                    o.dval = 16
                o.dsem = key
                qlast[key] = o.idx
        for o in ops:
            for d in o.deps:
                if not ops[d].is_dma:
                    ops[d].signal = True
        final = list(final_wait_ops or [])
        for d in final:
            if not d.is_dma:
                d.signal = True
        sigc = {e: 0 for e in COMPUTE}
        for o in ops:
            if not o.is_dma and o.signal:
                sigc[o.eng] += 1
                o.sigval = sigc[o.eng]
        sems = {}
        es = self.es
        for e in COMPUTE:
            sems[e] = es.enter_context(nc.semaphore("s_" + e))
        for q in QUEUES:
            for s in range(min(self.n_dma_sems, qcount[q])):
                sems[(q, s)] = es.enter_context(nc.semaphore(f"d_{q}{s}"))
        by_eng = {e: [] for e in ("pe", "act", "dve", "pool", "sp")}
        for o in ops:
            by_eng[o.eng].append(o)
        self.stats = {e: len(v) for e, v in by_eng.items()}

        def semkey_val(d):
            p = ops[d]
            if p.is_dma:
                return p.dsem, p.dval
            return p.eng, p.sigval

        def run_engine(ename, handle):
            waited = {}
            for o in by_eng[ename]:
                need = {}
                for d in o.deps:
                    k, v = semkey_val(d)
                    if need.get(k, 0) < v:
                        need[k] = v
                if o.is_dma and o.dprev is not None:
                    k, v = semkey_val(o.dprev)
                    if need.get(k, 0) < v:
                        need[k] = v
                for k, v in need.items():
                    if waited.get(k, 0) < v:
                        handle.wait_ge(sems[k], v)
                        waited[k] = v
                ins = o.fn(handle)
                if o.is_dma:
                    ins.then_inc(sems[o.dsem], 16)
                elif o.signal:
                    ins.then_inc(sems[o.eng], 1)
            if ename == "sp":
                for d in final:
                    k, v = semkey_val(d.idx)
                    if waited.get(k, 0) < v:
                        handle.wait_ge(sems[k], v)
                        waited[k] = v

        with nc.Block() as block:
            block.sync(lambda e: run_engine("sp", e))
            if by_eng["pe"]:
                block.tensor(lambda e: run_engine("pe", e))
            if by_eng["act"]:
                block.scalar(lambda e: run_engine("act", e))
            if by_eng["dve"]:
                block.vector(lambda e: run_engine("dve", e))
            if by_eng["pool"]:
                block.gpsimd(lambda e: run_engine("pool", e))
        self.es.close()


def make_ident(P, dt=BF16, name="ident"):
    idf = P.sb(name + "_f", [128, 128], F32)
    bf = P.buf()
    P.pool(lambda e: e.memset(idf[:], 0.0), writes=[bf])
    P.pool(lambda e: e.affine_select(out=idf[:], in_=idf[:], pattern=[[-1, 128]], compare_op=ALU.not_equal,
                                     fill=1.0, base=0, channel_multiplier=1), reads=[bf], writes=[bf])
    if dt == F32:
        return idf, bf
    idb = P.sb(name, [128, 128], dt)
    bb = P.buf()
    P.dve(lambda e: e.tensor_copy(out=idb[:], in_=idf[:]), reads=[bf], writes=[bb])
    return idb, bb

EPS = 1e-6


def build_e1():
    nc = bass.Bass("TRN2", target_bir_lowering=False)
    P = Prog(nc)
    xt = P.din("xt", [19, 128, 1024], F32)
    mcols = P.din("mcols", [128, 2, 2, 8], F32)
    win = P.din("win", [1024, 2048], F32)
    cwd = P.din("cw", [128, 3, 4], F32)
    cs128d = P.din("cs128", [128, 256], F32)
    c256d = P.din("c256", [128, 2, 256], F32)
    ns256d = P.din("ns256", [128, 2, 256], F32)
    flagsd = P.din("flags", [128, 2], F32)
    bout = P.dout("bout", [2048, 1024], BF16)
    ycT = P.dout("ycT", [512, 2304], BF16)
    fcT = P.dout("fcT", [512, 256], BF16)

    winb = P.sb("winb", [128, 8, 2048], BF16)
    mc = P.sb("mc", [128, 2, 2, 8], F32)
    cw = P.sb("cwt", [128, 3, 4], F32)
    cs128 = P.sb("cs128b", [128, 256], BF16)
    c256 = P.sb("c256b", [128, 2, 256], BF16)
    ns256 = P.sb("ns256b", [128, 2, 256], BF16)
    flags = P.sb("flagst", [128, 2], F32)
    XS = [P.sb(f"xs{i}", [128, 1024], F32) for i in range(2)]
    junk = P.sb("junk", [128, 1024], BF16)
    XN = [P.sb(f"xn{i}", [128, 1024], BF16) for i in range(2)]
    ss = P.sb("ss", [128, 19], F32)
    rstd = P.sb("rstd", [128, 19], F32)
    HT = [P.sb(f"hT{i}", [128, 8, 512], BF16) for i in range(2)]
    AT = [P.sb(f"aT{i}", [128, 4, 512], BF16) for i in range(2)]
    cgt = P.sb("cgt", [128, 4, 512], F32)
    uh = P.sb("uh", [128, 4, 2], F32)
    BgT = P.sb("BgT", [128, 4, 2304], BF16)
    uT = P.sb("uT", [128, 4, 2050], BF16)
    uTc = P.sb("uTc", [128, 4, 258], BF16)
    BT = [P.sb(f"bt{i}", [128, 1024], BF16) for i in range(2)]
    bctx = P.sb("bctx", [128, 2, 1024], BF16)
    fct = P.sb("fct", [128, 4, 256], BF16)
    yct = P.sb("yct", [128, 4, 2304], BF16)
    T1 = [P.sb(f"t1_{i}", [128, 1024], F32) for i in range(2)]
    T2 = [P.sb(f"t2_{i}", [128, 1024], F32) for i in range(2)]
    ident, b_ident = make_ident(P)
    epsc = P.sb("epsc", [128, 1], F32)
    b_eps = P.buf()
    P.pool(lambda e: e.memset(epsc[:], EPS), writes=[b_eps])

    PT = [P.ps(f"pT{i}", [128, 8, 128], BF16) for i in range(2)]
    PJ = [P.ps(f"pj{i}", [128, 512], F32) for i in range(3)]
    BS = P.ps("bs", [128, 4, 256], F32)
    PF = P.ps("pf", [128, 512], F32)

    b_win, b_mc, b_cw, b_cs, b_c256, b_ns256, b_flags = P.bufs(7, "c")
    b_xs = P.bufs(2, "xs"); b_junk = P.buf(); b_xn = P.bufs(2, "xn")
    b_ss = P.bufs(19, "ss"); b_rstd = P.bufs(19, "rs")
    b_ht = P.bufs(2, "ht"); b_at = P.bufs(2, "at"); b_cgt = P.bufs(4, "cgt"); b_uh = P.buf()
    b_bg = P.bufs(6, "bg"); b_u = P.bufs(7, "u"); b_bt = P.bufs(2, "bt"); b_bctx = P.bufs(2, "bctx")
    b_fct = P.bufs(4, "fct"); b_yct = P.bufs(8, "yct"); b_t1 = P.bufs(2, "t1"); b_t2 = P.bufs(2, "t2")
    b_pt = P.bufs(2, "pt"); b_pj = P.bufs(3, "pj"); b_bs = P.buf(); b_pf = P.buf()
    b_uz = P.buf()

    P.dma("pool", lambda e: e.dma_start(out=winb[:], in_=win.rearrange("(k p) n -> p k n", p=128)), writes=[b_win])
    P.dma("sp", lambda e: e.dma_start(out=mc[:], in_=mcols), writes=[b_mc])
    P.dma("sp", lambda e: e.dma_start(out=cw[:], in_=cwd), writes=[b_cw])
    P.dma("sp", lambda e: e.dma_start(out=flags[:], in_=flagsd), writes=[b_flags])
    P.dma("pool", lambda e: e.dma_start(out=cs128[:], in_=cs128d), writes=[b_cs])
    P.dma("pool", lambda e: e.dma_start(out=c256[:], in_=c256d), writes=[b_c256])
    P.dma("pool", lambda e: e.dma_start(out=ns256[:], in_=ns256d), writes=[b_ns256])
    P.pool(lambda e: e.memset(uTc[:], 0.0), writes=[b_uz])

    groups = [([18], 0, "halo")] + [([4 * g + i for i in range(4)], 0, "lat") for g in range(4)] + [([16, 17], 1, "ctx")]
    outs = []
    ti = 0
    pj_i = 0
    for gi, (tiles, s, kind) in enumerate(groups):
        ncols = 128 * len(tiles)
        hT = HT[gi % 2]; bht = b_ht[gi % 2]
        aT = AT[gi % 2]; bat = b_at[gi % 2]
        for tt, t in enumerate(tiles):
            xs = XS[ti % 2]; bxs = b_xs[ti % 2]
            xn = XN[ti % 2]; bxn = b_xn[ti % 2]
            pT = PT[ti % 2]; bpt = b_pt[ti % 2]
            P.dma("sp", lambda e, xs=xs, t=t: e.dma_start(out=xs[:], in_=xt[t]), writes=[bxs])
            P.act(lambda e, xs=xs, t=t: e.activation(out=junk[:], in_=xs[:], func=AF.Square, accum_out=ss[:, t:t + 1]),
                  reads=[bxs], writes=[b_junk, b_ss[t]])
            P.act(lambda e, t=t: e.activation(out=rstd[:, t:t + 1], in_=ss[:, t:t + 1], func=AF.Sqrt, scale=1.0 / 1024, bias=epsc[:, 0:1]),
                  reads=[b_ss[t], b_eps], writes=[b_rstd[t]])
            P.dve(lambda e, t=t: e.reciprocal(out=rstd[:, t:t + 1], in_=rstd[:, t:t + 1]),
                  reads=[b_rstd[t]], writes=[b_rstd[t]])
            P.dve(lambda e, xs=xs, xn=xn, t=t: e.tensor_scalar(out=xn[:], in0=xs[:], scalar1=rstd[:, t:t + 1], scalar2=None,
                                                               op0=ALU.mult), reads=[bxs, b_rstd[t]], writes=[bxn])
            for k in range(8):
                P.pe(lambda e, k=k, xn=xn, pT=pT: e.transpose(out=pT[:, k, :], in_=xn[:, k * 128:(k + 1) * 128], identity=ident[:]),
                     reads=[bxn, b_ident], writes=[bpt])
            for k in range(8):
                dst = hT[:, k, tt * 128:(tt + 1) * 128]
                if k % 2 == 0:
                    P.act(lambda e, dst=dst, pT=pT, k=k, s=s: e.activation(out=dst, in_=pT[:, k, :], func=AF.Identity,
                                                                          scale=mc[:, s, 1, k:k + 1], bias=mc[:, s, 0, k:k + 1]),
                          reads=[bpt, b_mc], writes=[bht])
                else:
                    P.dve(lambda e, dst=dst, pT=pT, k=k, s=s: e.tensor_scalar(out=dst, in0=pT[:, k, :], scalar1=mc[:, s, 1, k:k + 1],
                                                                             scalar2=mc[:, s, 0, k:k + 1], op0=ALU.mult, op1=ALU.add),
                          reads=[bpt, b_mc], writes=[bht])
            ti += 1
        if kind == "lat":
            tok0 = tiles[0] * 128
            bbg = b_bg[gi - 1]; bu = b_u[gi - 1]
        elif kind == "ctx":
            tok0 = 0
            bbg = b_bg[4]; bu = b_u[4]
        nlist = range(16) if kind != "halo" else range(8, 16)
        for n in nlist:
            pj = PJ[pj_i % 3]; bpj = b_pj[pj_i % 3]; pj_i += 1
            for k in range(8):
                P.pe(lambda e, pj=pj, k=k, n=n, hT=hT, ncols=ncols: e.matmul(pj[:, 0:ncols], lhsT=winb[:, k, n * 128:(n + 1) * 128],
                                                                             rhs=hT[:, k, 0:ncols], start=(k == 0), stop=(k == 7)),
                     reads=[bht, b_win], writes=[bpj])
            if n < 4:
                P.act(lambda e, pj=pj, n=n, aT=aT, ncols=ncols: e.activation(out=aT[:, n, 0:ncols], in_=pj[:, 0:ncols], func=AF.Copy),
                      reads=[bpj], writes=[bat])
            elif n < 8:
                j = n - 4
                if kind == "lat":
                    dst = BgT[:, j, tok0:tok0 + ncols]
                else:
                    dst = BgT[:, j, 2048:2304]
                P.dve(lambda e, pj=pj, dst=dst, ncols=ncols: e.tensor_copy(out=dst, in_=pj[:, 0:ncols]), reads=[bpj], writes=[bbg])
            elif n < 12:
                j = n - 8
                P.act(lambda e, pj=pj, j=j, ncols=ncols: e.activation(out=cgt[:, j, 0:ncols], in_=pj[:, 0:ncols], func=AF.Copy),
                      reads=[bpj], writes=[b_cgt[j]])
            else:
                j = n - 12
                if kind == "halo":
                    P.dve(lambda e, pj=pj, j=j: e.tensor_tensor(out=uh[:, j, :], in0=pj[:, 0:2], in1=cgt[:, j, 0:2], op=ALU.mult),
                          reads=[bpj, b_cgt[j]], writes=[b_uh])
                    P.dve(lambda e, j=j: e.tensor_scalar(out=uT[:, j, 0:1], in0=uh[:, j, 0:1], scalar1=flags[:, 0:1], scalar2=None,
                                                         op0=ALU.mult), reads=[b_uh, b_flags], writes=[b_u[5]])
                    P.dve(lambda e, j=j: e.tensor_scalar(out=uT[:, j, 2049:2050], in0=uh[:, j, 1:2], scalar1=flags[:, 1:2], scalar2=None,
                                                         op0=ALU.mult), reads=[b_uh, b_flags], writes=[b_u[6]])
                else:
                    if kind == "lat":
                        dst = uT[:, j, 1 + tok0:1 + tok0 + ncols]
                        wr = [bu]
                    else:
                        dst = uTc[:, j, 1:257]
                        wr = [bu]

                    rd = [bpj, b_cgt[j]] + ([b_uz] if kind == "ctx" else [])
                    P.dve(lambda e, pj=pj, j=j, dst=dst, ncols=ncols: e.tensor_tensor(out=dst, in0=pj[:, 0:ncols], in1=cgt[:, j, 0:ncols],
                                                                                      op=ALU.mult), reads=rd, writes=wr)
        if kind == "halo":
            continue
        for tt, t in enumerate(tiles):
            for g in range(4):
                P.pe(lambda e, g=g, aT=aT, tt=tt: e.matmul(BS[:, g, :], lhsT=aT[:, g, tt * 128:(tt + 1) * 128], rhs=cs128[:, :],
                                                           start=True, stop=True), reads=[bat, b_cs], writes=[b_bs])
            if kind == "lat":
                bt = BT[t % 2]; bbt = b_bt[t % 2]
                P.act(lambda e, bt=bt: e.activation(out=bt[:], in_=BS[:].rearrange("p g c -> p (g c)"), func=AF.Copy),
                      reads=[b_bs], writes=[bbt])
                outs.append(P.dma("sp", lambda e, bt=bt, t=t: e.dma_start(out=bout[t * 128:(t + 1) * 128, :], in_=bt[:]), reads=[bbt]))
            else:
                P.act(lambda e, tt=tt: e.activation(out=bctx[:, tt, :], in_=BS[:].rearrange("p g c -> p (g c)"), func=AF.Copy),
                      reads=[b_bs], writes=[b_bctx[tt]])
        if kind == "ctx":
            for g in range(4):
                for tt in range(2):
                    P.pe(lambda e, g=g, tt=tt: e.matmul(PF[:, 0:256], lhsT=bctx[:, tt, g * 256:g * 256 + 128], rhs=c256[:, tt, :],
                                                        start=(tt == 0), stop=False), reads=[b_bctx[tt], b_c256], writes=[b_pf])
                    P.pe(lambda e, g=g, tt=tt: e.matmul(PF[:, 0:256], lhsT=bctx[:, tt, g * 256 + 128:g * 256 + 256], rhs=ns256[:, tt, :],
                                                        start=False, stop=(tt == 1)), reads=[b_bctx[tt], b_ns256], writes=[b_pf])
                P.act(lambda e, g=g: e.activation(out=fct[:, g, :], in_=PF[:, 0:256], func=AF.Copy), reads=[b_pf], writes=[b_fct[g]])
                outs.append(P.dma("sp", lambda e, g=g: e.dma_start(out=fcT[g * 128:(g + 1) * 128, :], in_=fct[:, g, :]), reads=[b_fct[g]]))

    allu = b_u[0:4] + [b_u[5], b_u[6]]
    allbg = b_bg[0:4]
    segs = [("lat", 0, 1024), ("lat", 1024, 1024), ("ctx", 0, 256)]
    ci = 0
    for j in range(4):
        for (kind, c0, w) in segs:
            en = "dve"
            t1 = T1[ci % 2]; t2 = T2[ci % 2]; bt1 = b_t1[ci % 2]; bt2 = b_t2[ci % 2]
            ci += 1
            if kind == "lat":
                src = uT; off = c0; ub = allu; bgs = BgT[:, j, c0:c0 + w]; bgb = allbg; ydst = yct[:, j, c0:c0 + w]
                byc = b_yct[j * 2 + (c0 // 1024)]
            else:
                src = uTc; off = 0; ub = [b_u[4], b_uz]; bgs = BgT[:, j, 2048:2304]; bgb = [b_bg[4]]; ydst = yct[:, j, 2048:2304]
                byc = b_yct[j * 2]
                byc = P.buf()
            P.op(en, lambda e, t1=t1, src=src, off=off, w=w, j=j: e.tensor_scalar(out=t1[:, 0:w], in0=src[:, j, off + 1:off + 1 + w],
                                                                                  scalar1=cw[:, 1, j:j + 1], scalar2=None, op0=ALU.mult),
                 reads=ub + [b_cw], writes=[bt1])
            P.op(en, lambda e, t1=t1, t2=t2, src=src, off=off, w=w, j=j: e.scalar_tensor_tensor(
                out=t2[:, 0:w], in0=src[:, j, off:off + w], scalar=cw[:, 0, j:j + 1], in1=t1[:, 0:w], op0=ALU.mult, op1=ALU.add),
                 reads=ub + [b_cw, bt1], writes=[bt2])
            P.op(en, lambda e, t1=t1, t2=t2, src=src, off=off, w=w, j=j: e.scalar_tensor_tensor(
                out=t1[:, 0:w], in0=src[:, j, off + 2:off + 2 + w], scalar=cw[:, 2, j:j + 1], in1=t2[:, 0:w], op0=ALU.mult, op1=ALU.add),
                 reads=ub + [b_cw, bt2], writes=[bt1])
            P.op(en, lambda e, t1=t1, w=w, bgs=bgs, ydst=ydst: e.tensor_tensor(out=ydst, in0=t1[:, 0:w], in1=bgs, op=ALU.mult),
                 reads=[bt1] + bgb, writes=[byc])
            if kind == "lat":
                outs.append(P.dma("sp", lambda e, j=j, c0=c0, w=w: e.dma_start(out=ycT[j * 128:(j + 1) * 128, c0:c0 + w],
                                                                            in_=yct[:, j, c0:c0 + w]), reads=[byc]))
            else:
                outs.append(P.dma("sp", lambda e, j=j: e.dma_start(out=ycT[j * 128:(j + 1) * 128, 2048:2304], in_=yct[:, j, 2048:2304]),
                                  reads=[byc]))
    P.emit(final_wait_ops=outs)
    return nc


def dft_consts():
    n = np.arange(128)
    ang = 2 * np.pi * np.outer(n, n) / 128
    C = np.cos(ang); S = np.sin(ang)
    cs128 = np.concatenate([C, S], 1) / np.sqrt(128)
    t = np.arange(256)
    a256 = 2 * np.pi * np.outer(t, t) / 256
    c256 = (np.cos(a256) / 16).reshape(2, 128, 256).transpose(1, 0, 2)
    ns256 = (-np.sin(a256) / 16).reshape(2, 128, 256).transpose(1, 0, 2)
    return dict(cs128=cs128.astype(np.float32), c256=np.ascontiguousarray(c256).astype(np.float32),
                ns256=np.ascontiguousarray(ns256).astype(np.float32), C=C, S=S)


def build_e2():
    nc = bass.Bass("TRN2", target_bir_lowering=False)
    P = Prog(nc)
    zin = P.din("zin", [128, 2, 64, 128], BF16)
    m1d = P.din("m1", [128, 256], F32)
    m2d = P.din("m2", [128, 256], F32)
    twd = P.din("tw", [128, 2, 128], F32)
    c2d = P.din("c2", [128, 2, 128], F32)
    FT = P.dout("FT", [64, 128, 128], BF16)

    z = P.sb("z", [128, 2, 64, 128], BF16)
    m1 = P.sb("m1b", [128, 256], BF16)
    m2 = P.sb("m2b", [128, 256], BF16)
    tw = P.sb("twt", [128, 2, 128], F32)
    c2 = P.sb("c2b", [128, 2, 128], BF16)
    YS = [P.sb(f"ys{i}", [128, 4, 256], F32) for i in range(2)]
    TA = [P.sb(f"ta{i}", [128, 4, 128], F32) for i in range(2)]
    TB = [P.sb(f"tb{i}", [128, 4, 128], F32) for i in range(2)]
    TC = [P.sb(f"tc{i}", [128, 4, 128], F32) for i in range(2)]
    TD = [P.sb(f"td{i}", [128, 4, 128], F32) for i in range(2)]
    ZC = [P.sb(f"zc{i}", [128, 4, 128], BF16) for i in range(2)]
    ZS = [P.sb(f"zs{i}", [128, 4, 128], BF16) for i in range(2)]
    GS = [P.sb(f"gs{i}", [128, 4, 128], BF16) for i in range(2)]
    PY = [P.ps(f"py{i}", [128, 4, 256], F32) for i in range(2)]
    PG = [P.ps(f"pg{i}", [128, 4, 128], F32) for i in range(2)]

    b_z, b_m1, b_m2, b_tw, b_c2 = P.bufs(5, "c")
    b_ys = P.bufs(2); b_ta = P.bufs(2); b_tb = P.bufs(2); b_tc = P.bufs(2); b_td = P.bufs(2)
    b_zc = P.bufs(2); b_zs = P.bufs(2); b_gs = P.bufs(2); b_py = P.bufs(2); b_pg = P.bufs(2)

    P.dma("sp", lambda e: e.dma_start(out=z[:], in_=zin), writes=[b_z])
    P.dma("pool", lambda e: e.dma_start(out=m1[:], in_=m1d), writes=[b_m1])
    P.dma("pool", lambda e: e.dma_start(out=m2[:], in_=m2d), writes=[b_m2])
    P.dma("pool", lambda e: e.dma_start(out=c2[:], in_=c2d), writes=[b_c2])
    P.dma("sp", lambda e: e.dma_start(out=tw[:], in_=twd), writes=[b_tw])
    outs = []
    for st in range(16):
        i = st % 2
        py = PY[i]; ys = YS[i]; ta = TA[i]; tb = TB[i]; tc_ = TC[i]; td = TD[i]; zc = ZC[i]; zs = ZS[i]; gs = GS[i]; pg = PG[i]
        for q in range(4):
            n = st * 4 + q
            P.pe(lambda e, py=py, q=q, n=n: e.matmul(py[:, q, :], lhsT=z[:, 0, n, :], rhs=m1[:, :], start=True, stop=False),
                 reads=[b_z, b_m1], writes=[b_py[i]])
            P.pe(lambda e, py=py, q=q, n=n: e.matmul(py[:, q, :], lhsT=z[:, 1, n, :], rhs=m2[:, :], start=False, stop=True),
                 reads=[b_z, b_m2], writes=[b_py[i]])
        P.act(lambda e, py=py, ys=ys: e.activation(out=ys[:], in_=py[:], func=AF.Copy), reads=[b_py[i]], writes=[b_ys[i]])
        tcos = tw[:, 0:1, :].to_broadcast([128, 4, 128])
        tsin = tw[:, 1:2, :].to_broadcast([128, 4, 128])
        P.dve(lambda e, ys=ys, ta=ta, tcos=tcos: e.tensor_tensor(out=ta[:], in0=ys[:, :, 0:128], in1=tcos, op=ALU.mult),
              reads=[b_ys[i], b_tw], writes=[b_ta[i]])
        P.dve(lambda e, ys=ys, tb=tb, tsin=tsin: e.tensor_tensor(out=tb[:], in0=ys[:, :, 128:256], in1=tsin, op=ALU.mult),
              reads=[b_ys[i], b_tw], writes=[b_tb[i]])
        P.dve(lambda e, ta=ta, tb=tb, zc=zc: e.tensor_tensor(out=zc[:], in0=ta[:], in1=tb[:], op=ALU.subtract),
              reads=[b_ta[i], b_tb[i]], writes=[b_zc[i]])
        P.pool(lambda e, ys=ys, tc_=tc_, tsin=tsin: e.tensor_tensor(out=tc_[:], in0=ys[:, :, 0:128], in1=tsin, op=ALU.mult),
               reads=[b_ys[i], b_tw], writes=[b_tc[i]])
        P.pool(lambda e, ys=ys, td=td, tcos=tcos: e.tensor_tensor(out=td[:], in0=ys[:, :, 128:256], in1=tcos, op=ALU.mult),
               reads=[b_ys[i], b_tw], writes=[b_td[i]])
        P.pool(lambda e, tc_=tc_, td=td, zs=zs: e.tensor_tensor(out=zs[:], in0=tc_[:], in1=td[:], op=ALU.add),
               reads=[b_tc[i], b_td[i]], writes=[b_zs[i]])
        P.pe(lambda e, pg=pg, zc=zc: e.matmul(pg[:].rearrange("p a b -> p (a b)"), lhsT=c2[:, 0, :], rhs=zc[:].rearrange("p a b -> p (a b)"),
                                              start=True, stop=False), reads=[b_zc[i], b_c2], writes=[b_pg[i]])
        P.pe(lambda e, pg=pg, zs=zs: e.matmul(pg[:].rearrange("p a b -> p (a b)"), lhsT=c2[:, 1, :], rhs=zs[:].rearrange("p a b -> p (a b)"),
                                              start=False, stop=True), reads=[b_zs[i], b_c2], writes=[b_pg[i]])
        P.act(lambda e, pg=pg, gs=gs: e.activation(out=gs[:], in_=pg[:], func=AF.Copy), reads=[b_pg[i]], writes=[b_gs[i]])
        outs.append(P.dma("sp", lambda e, gs=gs, st=st: e.dma_start(out=FT[st * 4:(st + 1) * 4].rearrange("n a b -> a n b"), in_=gs[:]),
                          reads=[b_gs[i]]))
    P.emit(final_wait_ops=outs)
    return nc


def fft_consts():
    n = np.arange(128)
    ang = 2 * np.pi * np.outer(n, n) / 128
    C = np.cos(ang) / np.sqrt(128); S = np.sin(ang) / np.sqrt(128)
    m1 = np.concatenate([C, S], 1); m2 = np.concatenate([-S, C], 1)
    ta = 2 * np.pi * np.outer(n, n) / 16384
    tw = np.stack([np.cos(ta), np.sin(ta)], 1)
    c2 = np.stack([C, -S], 1)
    f = lambda a: np.ascontiguousarray(a).astype(np.float32)
    return dict(m1=f(m1), m2=f(m2), tw=f(tw), c2=f(c2))


def build_e3():
    nc = bass.Bass("TRN2", target_bir_lowering=False)
    P = Prog(nc)
    yT = P.din("yT", [1024, 2304], BF16)
    xt = P.din("xt", [18, 128, 1024], F32)
    g1d = P.din("g1", [128, 2, 1024], F32)
    wd = P.din("w", [1024, 1024], F32)
    xo = P.dout("xo", [18, 128, 1024], F32)
    ysb = P.sb("ysb", [128, 8, 2304], BF16)
    wb = P.sb("wb", [128, 8, 1024], BF16)
    g1 = P.sb("g1t", [128, 2, 1024], F32)
    XS = [P.sb(f"xs{i}", [128, 1024], F32) for i in range(2)]
    TM = [P.sb(f"tm{i}", [128, 1024], F32) for i in range(2)]
    PY = [P.ps(f"py{i}", [128, 1024], F32) for i in range(2)]
    b_y, b_w, b_g = P.bufs(3); b_xs = P.bufs(2); b_tm = P.bufs(2); b_py = P.bufs(2)
    P.dma("sp", lambda e: e.dma_start(out=ysb[:], in_=yT.rearrange("(k p) t -> p k t", p=128)), writes=[b_y])
    P.dma("pool", lambda e: e.dma_start(out=wb[:], in_=wd.rearrange("(k p) n -> p k n", p=128)), writes=[b_w])
    P.dma("sp", lambda e: e.dma_start(out=g1[:], in_=g1d), writes=[b_g])
    outs = []
    for t in range(18):
        i = t % 2
        s = 0 if t < 16 else 1
        xs = XS[i]; tm = TM[i]; py = PY[i]
        P.dma("sp", lambda e, xs=xs, t=t: e.dma_start(out=xs[:], in_=xt[t]), writes=[b_xs[i]])
        for h in range(2):
            for k in range(8):
                P.pe(lambda e, py=py, h=h, k=k, t=t: e.matmul(py[:, h * 512:(h + 1) * 512], lhsT=ysb[:, k, t * 128:(t + 1) * 128],
                                                             rhs=wb[:, k, h * 512:(h + 1) * 512], start=(k == 0), stop=(k == 7)),
                     reads=[b_y, b_w], writes=[b_py[i]])
        P.dve(lambda e, py=py, tm=tm, s=s: e.tensor_tensor(out=tm[:], in0=py[:], in1=g1[:, s, :], op=ALU.mult),
              reads=[b_py[i], b_g], writes=[b_tm[i]])
        P.dve(lambda e, tm=tm, xs=xs: e.tensor_tensor(out=tm[:], in0=tm[:], in1=xs[:], op=ALU.add),
              reads=[b_tm[i], b_xs[i]], writes=[b_tm[i]])
        outs.append(P.dma("sp", lambda e, tm=tm, t=t: e.dma_start(out=xo[t], in_=tm[:]), reads=[b_tm[i]]))
    P.emit(final_wait_ops=outs)
    return nc


NB = 68


def build_moe2():
    nc = bass.Bass("TRN2", target_bir_lowering=False)
    P = Prog(nc)
    xt = P.din("xt", [18, 128, 1024], F32)
    rowsd = P.din("rows", [2, 3, 128, 1024], F32)
    wrd = P.din("wr", [1024, 36], F32)
    brd = P.din("br", [128, 36], F32)
    w1L = P.din("w1L", [4096, 4096], BF16)
    w3L = P.din("w3L", [4096, 4096], BF16)
    w2L = P.din("w2L", [4096, 4096], BF16)
    bvd = P.din("bvals", [128, NB * 32], F32)
    thrd = P.din("thr", [128, 32 * 18], F32)
    pcd = P.din("pcol", [128, 1], F32)
    tokd = P.din("tokid", [128, 18, 16], I32)
    xo = P.dout("xo", [18, 128, 1024], F32)
    Hd = nc.dram_tensor("Hd", [2305, 1024], BF16).ap()
    table = nc.dram_tensor("table", [(NB + 1) * 128, 16], I32).ap()
    Yd = nc.dram_tensor("Yd", [(NB + 1) * 128, 1024], BF16).ap()

    WG1 = [P.sb("wg1", [128, 4096], BF16)] * 2
    WG3 = [P.sb("wg3", [128, 4096], BF16)] * 2
    WG2 = [P.sb("wg2", [128, 4096], BF16)] * 2
    W1 = [P.sb(f"w1b{i}", [128, 8, 512], BF16) for i in range(2)]
    W3 = [P.sb(f"w3b{i}", [128, 8, 512], BF16) for i in range(2)]
    W2 = [P.sb(f"w2b{i}", [128, 4, 1024], BF16) for i in range(2)]
    rows = P.sb("rowst", [128, 2, 1024], F32)
    XS = [P.sb(f"xs{i}", [128, 1024], F32) for i in range(2)]
    H32 = [P.sb(f"h32_{i}", [128, 1024], F32) for i in range(2)]
    wrt = P.sb("wrt", [128, 8, 36], F32)
    brt = P.sb("brt", [128, 36], F32)
    wrh = P.sb("wrh", [128, 8, 36], BF16)
    wrl = P.sb("wrl", [128, 8, 36], BF16)
    HHI = [P.sb(f"hhi{i}", [128, 1024], BF16) for i in range(2)]
    HLO = [P.sb(f"hlo{i}", [128, 1024], BF16) for i in range(2)]
    HTHI = [P.sb(f"hThi{i}", [128, 8, 128], BF16) for i in range(2)]
    HTLO = [P.sb(f"hTlo{i}", [128, 8, 128], BF16) for i in range(2)]
    identb = P.sb("identb", [128, 128], BF16)
    onesb = P.sb("onesb", [128, 128], BF16)
    onesf = P.sb("onesf", [128, 128], F32)
    Ub = P.sb("Ub", [128, 128], BF16)
    SM = [P.sb(f"sm{i}", [128, 64], F32) for i in range(2)]
    LG = [P.sb(f"lg{i}", [128, 36], F32) for i in range(2)]
    ss = P.sb("ss", [128, 18], F32)
    rstd = P.sb("rstd", [128, 18], F32)
    epsc = P.sb("epsc", [128, 1], F32)
    M12 = P.sb("M12", [128, 18, 2, 32], BF16)
    Msum = P.sb("Msum", [128, 18, 32], BF16)
    g12 = P.sb("g12", [128, 18, 2], F32)
    LGA = P.sb("LGA", [128, 18, 36], F32)
    bvals = P.sb("bvalst", [128, NB, 32], F32)
    thr = P.sb("thrt", [128, 32, 18], F32)
    pcol = P.sb("pcolt", [128, 1], F32)
    tokid = P.sb("tokidt", [128, 18, 16], I32)
    tinit = P.sb("tinit", [128, NB + 1, 16], I32)
    zrow = P.sb("zrow", [1, 1024], BF16)
    cnt = P.sb("cnt", [128, 32], F32)
    big = P.sb("big", [128, NB * 32], F32)
    nblk = P.sb("nblk", [128, 32], F32)
    padded = P.sb("padded", [128, 32], F32)
    ca = P.sb("ca", [128, 32], F32)
    cb = P.sb("cb", [128, 32], F32)
    pstart = P.sb("pstart", [128, 32], F32)
    bexp = P.sb("bexp", [128, NB], F32)
    same = P.sb("same", [128, NB], F32)
    widx = P.sb("widx", [128, NB], I32)
    carry = P.sb("carry", [128, 32], F32)
    tA = P.sb("tA", [128, 32], F32)
    tB = P.sb("tB", [128, 32], F32)
    destf = P.sb("destf", [128, 18, 2], F32)
    desti = P.sb("desti", [128, 18, 2], I32)
    IDX = [P.sb(f"idx{i}", [128, 16], I32) for i in range(2)]
    XG = [P.sb(f"xg{i}", [128, 1024], BF16) for i in range(2)]
    XGT = [P.sb(f"XgT{i}", [128, 8, 128], BF16) for i in range(2)]
    SSB = [P.sb(f"ssb{i}", [128, 512], F32) for i in range(2)]
    HB = [P.sb(f"hb{i}", [128, 512], BF16) for i in range(2)]
    HTT = [P.sb(f"hT{i}", [128, 4, 128], BF16) for i in range(2)]
    YS = [P.sb(f"ys{i}", [128, 1024], BF16) for i in range(2)]
    YY1 = [P.sb(f"Y1_{i}", [128, 1024], BF16) for i in range(2)]
    YY2 = [P.sb(f"Y2_{i}", [128, 1024], BF16) for i in range(2)]
    identf, b_identf = make_ident(P, F32, "identf")

    PTB = P.ps("ptb", [128, 8, 128], BF16)
    PRG = P.ps("prg", [128, 512], F32)
    PRK = P.ps("prk", [128, 512], F32)
    PH1 = [P.ps("ph1", [128, 512], F32), PRG]
    PH3 = [P.ps("ph3", [128, 512], F32), PRK]
    PTH2 = P.ps("pth", [128, 2, 4, 128], BF16)
    PY = P.ps("py", [128, 1024], F32)

    b_wg1 = [P.buf()] * 2; b_wg3 = [P.buf()] * 2; b_wg2 = [P.buf()] * 2; b_w1 = P.bufs(2); b_w3 = P.bufs(2); b_w2 = P.bufs(2)
    b_rows = P.buf(); B_xs = P.bufs(2); B_h32 = P.bufs(2)
    b_wr = P.buf(); b_br = P.buf(); B_sm = P.bufs(2); B_lg = P.bufs(2); b_lga = P.bufs(18)
    b_ss = P.bufs(18); b_rstd = P.bufs(18); b_eps = P.buf()
    b_ptb = P.buf(); B_hhi = P.bufs(2); B_hlo = P.bufs(2); B_hthi = P.bufs(2); B_htlo = P.bufs(2); b_wrh = P.buf(); b_wrl = P.buf(); b_idb = P.buf()
    b_prg = P.buf(); b_prk = P.buf(); B_ph1 = [P.buf(), b_prg]; B_ph3 = [P.buf(), b_prk]; B_pth = P.bufs(2); b_py = P.buf()
    b_ones = P.buf(); b_U = P.buf(); b_M = P.bufs(18); b_ms = P.bufs(18); b_g12 = P.bufs(18)
    b_bv, b_thr, b_pc, b_tok, b_tinit, b_zrow, b_tab0, b_hdz = P.bufs(8)
    b_hd = P.bufs(18); b_cnt = P.buf(); b_big = P.buf(); b_nblk = P.buf(); b_pad = P.buf(); b_ca = P.buf(); b_cb = P.buf()
    b_ps = P.buf(); b_same = P.buf(); b_bexp = P.buf(); b_widx = P.buf(); b_carry = P.buf(); b_tA = P.buf(); b_tB = P.buf()
    b_destf = P.bufs(18); b_desti = P.buf(); b_sc = P.bufs(36)
    b_idx = P.bufs(2); b_xg = P.bufs(2); B_xgt = P.bufs(2); B_ssb = P.bufs(2); B_hb = P.bufs(2); B_hT = P.bufs(2); b_ys = P.bufs(2)
    b_yd = P.bufs(NB); B_y1 = P.bufs(2); B_y2 = P.bufs(2)

    regs = {}

    def _mkreg(e):
        regs['b'] = e.alloc_register("bnd")
        return e.reg_mov(regs['b'], 4095)
    P.pool(_mkreg)
    P.pool(lambda e: e.memset(epsc[:], EPS), writes=[b_eps])
    P.pool(lambda e: e.memset(onesf[:], 1.0), writes=[b_ones])
    P.pool(lambda e: e.affine_select(out=onesf[:], in_=onesf[:], pattern=[[1, 128]], compare_op=ALU.is_gt, fill=0.0, base=0,
                                     channel_multiplier=-1), reads=[b_ones], writes=[b_ones])
    P.dve(lambda e: e.tensor_copy(out=Ub[:], in_=onesf[:]), reads=[b_ones], writes=[b_U])
    P.pool(lambda e: e.memset(onesb[:], 1.0), writes=[b_U])
    P.pool(lambda e: e.memset(tinit[:], 2304), writes=[b_tinit])
    P.pool(lambda e: e.memset(zrow[:], 0.0), writes=[b_zrow])
    P.pool(lambda e: e.memset(carry[:], 0.0), writes=[b_carry])
    P.dma("sp", lambda e: e.dma_start(out=wrt[:], in_=wrd.rearrange("(k p) n -> p k n", p=128)), writes=[b_wr])
    P.dma("sp", lambda e: e.dma_start(out=brt[:], in_=brd), writes=[b_br])
    P.dma("sp", lambda e: e.dma_start(out=bvals[:].rearrange("p b e -> p (b e)"), in_=bvd), writes=[b_bv])
    P.dma("sp", lambda e: e.dma_start(out=thr[:].rearrange("p e j -> p (e j)"), in_=thrd), writes=[b_thr])
    P.dma("sp", lambda e: e.dma_start(out=pcol[:], in_=pcd), writes=[b_pc])
    P.dma("sp", lambda e: e.dma_start(out=tokid[:], in_=tokd), writes=[b_tok])
    P.dma("sp", lambda e: e.dma_start(out=table.rearrange("(b p) c -> p b c", p=128), in_=tinit[:]), reads=[b_tinit], writes=[b_tab0])
    P.dma("sp", lambda e: e.dma_start(out=Hd[2304:2305, :], in_=zrow[:]), reads=[b_zrow], writes=[b_hdz])
    P.dve(lambda e: e.tensor_copy(out=identb[:], in_=identf[:]), reads=[b_identf], writes=[b_idb])
    P.dve(lambda e: e.tensor_copy(out=wrh[:], in_=wrt[:]), reads=[b_wr], writes=[b_wrh])
    P.dve(lambda e: e.tensor_tensor(out=wrl[:], in0=wrt[:], in1=wrh[:], op=ALU.subtract), reads=[b_wr, b_wrh], writes=[b_wrl])

    def phase_a(t, xs, h32, hhi, hlo, hThi, hTlo, sm, lg, b_xs, b_h32, b_hhi, b_hlo, b_hthi, b_htlo, b_sm, b_lg):
        junk = hlo; b_junk = b_hlo

        def sc(i):
            return sm[:, i:i + 1]
        s = 0 if t < 16 else 1
        if t == 0 or t == 16:
            P.dma("sp", lambda e, s=s: e.dma_start(out=rows[:, 0, :], in_=rowsd[s, 0]), writes=[b_rows])
            P.dma("sp", lambda e, s=s: e.dma_start(out=rows[:, 1, :], in_=rowsd[s, 1]), writes=[b_rows])
        P.dma("sp", lambda e, t=t: e.dma_start(out=xs[:], in_=xt[t]), writes=[b_xs])
        P.act(lambda e, t=t: e.activation(out=junk[:], in_=xs[:], func=AF.Square, accum_out=ss[:, t:t + 1]),
              reads=[b_xs], writes=[b_junk, b_ss[t]])
        P.act(lambda e, t=t: e.activation(out=rstd[:, t:t + 1], in_=ss[:, t:t + 1], func=AF.Sqrt, scale=1.0 / 1024, bias=epsc[:, 0:1]),
              reads=[b_ss[t], b_eps], writes=[b_rstd[t]])
        P.dve(lambda e, t=t: e.reciprocal(out=rstd[:, t:t + 1], in_=rstd[:, t:t + 1]), reads=[b_rstd[t]], writes=[b_rstd[t]])
        P.dve(lambda e, t=t: e.scalar_tensor_tensor(out=h32[:], in0=xs[:], scalar=rstd[:, t:t + 1], in1=rows[:, 0, :],
                                                    op0=ALU.mult, op1=ALU.mult), reads=[b_xs, b_rstd[t], b_rows], writes=[b_h32])
        P.dve(lambda e: e.tensor_tensor(out=h32[:], in0=h32[:], in1=rows[:, 1, :], op=ALU.add), reads=[b_h32, b_rows], writes=[b_h32])
        P.dve(lambda e: e.tensor_copy(out=hhi[:], in_=h32[:]), reads=[b_h32], writes=[b_hhi])
        P.dve(lambda e: e.tensor_tensor(out=hlo[:], in0=h32[:], in1=hhi[:], op=ALU.subtract), reads=[b_h32, b_hhi], writes=[b_hlo])
        for k in range(8):
            P.pe(lambda e, k=k: e.transpose(out=PTB[:, k, :], in_=hhi[:, k * 128:(k + 1) * 128], identity=identb[:]),
                 reads=[b_hhi, b_idb], writes=[b_ptb])
        P.dma("sp", lambda e, t=t: e.dma_start(out=Hd[t * 128:(t + 1) * 128, :], in_=hhi[:]), reads=[b_hhi], writes=[b_hd[t]])
        P.dve(lambda e: e.tensor_copy(out=hThi[:], in_=PTB[:]), reads=[b_ptb], writes=[b_hthi])
        for k in range(8):
            P.pe(lambda e, k=k: e.transpose(out=PTB[:, k, :], in_=hlo[:, k * 128:(k + 1) * 128], identity=identb[:]),
                 reads=[b_hlo, b_idb], writes=[b_ptb])
        P.act(lambda e: e.activation(out=hTlo[:], in_=PTB[:], func=AF.Copy), reads=[b_ptb], writes=[b_htlo])
        first = True
        for k in range(8):
            for (lh, bl, rw, bw) in ((0, b_hthi, wrh, b_wrh), (0, b_hthi, wrl, b_wrl), (1, b_htlo, wrh, b_wrh)):
                lhs = hThi[:, k, :] if lh == 0 else hTlo[:, k, :]
                last = (k == 7 and lh == 1)
                P.pe(lambda e, lhs=lhs, rw=rw, k=k, first=first, last=last: e.matmul(PRG[:, 0:36], lhsT=lhs, rhs=rw[:, k, :], start=first, stop=last),
                     reads=[bl, bw], writes=[b_prg])
                first = False
        P.dve(lambda e: e.tensor_tensor(out=LGA[:, t, :], in0=PRG[:, 0:36], in1=brt[:], op=ALU.add), reads=[b_prg, b_br], writes=[b_lga[t]])

    for t in range(18):
        i = t % 2
        phase_a(t, XS[i], H32[i], HHI[i], HLO[i], HTHI[i], HTLO[i], SM[i], LG[i], B_xs[i], B_h32[i], B_hhi[i], B_hlo[i], B_hthi[i], B_htlo[i], B_sm[i], B_lg[i])

    T = 18
    rp = big[:, 0:T * 32].rearrange("p (t g j) -> p t g j", g=4, j=8)
    rt = big[:, 576:576 + T * 64].rearrange("p (t c) -> p t c", c=64)
    RB = [b_big]

    def rc(c0, c1=None):
        return rt[:, :, c0] if c1 is None else rt[:, :, c0:c1]

    def bc(ap2, n):
        return ap2.unsqueeze(2).to_broadcast([128, T, n])
    Lg = LGA[:, :, 0:4]
    Le = LGA[:, :, 4:36].rearrange("p t (g j) -> p t g j", j=8)
    P.dve(lambda e: e.tensor_reduce(out=rc(0), in_=Lg, axis=AX.X, op=ALU.max), reads=b_lga, writes=RB)
    P.dve(lambda e: e.tensor_tensor(out=rc(8, 12), in0=Lg, in1=bc(rc(0), 4), op=ALU.subtract), reads=b_lga + RB, writes=RB)
    P.act(lambda e: e.activation(out=rc(8, 12), in_=rc(8, 12), func=AF.Exp), reads=RB, writes=RB)
    P.dve(lambda e: e.tensor_reduce(out=rc(2), in_=rc(8, 12), axis=AX.X, op=ALU.add), reads=RB, writes=RB)
    P.dve(lambda e: e.reciprocal(out=rc(3), in_=rc(2)), reads=RB, writes=RB)
    P.dve(lambda e: e.tensor_tensor(out=rc(12, 16), in0=Lg, in1=bc(rc(0), 4), op=ALU.is_equal), reads=b_lga + RB, writes=RB)
    P.dve(lambda e: e.tensor_tensor(out=rp, in0=Le, in1=rc(12, 16).unsqueeze(3).to_broadcast([128, T, 4, 8]), op=ALU.mult),
          reads=b_lga + RB, writes=RB)
    P.dve(lambda e: e.tensor_reduce(out=rc(16, 24), in_=rp.rearrange("p t g j -> p t j g"), axis=AX.X, op=ALU.add), reads=RB, writes=RB)
    P.dve(lambda e: e.tensor_reduce(out=rc(4), in_=rc(16, 24), axis=AX.X, op=ALU.max), reads=RB, writes=RB)
    P.dve(lambda e: e.tensor_tensor(out=rc(24, 32), in0=rc(16, 24), in1=bc(rc(4), 8), op=ALU.is_equal), reads=RB, writes=RB)
    P.dve(lambda e: e.scalar_tensor_tensor(out=rc(32, 40), in0=rc(24, 32), scalar=-1e30, in1=rc(16, 24), op0=ALU.mult, op1=ALU.add),
          reads=RB, writes=RB)
    P.dve(lambda e: e.tensor_reduce(out=rc(5), in_=rc(32, 40), axis=AX.X, op=ALU.max), reads=RB, writes=RB)
    P.dve(lambda e: e.tensor_tensor(out=rc(40, 48), in0=rc(32, 40), in1=bc(rc(5), 8), op=ALU.is_equal), reads=RB, writes=RB)
    P.dve(lambda e: e.tensor_tensor(out=rc(6), in0=rc(5), in1=rc(4), op=ALU.subtract), reads=RB, writes=RB)
    P.act(lambda e: e.activation(out=rc(7), in_=rc(6), func=AF.Exp), reads=RB, writes=RB)
    P.dve(lambda e: e.tensor_scalar(out=rc(7), in0=rc(7), scalar1=1.0, scalar2=None, op0=ALU.add), reads=RB, writes=RB)
    P.dve(lambda e: e.reciprocal(out=rc(7), in_=rc(7)), reads=RB, writes=RB)
    P.dve(lambda e: e.tensor_tensor(out=g12[:, :, 0], in0=rc(7), in1=rc(3), op=ALU.mult), reads=RB, writes=b_g12)
    P.dve(lambda e: e.tensor_tensor(out=g12[:, :, 1], in0=rc(3), in1=g12[:, :, 0], op=ALU.subtract), reads=RB + b_g12, writes=b_g12)
    for k, c0 in ((0, 24), (1, 40)):
        P.dve(lambda e, k=k, c0=c0: e.tensor_tensor(out=M12[:, :, k, :].rearrange("p t (g j) -> p t g j", j=8),
                                                    in0=rc(12, 16).unsqueeze(3).to_broadcast([128, T, 4, 8]),
                                                    in1=rc(c0, c0 + 8).unsqueeze(2).to_broadcast([128, T, 4, 8]), op=ALU.mult),
              reads=RB, writes=b_M)
    P.dve(lambda e: e.tensor_tensor(out=Msum[:, :, :], in0=M12[:, :, 0, :], in1=M12[:, :, 1, :], op=ALU.add), reads=b_M, writes=b_ms)

    for t in range(18):
        P.pe(lambda e, t=t: e.matmul(PRG[:, 0:32], lhsT=onesb[:, :], rhs=Msum[:, t, :], start=(t == 0), stop=(t == 17)),
             reads=[b_U, b_ms[t]], writes=[b_prg])
    P.dve(lambda e: e.tensor_copy(out=cnt[:], in_=PRG[:, 0:32]), reads=[b_prg], writes=[b_cnt])
    big3 = big[:, 0:32 * 18].rearrange("p (e j) -> p e j", j=18)
    P.dve(lambda e: e.tensor_tensor(out=big3, in0=cnt[:, :].unsqueeze(2).to_broadcast([128, 32, 18]), in1=thr[:], op=ALU.is_gt),
          reads=[b_cnt, b_thr], writes=[b_big])
    P.dve(lambda e: e.tensor_reduce(out=nblk[:], in_=big3, axis=AX.X, op=ALU.add), reads=[b_big], writes=[b_nblk])
    P.dve(lambda e: e.tensor_scalar(out=padded[:], in0=nblk[:], scalar1=128.0, scalar2=None, op0=ALU.mult), reads=[b_nblk], writes=[b_pad])
    P.dve(lambda e: e.tensor_copy(out=ca[:], in_=padded[:]), reads=[b_pad], writes=[b_ca])
    src, bsrc, dst, bdst = ca, b_ca, cb, b_cb
    for sft in (1, 2, 4, 8, 16):
        P.dve(lambda e, src=src, dst=dst, sft=sft: e.tensor_tensor(out=dst[:, sft:32], in0=src[:, sft:32], in1=src[:, 0:32 - sft], op=ALU.add),
              reads=[bsrc], writes=[bdst])
        P.dve(lambda e, src=src, dst=dst, sft=sft: e.tensor_copy(out=dst[:, 0:sft], in_=src[:, 0:sft]), reads=[bsrc], writes=[bdst])
        src, bsrc, dst, bdst = dst, bdst, src, bsrc
    pend, b_pend = src, bsrc
    P.dve(lambda e: e.tensor_tensor(out=pstart[:], in0=pend[:], in1=padded[:], op=ALU.subtract), reads=[b_pend, b_pad], writes=[b_ps])
    big4 = big[:].rearrange("p (b e) -> p b e", e=32)
    P.dve(lambda e: e.tensor_tensor(out=big4, in0=pend[:, :].unsqueeze(1).to_broadcast([128, NB, 32]), in1=bvals[:], op=ALU.is_le),
          reads=[b_pend, b_bv, b_nblk], writes=[b_big])
    P.dve(lambda e: e.tensor_reduce(out=bexp[:], in_=big4, axis=AX.X, op=ALU.add), reads=[b_big], writes=[b_bexp])
    P.dve(lambda e: e.tensor_scalar(out=bexp[:], in0=bexp[:], scalar1=31.0, scalar2=None, op0=ALU.min), reads=[b_bexp], writes=[b_bexp])
    P.dve(lambda e: e.memset(same[:], 0.0), writes=[b_same])
    P.dve(lambda e: e.tensor_tensor(out=same[:, 1:NB], in0=bexp[:, 1:NB], in1=bexp[:, 0:NB - 1], op=ALU.is_equal),
          reads=[b_bexp], writes=[b_same])
    P.dve(lambda e: e.tensor_scalar(out=bexp[:], in0=bexp[:], scalar1=128.0, scalar2=pcol[:, 0:1], op0=ALU.mult, op1=ALU.add),
          reads=[b_bexp, b_pc, b_same], writes=[b_bexp])
    P.dve(lambda e: e.scalar_tensor_tensor(out=bexp[:], in0=same[:], scalar=8192.0, in1=bexp[:], op0=ALU.mult, op1=ALU.add),
          reads=[b_bexp, b_same], writes=[b_bexp])
    P.dve(lambda e: e.tensor_copy(out=widx[:], in_=bexp[:]), reads=[b_bexp], writes=[b_widx])
    for t in range(18):
        P.pe(lambda e, t=t: e.matmul(PRK[:, 0:32], lhsT=Ub[:, :], rhs=Msum[:, t, :], start=True, stop=True), reads=[b_U, b_ms[t]], writes=[b_prk])
        P.pe(lambda e, t=t: e.matmul(PRK[:, 32:64], lhsT=onesb[:, :], rhs=Msum[:, t, :], start=True, stop=True), reads=[b_U, b_ms[t]], writes=[b_prk])
        P.dve(lambda e: e.tensor_tensor(out=tA[:], in0=PRK[:, 0:32], in1=carry[:], op=ALU.add), reads=[b_prk, b_carry], writes=[b_tA])
        P.dve(lambda e: e.tensor_tensor(out=tA[:], in0=tA[:], in1=pstart[:], op=ALU.add), reads=[b_tA, b_ps], writes=[b_tA])
        for k in range(2):
            P.dve(lambda e, t=t, k=k: e.tensor_tensor(out=tB[:], in0=tA[:], in1=M12[:, t, k, :], op=ALU.mult), reads=[b_tA, b_M[t]], writes=[b_tB])
            P.dve(lambda e, t=t, k=k: e.tensor_reduce(out=destf[:, t, k:k + 1], in_=tB[:], axis=AX.X, op=ALU.add), reads=[b_tB], writes=[b_destf[t]])
        P.dve(lambda e: e.tensor_tensor(out=carry[:], in0=carry[:], in1=PRK[:, 32:64], op=ALU.add), reads=[b_prk, b_carry], writes=[b_carry])
    P.dve(lambda e: e.tensor_copy(out=desti[:], in_=destf[:]), reads=b_destf, writes=[b_desti])
    for t in range(18):
        for k in range(2):
            P.dma("pool", lambda e, t=t, k=k: e.indirect_dma_start(out=table, out_offset=bass.IndirectOffsetOnAxis(ap=desti[:, t, k:k + 1], axis=0),
                                                                   in_=tokid[:, t, :], in_offset=None),
                  reads=[b_desti, b_tok, b_tab0], writes=[b_sc[2 * t + k]])

    def gather_x(b):
        i = b % 2
        P.dma("sp", lambda e, b=b, i=i: e.dma_start(out=IDX[i][:], in_=table[b * 128:(b + 1) * 128, :]), reads=b_sc + [b_tab0], writes=[b_idx[i]])
        P.dma("pool", lambda e, i=i: e.indirect_dma_start(out=XG[i][:], out_offset=None, in_=Hd,
                                                          in_offset=bass.IndirectOffsetOnAxis(ap=IDX[i][:, 0:1], axis=0)),
              reads=[b_idx[i], b_hdz] + b_hd, writes=[b_xg[i]])

    def gather_w(b):
        for (wg, bwg, wl) in ((WG1[0], b_wg1[0], w1L), (WG3[0], b_wg3[0], w3L), (WG2[0], b_wg2[0], w2L)):
            P.dma("pool", lambda e, wg=wg, wl=wl, b=b: e.indirect_dma_start(out=wg[:], out_offset=None, in_=wl,
                                                                            in_offset=bass.IndirectOffsetOnAxis(ap=widx[:, b:b + 1], axis=0),
                                                                            bounds_check=regs['b'], oob_is_err=False),
                  reads=[b_widx], writes=[bwg])

    def copy_w(b):
        i = b % 2
        w1f = W1[i][:].rearrange("p k f -> p (k f)"); w3f = W3[i][:].rearrange("p k f -> p (k f)"); w2f = W2[i][:].rearrange("p k f -> p (k f)")
        P.act(lambda e: e.activation(out=w1f[:, 0:2048], in_=WG1[0][:, 0:2048], func=AF.Copy), reads=[b_wg1[0]], writes=[b_w1[i]])
        P.dve(lambda e: e.tensor_copy(out=w1f[:, 2048:4096], in_=WG1[0][:, 2048:4096]), reads=[b_wg1[0]], writes=[b_w1[i]])
        P.act(lambda e: e.activation(out=w3f[:, 0:2048], in_=WG3[0][:, 0:2048], func=AF.Copy), reads=[b_wg3[0]], writes=[b_w3[i]])
        P.dve(lambda e: e.tensor_copy(out=w3f[:, 2048:4096], in_=WG3[0][:, 2048:4096]), reads=[b_wg3[0]], writes=[b_w3[i]])
        P.dve(lambda e: e.tensor_copy(out=w2f, in_=WG2[0][:]), reads=[b_wg2[0]], writes=[b_w2[i]])

    def st_t8(b):
        i = b % 2
        for k in range(8):
            P.pe(lambda e, k=k, i=i: e.transpose(out=PTB[:, k, :], in_=XG[i][:, k * 128:(k + 1) * 128], identity=identb[:]),
                 reads=[b_xg[i], b_idb], writes=[b_ptb])
        P.act(lambda e, i=i: e.activation(out=XGT[i][:], in_=PTB[:], func=AF.Copy), reads=[b_ptb], writes=[B_xgt[i]])

    def st_h(b):
        i = b % 2
        for k in range(8):
            P.pe(lambda e, k=k, i=i: e.matmul(PH1[i][:, :], lhsT=XGT[i][:, k, :], rhs=W1[i][:, k, :], start=(k == 0), stop=(k == 7)),
                 reads=[B_xgt[i], b_w1[i]], writes=[B_ph1[i]])
        for k in range(8):
            P.pe(lambda e, k=k, i=i: e.matmul(PH3[i][:, :], lhsT=XGT[i][:, k, :], rhs=W3[i][:, k, :], start=(k == 0), stop=(k == 7)),
                 reads=[B_xgt[i], b_w3[i]], writes=[B_ph3[i]])
        P.act(lambda e, i=i: e.activation(out=SSB[i][:], in_=PH1[i][:, :], func=AF.Silu), reads=[B_ph1[i]], writes=[B_ssb[i]])
        P.dve(lambda e, i=i: e.tensor_tensor(out=HB[i][:], in0=PH3[i][:, :], in1=SSB[i][:], op=ALU.mult), reads=[B_ph3[i], B_ssb[i]], writes=[B_hb[i]])

    def st_t4(b):
        i = b % 2
        for f in range(4):
            P.pe(lambda e, f=f, i=i: e.transpose(out=PTH2[:, i, f, :], in_=HB[i][:, f * 128:(f + 1) * 128], identity=identb[:]),
                 reads=[B_hb[i], b_idb], writes=[B_pth[i]])
        P.act(lambda e, i=i: e.activation(out=HTT[i][:], in_=PTH2[:, i, :, :], func=AF.Copy), reads=[B_pth[i]], writes=[B_hT[i]])

    def st_y(b):
        i = b % 2
        for h in range(2):
            for f in range(4):
                P.pe(lambda e, h=h, f=f, i=i: e.matmul(PY[:, h * 512:(h + 1) * 512], lhsT=HTT[i][:, f, :], rhs=W2[i][:, f, h * 512:(h + 1) * 512],
                                                      start=(f == 0), stop=(f == 3)), reads=[B_hT[i], b_w2[i]], writes=[b_py])
        P.dve(lambda e, i=i: e.tensor_copy(out=YS[i][:], in_=PY[:]), reads=[b_py], writes=[b_ys[i]])
        P.dma("sp", lambda e, i=i, b=b: e.dma_start(out=Yd[b * 128:(b + 1) * 128, :], in_=YS[i][:]), reads=[b_ys[i]], writes=[b_yd[b]])

    gather_x(0); gather_x(1)
    gather_w(0)
    for b0 in range(0, NB, 2):
        b1 = b0 + 1
        copy_w(b0)
        gather_w(b1)
        copy_w(b1)
        if b1 + 1 < NB:
            gather_w(b1 + 1)
        st_t8(b0); st_t8(b1)
        if b0 + 2 < NB:
            gather_x(b0 + 2); gather_x(b0 + 3)
        st_h(b0); st_h(b1)
        st_t4(b0); st_t4(b1)
        st_y(b0); st_y(b1)

    outs = []

    def phase_c(t, xs, h32, Y1, Y2, b_xs, b_h32, b_y1, b_y2):
        s = 0 if t < 16 else 1
        if t == 0 or t == 16:
            P.dma("sp", lambda e, s=s: e.dma_start(out=rows[:, 0, :], in_=rowsd[s, 2]), writes=[b_rows])
        P.dma("sp", lambda e, t=t: e.dma_start(out=xs[:], in_=xt[t]), writes=[b_xs])
        P.dma("pool", lambda e, t=t: e.indirect_dma_start(out=Y1[:], out_offset=None, in_=Yd,
                                                          in_offset=bass.IndirectOffsetOnAxis(ap=desti[:, t, 0:1], axis=0)),
              reads=[b_desti] + b_yd, writes=[b_y1])
        P.dma("pool", lambda e, t=t: e.indirect_dma_start(out=Y2[:], out_offset=None, in_=Yd,
                                                          in_offset=bass.IndirectOffsetOnAxis(ap=desti[:, t, 1:2], axis=0)),
              reads=[b_desti] + b_yd, writes=[b_y2])
        P.dve(lambda e, t=t: e.tensor_scalar(out=h32[:], in0=Y1[:], scalar1=g12[:, t, 0:1], scalar2=None, op0=ALU.mult),
              reads=[b_y1, b_g12[t]], writes=[b_h32])
        P.dve(lambda e, t=t: e.scalar_tensor_tensor(out=h32[:], in0=Y2[:], scalar=g12[:, t, 1:2], in1=h32[:], op0=ALU.mult, op1=ALU.add),
              reads=[b_y2, b_g12[t], b_h32], writes=[b_h32])
        P.dve(lambda e: e.tensor_tensor(out=h32[:], in0=h32[:], in1=rows[:, 0, :], op=ALU.mult), reads=[b_h32, b_rows], writes=[b_h32])
        P.dve(lambda e: e.tensor_tensor(out=xs[:], in0=xs[:], in1=h32[:], op=ALU.add), reads=[b_h32, b_xs], writes=[b_xs])
        outs.append(P.dma("sp", lambda e, t=t: e.dma_start(out=xo[t], in_=xs[:]), reads=[b_xs]))
    for t in range(18):
        i = t % 2
        phase_c(t, XS[i], H32[i], YY1[i], YY2[i], B_xs[i], B_h32[i], B_y1[i], B_y2[i])
    P.emit(final_wait_ops=outs)
    return nc


def build_att():
    nc = bass.Bass("TRN2", target_bir_lowering=False)
    P = Prog(nc)
    xt = P.din("xt", [20, 128, 1024], F32)
    mcols = P.din("mcols", [128, 2, 2, 8], F32)
    wqkv = P.din("wqkv", [1024, 1536], F32)
    gqd = P.din("gq", [128, 64], F32)
    gkd = P.din("gk", [128, 64], F32)
    roped = P.din("rope", [18, 128, 2, 32], F32)
    sinkd = P.din("sink", [128, 16], F32)
    maskd = P.din("masks", [128, 4, 128], F32)
    OT = P.dout("OT", [1024, 2304], BF16)

    wb = P.sb("wb", [128, 8, 1536], BF16)
    mc = P.sb("mc", [128, 2, 2, 8], F32)
    gq = P.sb("gqt", [128, 64], F32)
    gk = P.sb("gkt", [128, 64], F32)
    rope = P.sb("ropet", [128, 18, 2, 32], F32)
    esink = P.sb("esink", [128, 16], F32)
    masks = P.sb("maskst", [128, 4, 128], BF16)
    epsc = P.sb("epsc", [128, 1], F32)
    XS = [P.sb(f"xs{i}", [128, 1024], F32) for i in range(2)]
    junk = P.sb("junk", [128, 1024], BF16)
    XN = [P.sb(f"xn{i}", [128, 1024], BF16) for i in range(2)]
    ss = P.sb("ss", [128, 20], F32)
    rstd = P.sb("rstd", [128, 20], F32)
    HT = [P.sb(f"hT{i}", [128, 8, 128], BF16) for i in range(2)]
    qf = P.sb("qf", [128, 1024], F32)
    kf = P.sb("kf", [128, 256], F32)
    tmp = P.sb("tmp", [128, 1024], F32)
    tmp2 = P.sb("tmp2", [128, 512], F32)
    tmp3 = P.sb("tmp3", [128, 512], F32)
    sq = P.sb("sq", [128, 20], F32)
    qr = P.sb("qr", [128, 1024], BF16)
    kr = P.sb("kr", [128, 256], BF16)
    QT = P.sb("QT", [64, 18, 16, 128], BF16)
    KT = P.sb("KT", [64, 20, 4, 128], BF16)
    VX = P.sb("VX", [128, 20, 4, 65], BF16)
    PTS = [P.sb(f"pts{i}", [128, 5, 512], BF16) for i in range(2)]
    den = P.sb("den", [128, 8], F32)
    osb = P.sb("osb", [128, 1024], BF16)
    ots = P.sb("ots", [128, 8, 128], BF16)
    ident, b_ident = make_ident(P)

    b_w, b_mc, b_gq, b_gk, b_rope, b_esink, b_masks, b_eps = P.bufs(8)
    _u1, b_junk, _u2, _u3, b_qf, b_kf, b_tmp, b_tmp2, b_tmp3, b_sq, b_qr, b_kr = P.bufs(12)
    B_xs = P.bufs(2); B_xn = P.bufs(2); B_hT = P.bufs(2)
    b_ss = P.bufs(20); b_rstd = P.bufs(20)
    b_QT = P.bufs(18); b_KT = P.bufs(20); b_VX = P.bufs(20); b_vone = P.buf()
    b_pts = P.bufs(2); b_den = P.buf(); b_osb = P.buf(); b_ots = P.buf()
    b_phase = P.buf()

    P.dma("pool", lambda e: e.dma_start(out=wb[:], in_=wqkv.rearrange("(k p) n -> p k n", p=128)), writes=[b_w])
    P.dma("sp", lambda e: e.dma_start(out=mc[:], in_=mcols), writes=[b_mc])
    P.dma("sp", lambda e: e.dma_start(out=gq[:], in_=gqd), writes=[b_gq])
    P.dma("sp", lambda e: e.dma_start(out=gk[:], in_=gkd), writes=[b_gk])
    P.dma("sp", lambda e: e.dma_start(out=rope[:], in_=roped.rearrange("t p a i -> p t a i")), writes=[b_rope])
    P.dma("sp", lambda e: e.dma_start(out=esink[:], in_=sinkd), writes=[b_esink])
    P.dma("pool", lambda e: e.dma_start(out=masks[:], in_=maskd), writes=[b_masks])
    P.pool(lambda e: e.memset(epsc[:], EPS), writes=[b_eps])
    P.act(lambda e: e.activation(out=esink[:], in_=esink[:], func=AF.Exp), reads=[b_esink], writes=[b_esink])
    P.dve(lambda e: e.tensor_scalar(out=gq[:], in0=gq[:], scalar1=0.125, scalar2=None, op0=ALU.mult), reads=[b_gq], writes=[b_gq])
    P.pool(lambda e: e.memset(VX[:], 1.0), writes=[b_vone])

    ps1 = ExitStack()
    PTx = ps1.enter_context(nc.psum_tensor("ptx", [128, 8, 128], BF16))
    PQ = ps1.enter_context(nc.psum_tensor("pq", [128, 1024], F32))
    PKV = ps1.enter_context(nc.psum_tensor("pkv", [128, 512], F32))
    PTq = ps1.enter_context(nc.psum_tensor("ptq", [64, 16, 128], BF16))
    PTk = ps1.enter_context(nc.psum_tensor("ptk", [64, 4, 128], BF16))
    b_ptx, b_pq, b_pkv, b_ptq, b_ptk = P.bufs(5)

    def qknorm_rope(src, H, gain, bgain, dst, bdst, t, do_rope, bsrc):
        W = H * 64
        P.dve(lambda e: e.tensor_tensor(out=tmp[:, 0:W], in0=src[:, 0:W], in1=src[:, 0:W], op=ALU.mult), reads=[bsrc], writes=[b_tmp])
        P.dve(lambda e: e.tensor_reduce(out=sq[:, 0:H], in_=tmp[:, 0:W].rearrange("p (h d) -> p h d", d=64), axis=AX.X, op=ALU.add),
              reads=[b_tmp], writes=[b_sq])
        P.act(lambda e: e.activation(out=sq[:, 0:H], in_=sq[:, 0:H], func=AF.Sqrt, scale=1.0 / 64, bias=epsc[:, 0:1]),
              reads=[b_sq, b_eps], writes=[b_sq])
        P.dve(lambda e: e.reciprocal(out=sq[:, 0:H], in_=sq[:, 0:H]), reads=[b_sq], writes=[b_sq])
        s3 = src[:, 0:W].rearrange("p (h d) -> p h d", d=64)
        t3 = tmp[:, 0:W].rearrange("p (h d) -> p h d", d=64)
        P.dve(lambda e: e.tensor_tensor(out=t3, in0=s3, in1=sq[:, 0:H].unsqueeze(2).to_broadcast([128, H, 64]), op=ALU.mult),
              reads=[bsrc, b_sq], writes=[b_tmp])
        if not do_rope:
            d3 = dst[:, 0:W].rearrange("p (h d) -> p h d", d=64)
            P.dve(lambda e: e.tensor_tensor(out=d3, in0=t3, in1=gain[:, :].unsqueeze(1).to_broadcast([128, H, 64]), op=ALU.mult),
                  reads=[b_tmp, bgain], writes=[bdst])
            return
        P.dve(lambda e: e.tensor_tensor(out=t3, in0=t3, in1=gain[:, :].unsqueeze(1).to_broadcast([128, H, 64]), op=ALU.mult),
              reads=[b_tmp, bgain], writes=[b_tmp])
        x5 = tmp[:, 0:W].rearrange("p (h a f i) -> p h a f i", a=2, f=2, i=16)
        d5 = dst[:, 0:W].rearrange("p (h a f i) -> p h a f i", a=2, f=2, i=16)
        x1 = x5[:, :, :, 0, :]; x2 = x5[:, :, :, 1, :]
        cs = rope[:, t, 0, :].rearrange("p (a i) -> p a i", a=2).unsqueeze(1).to_broadcast([128, H, 2, 16])
        sn = rope[:, t, 1, :].rearrange("p (a i) -> p a i", a=2).unsqueeze(1).to_broadcast([128, H, 2, 16])
        n = H * 32
        a4 = tmp2[:, 0:n].rearrange("p (h a i) -> p h a i", a=2, i=16)
        b4 = tmp3[:, 0:n].rearrange("p (h a i) -> p h a i", a=2, i=16)
        P.dve(lambda e: e.tensor_tensor(out=a4, in0=x1, in1=cs, op=ALU.mult), reads=[b_tmp, b_rope], writes=[b_tmp2])
        P.dve(lambda e: e.tensor_tensor(out=b4, in0=x2, in1=sn, op=ALU.mult), reads=[b_tmp, b_rope], writes=[b_tmp3])
        P.dve(lambda e: e.tensor_tensor(out=d5[:, :, :, 0, :], in0=a4, in1=b4, op=ALU.subtract), reads=[b_tmp2, b_tmp3], writes=[bdst])
        P.dve(lambda e: e.tensor_tensor(out=a4, in0=x1, in1=sn, op=ALU.mult), reads=[b_tmp, b_rope, bdst], writes=[b_tmp2])
        P.dve(lambda e: e.tensor_tensor(out=b4, in0=x2, in1=cs, op=ALU.mult), reads=[b_tmp, b_rope, bdst], writes=[b_tmp3])
        P.dve(lambda e: e.tensor_tensor(out=d5[:, :, :, 1, :], in0=a4, in1=b4, op=ALU.add), reads=[b_tmp2, b_tmp3], writes=[bdst])

    qidx = {}

    def pass1_tile(t, xs, xn, hT, b_xs, b_xn, b_hT):
        s = 0 if t < 18 else 1
        is_q = (1 <= t <= 16) or t >= 18
        do_rope = t < 18
        P.dma("sp", lambda e, t=t: e.dma_start(out=xs[:], in_=xt[t]), writes=[b_xs])
        P.act(lambda e, t=t: e.activation(out=junk[:], in_=xs[:], func=AF.Square, accum_out=ss[:, t:t + 1]),
              reads=[b_xs], writes=[b_junk, b_ss[t]])
        P.act(lambda e, t=t: e.activation(out=rstd[:, t:t + 1], in_=ss[:, t:t + 1], func=AF.Sqrt, scale=1.0 / 1024, bias=epsc[:, 0:1]),
              reads=[b_ss[t], b_eps], writes=[b_rstd[t]])
        P.dve(lambda e, t=t: e.reciprocal(out=rstd[:, t:t + 1], in_=rstd[:, t:t + 1]), reads=[b_rstd[t]], writes=[b_rstd[t]])
        P.dve(lambda e, t=t: e.tensor_scalar(out=xn[:], in0=xs[:], scalar1=rstd[:, t:t + 1], scalar2=None, op0=ALU.mult),
              reads=[b_xs, b_rstd[t]], writes=[b_xn])
        for k in range(8):
            P.pe(lambda e, k=k: e.transpose(out=PTx[:, k, :], in_=xn[:, k * 128:(k + 1) * 128], identity=ident[:]),
                 reads=[b_xn, b_ident], writes=[b_ptx])
        for k in range(8):
            P.act(lambda e, k=k, s=s: e.activation(out=hT[:, k, :], in_=PTx[:, k, :], func=AF.Identity,
                                                   scale=mc[:, s, 1, k:k + 1], bias=mc[:, s, 0, k:k + 1]),
                  reads=[b_ptx, b_mc], writes=[b_hT, b_phase])
        if is_q:
            for nb in range(2):
                for k in range(8):
                    P.pe(lambda e, nb=nb, k=k: e.matmul(PQ[:, nb * 512:(nb + 1) * 512], lhsT=hT[:, k, :], rhs=wb[:, k, nb * 512:(nb + 1) * 512],
                                                        start=(k == 0), stop=(k == 7)), reads=[b_hT, b_w], writes=[b_pq])
        for k in range(8):
            P.pe(lambda e, k=k: e.matmul(PKV[:, :], lhsT=hT[:, k, :], rhs=wb[:, k, 1024:1536], start=(k == 0), stop=(k == 7)),
                 reads=[b_hT, b_w], writes=[b_pkv])
        P.act(lambda e: e.activation(out=kf[:], in_=PKV[:, 0:256], func=AF.Copy), reads=[b_pkv], writes=[b_kf, b_phase])
        P.act(lambda e, t=t: e.activation(out=VX[:, t, :, 0:64], in_=PKV[:, 256:512].rearrange("p (j d) -> p j d", d=64), func=AF.Copy),
              reads=[b_pkv, b_vone], writes=[b_VX[t], b_phase])
        qknorm_rope(kf, 4, gk, b_gk, kr, b_kr, t, do_rope, b_kf)
        for j in range(4):
            P.pe(lambda e, j=j: e.transpose(out=PTk[:, j, :], in_=kr[:, j * 64:(j + 1) * 64], identity=ident[:]),
                 reads=[b_kr, b_ident], writes=[b_ptk])
        P.act(lambda e, t=t: e.activation(out=KT[:, t, :, :], in_=PTk[:, :, :], func=AF.Copy), reads=[b_ptk], writes=[b_KT[t], b_phase])
        if is_q:
            qi = len(qidx); qidx[t] = qi
            P.act(lambda e: e.activation(out=qf[:], in_=PQ[:], func=AF.Copy), reads=[b_pq], writes=[b_qf, b_phase])
            qknorm_rope(qf, 16, gq, b_gq, qr, b_qr, t, do_rope, b_qf)
            for h in range(16):
                P.pe(lambda e, h=h: e.transpose(out=PTq[:, h, :], in_=qr[:, h * 64:(h + 1) * 64], identity=ident[:]),
                     reads=[b_qr, b_ident], writes=[b_ptq])
            P.act(lambda e, qi=qi: e.activation(out=QT[:, qi, :, :], in_=PTq[:, :, :], func=AF.Copy), reads=[b_ptq], writes=[b_QT[qi], b_phase])
    for t in range(20):
        i = t % 2
        pass1_tile(t, XS[i], XN[i], HT[i], B_xs[i], B_xn[i], B_hT[i])
    ps1.close()

    PS = [P.ps(f"ps{i}", [128, 512], F32) for i in range(5)]
    PO = P.ps("po", [128, 4, 65], F32)
    POT = P.ps("pot", [128, 8, 128], BF16)
    b_ps = P.bufs(5); b_po = P.buf(); b_pot = P.buf()
    outs = []
    first = True
    pi = 0
    for t in list(range(1, 17)) + [18, 19]:
        qi = qidx[t]
        if t < 18:
            chunks = [(t - 1, 0 if t == 1 else 1), (t, None), (t + 1, 3 if t == 16 else 2), (18, None), (19, None)]
            col0 = (t - 1) * 128
        else:
            chunks = [(18, None), (19, None)]
            col0 = 2048 + (t - 18) * 128
        nch = len(chunks)
        for j in range(4):
            pts = PTS[pi % 2]; bpts = b_pts[pi % 2]; pi += 1
            for ci, (kt, m) in enumerate(chunks):
                wr = [b_ps[ci]] + ([b_phase] if first else [])
                first = False
                P.pe(lambda e, ci=ci, kt=kt, j=j, qi=qi: e.matmul(PS[ci][:, :], lhsT=KT[:, kt, j, :],
                                                                  rhs=QT[:, qi, 4 * j:4 * j + 4, :].rearrange("p h q -> p (h q)"),
                                                                  start=True, stop=True),
                     reads=[b_KT[kt], b_QT[qi]], writes=wr)
                P.act(lambda e, ci=ci, pts=pts: e.activation(out=pts[:, ci, :], in_=PS[ci][:, :], func=AF.Exp), reads=[b_ps[ci]], writes=[bpts])
                if m is not None:
                    P.dve(lambda e, ci=ci, pts=pts, m=m: e.tensor_tensor(out=pts[:, ci, :].rearrange("p (h q) -> p h q", h=4),
                                                                         in0=pts[:, ci, :].rearrange("p (h q) -> p h q", h=4),
                                                                         in1=masks[:, m, :].unsqueeze(1).to_broadcast([128, 4, 128]), op=ALU.mult),
                          reads=[bpts, b_masks], writes=[bpts])
            for g in range(4):
                for ci, (kt, m) in enumerate(chunks):
                    P.pe(lambda e, g=g, ci=ci, kt=kt, j=j, pts=pts, nch=nch: e.matmul(PO[:, g, :], lhsT=pts[:, ci, g * 128:(g + 1) * 128],
                                                                                      rhs=VX[:, kt, j, :], start=(ci == 0), stop=(ci == nch - 1)),
                         reads=[bpts, b_VX[kt]], writes=[b_po])
            P.dve(lambda e, j=j: e.tensor_tensor(out=den[:, 0:4], in0=PO[:, :, 64], in1=esink[:, 4 * j:4 * j + 4], op=ALU.add),
                  reads=[b_po, b_esink], writes=[b_den])
            P.dve(lambda e: e.reciprocal(out=den[:, 0:4], in_=den[:, 0:4]), reads=[b_den], writes=[b_den])
            P.dve(lambda e, j=j: e.tensor_tensor(out=osb[:, 256 * j:256 * j + 256].rearrange("p (g d) -> p g d", d=64), in0=PO[:, :, 0:64],
                                                 in1=den[:, 0:4].unsqueeze(2).to_broadcast([128, 4, 64]), op=ALU.mult),
                  reads=[b_po, b_den], writes=[b_osb])
        for k in range(8):
            P.pe(lambda e, k=k: e.transpose(out=POT[:, k, :], in_=osb[:, k * 128:(k + 1) * 128], identity=ident[:]),
                 reads=[b_osb, b_ident], writes=[b_pot])
        P.act(lambda e: e.activation(out=ots[:], in_=POT[:], func=AF.Copy), reads=[b_pot], writes=[b_ots])
        outs.append(P.dma("sp", lambda e, col0=col0: e.dma_start(out=OT.rearrange("(k p) t -> p k t", p=128)[:, :, col0:col0 + 128], in_=ots[:]),
                          reads=[b_ots]))
    P.emit(final_wait_ops=outs)
    return nc


def rope_tables(core):
    t0 = core * 2048 - 128
    t = np.arange(t0, t0 + 18 * 128)
    t = np.clip(t, 0, 16383)
    pos = np.stack([t // 64, t % 64], -1).astype(np.float32)
    freqs = (10000.0 ** (-np.arange(16, dtype=np.float32) / 16)).astype(np.float32)
    ang = pos[:, :, None] * freqs
    cs = np.cos(ang).reshape(-1, 32); sn = np.sin(ang).reshape(-1, 32)
    return np.stack([cs, sn], 1).reshape(18, 128, 2, 32).astype(np.float32)


def att_masks(core):
    j = np.arange(128)[:, None]; i = np.arange(128)[None, :]
    prev = (j >= i).astype(np.float32); nxt = (j <= i).astype(np.float32)
    m = np.stack([prev if core > 0 else np.zeros_like(prev), prev, nxt, nxt if core < 7 else np.zeros_like(nxt)], 1)
    return np.ascontiguousarray(m).astype(np.float32)


def build_mod():
    nc = bass.Bass("TRN2", target_bir_lowering=False)
    P = Prog(nc)
    ccd = P.din("cc", [128, 8, 2], F32)
    wmd = P.din("wm", [4, 1024, 768], F32)
    bmd = P.din("bm", [1, 4, 768], F32)
    gmd = P.din("gm", [1, 4, 256], F32)
    mo = P.dout("mo", [4, 2, 768], F32)
    cc = P.sb("cct", [128, 8, 2], F32)
    S = P.sb("S", [128, 8, 33], F32)
    Sh = P.sb("Sh", [128, 8, 33], BF16)
    Sl = P.sb("Sl", [128, 8, 33], BF16)
    brow = P.sb("brow", [33, 4, 768], F32)
    grow = P.sb("grow", [33, 4, 256], F32)
    WT = [P.sb(f"wt{i}", [128, 8, 768], F32) for i in range(2)]
    Wh = P.sb("Wh", [128, 8, 768], BF16)
    Wl = P.sb("Wl", [128, 8, 768], BF16)
    RR = [P.sb(f"r{i}", [33, 768], F32) for i in range(2)]
    PM = P.ps("pm", [128, 1024], F32)
    b_cc, b_S, b_Sh, b_Sl, b_brow, b_grow, b_Wh, b_Wl, b_pm = P.bufs(9)
    b_wt = P.bufs(2); b_r = P.bufs(2)
    P.dma("sp", lambda e: e.dma_start(out=cc[:], in_=ccd), writes=[b_cc])
    P.pool(lambda e: e.memset(S[:], 0.0), writes=[b_S])
    P.pool(lambda e: e.memset(brow[:], 0.0), writes=[b_brow])
    P.pool(lambda e: e.memset(grow[:], 0.0), writes=[b_grow])
    for prt in (0, 32):
        P.dma("sp", lambda e, prt=prt: e.dma_start(out=brow[prt:prt + 1], in_=bmd), reads=[], writes=[b_brow])
        P.dma("sp", lambda e, prt=prt: e.dma_start(out=grow[prt:prt + 1], in_=gmd), reads=[], writes=[b_grow])
    P.act(lambda e: e.activation(out=S[:, :, 0], in_=cc[:, :, 0], func=AF.Silu), reads=[b_cc], writes=[b_S])
    P.act(lambda e: e.activation(out=S[:, :, 32], in_=cc[:, :, 1], func=AF.Silu), reads=[b_cc], writes=[b_S])
    P.dve(lambda e: e.tensor_copy(out=Sh[:], in_=S[:]), reads=[b_S], writes=[b_Sh])
    P.dve(lambda e: e.tensor_tensor(out=Sl[:], in0=S[:], in1=Sh[:], op=ALU.subtract), reads=[b_S, b_Sh], writes=[b_Sl])
    outs = []
    for l in range(4):
        wt = WT[l % 2]; bwt = b_wt[l % 2]; r = RR[l % 2]; br_ = b_r[l % 2]
        P.dma("sp", lambda e, wt=wt, l=l: e.dma_start(out=wt[:], in_=wmd[l].rearrange("(k p) n -> p k n", p=128)), writes=[bwt])
        P.dve(lambda e, wt=wt: e.tensor_copy(out=Wh[:], in_=wt[:]), reads=[bwt], writes=[b_Wh])
        P.dve(lambda e, wt=wt: e.tensor_tensor(out=Wl[:], in0=wt[:], in1=Wh[:], op=ALU.subtract), reads=[bwt, b_Wh], writes=[b_Wl])
        for half in range(2):
            n = 0
            for k in range(8):
                for (sa, bsa, wa, bwa) in ((Sh, b_Sh, Wh, b_Wh), (Sh, b_Sh, Wl, b_Wl), (Sl, b_Sl, Wh, b_Wh)):
                    P.pe(lambda e, sa=sa, wa=wa, k=k, half=half, n=n: e.matmul(PM[0:33, half * 512:half * 512 + 384], lhsT=sa[:, k, :],
                                                                             rhs=wa[:, k, half * 384:(half + 1) * 384],
                                                                             start=(n == 0), stop=(n == 23)),
                         reads=[bsa, bwa], writes=[b_pm])
                    n += 1
        for half in range(2):
            P.dve(lambda e, r=r, half=half, l=l: e.tensor_tensor(out=r[:, half * 384:(half + 1) * 384], in0=PM[0:33, half * 512:half * 512 + 384],
                                                                 in1=brow[:, l, half * 384:(half + 1) * 384], op=ALU.add),
                  reads=[b_pm, b_brow], writes=[br_])
        P.dve(lambda e, r=r, l=l: e.scalar_tensor_tensor(out=r[:, 128:256], in0=r[:, 128:256], scalar=1.0, in1=grow[:, l, 0:128],
                                                         op0=ALU.add, op1=ALU.mult), reads=[br_, b_grow], writes=[br_])
        P.dve(lambda e, r=r, l=l: e.scalar_tensor_tensor(out=r[:, 512:640], in0=r[:, 512:640], scalar=1.0, in1=grow[:, l, 128:256],
                                                         op0=ALU.add, op1=ALU.mult), reads=[br_, b_grow], writes=[br_])
        outs.append(P.dma("sp", lambda e, r=r, l=l: e.dma_start(out=mo[l, 0:1, :], in_=r[0:1, :]), reads=[br_]))
        outs.append(P.dma("sp", lambda e, r=r, l=l: e.dma_start(out=mo[l, 1:2, :], in_=r[32:33, :]), reads=[br_]))
    P.emit(final_wait_ops=outs)
    return nc


def run_mod(inp, progs):
    c = np.asarray(inp['c'], np.float32).reshape(1024)
    cx = np.asarray(inp['c_ctx'], np.float32).reshape(1024)
    cc = np.ascontiguousarray(np.stack([c.reshape(8, 128).T, cx.reshape(8, 128).T], -1))
    wm6 = np.asarray(inp['w_mod'], np.float32).reshape(4, 1024, 6, 1024)
    bm6 = np.asarray(inp['b_mod'], np.float32).reshape(4, 6, 1024)
    gmix = np.asarray(inp['norm_mix_g'], np.float32); gffn = np.asarray(inp['norm_ffn_g'], np.float32)
    ins = []
    for k in range(8):
        sl = slice(128 * k, 128 * k + 128)
        ins.append(dict(cc=cc, wm=np.ascontiguousarray(wm6[:, :, :, sl]).reshape(4, 1024, 768),
                        bm=np.ascontiguousarray(bm6[:, :, sl]).reshape(1, 4, 768),
                        gm=np.ascontiguousarray(np.stack([gmix[:, sl], gffn[:, sl]], 1)).reshape(1, 4, 256)))
    res = run_bass_kernel_spmd(progs['mod'], ins, core_ids=list(range(8)))
    modx = np.zeros((4, 2, 6, 1024), np.float32)
    for k in range(8):
        modx[:, :, :, 128 * k:128 * k + 128] = np.asarray(res.results[k]['mo']).reshape(4, 2, 6, 128)
    return modx


NCORES = 8
CORES = list(range(NCORES))


def _cols(v):
    return v.reshape(8, 128).T


def _rep(v):
    return np.ascontiguousarray(np.tile(np.asarray(v, np.float32).reshape(1, -1), (128, 1)))


def _wlayout(w, kch):
    E, K, N = w.shape
    return np.ascontiguousarray(np.asarray(w, np.float32).reshape(E, kch, 128, N).transpose(0, 2, 1, 3)).reshape(E * 128, kch * N)


def _moe_consts():
    bv = np.tile((128.0 * np.arange(NB, dtype=np.float32))[:, None], (1, 32)).reshape(1, -1)
    thr = np.tile((128.0 * np.arange(18, dtype=np.float32))[None, :], (32, 1)).reshape(1, -1)
    tok = (np.arange(18)[None, :, None] * 128 + np.arange(128)[:, None, None] + np.zeros((1, 1, 16))).astype(np.int32)
    return dict(bvals=np.tile(bv, (128, 1)).astype(np.float32), thr=np.tile(thr, (128, 1)).astype(np.float32),
                pcol=np.arange(128, dtype=np.float32).reshape(128, 1), tokid=np.ascontiguousarray(tok))


def _run(nc, ins):
    res = run_bass_kernel_spmd(nc, ins, core_ids=CORES)
    return res.results


def kernel(**inp):
    inp = {k: np.asarray(v) for k, v in inp.items()}
    progs = dict(mod=build_mod(), e1=build_e1(), e2=build_e2(), e3=build_e3(), att=build_att(), moe=build_moe2(), cast=build_cast())
    xl = np.ascontiguousarray(inp['x'][0], dtype=np.float32)
    xc = np.ascontiguousarray(inp['ctx'][0], dtype=np.float32)
    modx = run_mod(inp, progs)
    K1 = dft_consts()
    K2 = fft_consts()
    MC = _moe_consts()
    wbf = run_cast(inp, progs)
    for layer in range(4):
        j = layer // 2
        mcols = np.zeros((128, 2, 2, 8), np.float32)
        for s in range(2):
            mcols[:, s, 0] = _cols(modx[layer, s, 0]); mcols[:, s, 1] = _cols(modx[layer, s, 1])
        g1 = np.ascontiguousarray(np.stack([_rep(modx[layer, 0, 2]), _rep(modx[layer, 1, 2])], 1))
        if layer % 2 == 0:
            cwh = np.ascontiguousarray(inp['conv_w'][j].reshape(3, 4, 128).transpose(2, 0, 1))
            ins = []
            for c in CORES:
                xt = np.zeros((19, 128, 1024), np.float32)
                xt[:16] = xl[2048 * c:2048 * (c + 1)].reshape(16, 128, 1024)
                xt[16:18] = xc.reshape(2, 128, 1024)
                if c > 0:
                    xt[18, 0] = xl[2048 * c - 1]
                if c < 7:
                    xt[18, 1] = xl[2048 * (c + 1)]
                flags = np.zeros((128, 2), np.float32); flags[:, 0] = float(c > 0); flags[:, 1] = float(c < 7)
                ins.append(dict(xt=xt, mcols=mcols, win=inp['w_in_even'][j], cw=cwh, cs128=K1['cs128'], c256=K1['c256'],
                                ns256=K1['ns256'], flags=flags))
            r1 = _run(progs['e1'], ins)
            Bfull = np.concatenate([np.asarray(r1[c]['bout']) for c in CORES], 0)
            B5 = Bfull.reshape(128, 128, 4, 2, 128)
            ins = []
            for c in CORES:
                g = c // 2; m0 = 64 * (c % 2)
                zin = np.ascontiguousarray(B5[:, :, g, :, m0:m0 + 64].transpose(0, 2, 3, 1))
                ins.append(dict(zin=zin, m1=K2['m1'], m2=K2['m2'], tw=K2['tw'], c2=K2['c2']))
            r2 = _run(progs['e2'], ins)
            fmT = np.zeros((512, 16384), dtype=Bfull.dtype)
            for c in CORES:
                g = c // 2; m0 = 64 * (c % 2)
                fmT[g * 128 + m0:g * 128 + m0 + 64] = np.asarray(r2[c]['FT']).reshape(64, 16384)
            yTs = []
            for c in CORES:
                top = np.concatenate([fmT[:, 2048 * c:2048 * (c + 1)], np.asarray(r1[c]['fcT'])], 1)
                yTs.append(np.ascontiguousarray(np.concatenate([top, np.asarray(r1[c]['ycT'])], 0)))
            wproj = inp['w_out_even'][j]
        else:
            ins = []
            for c in CORES:
                xt = np.zeros((20, 128, 1024), np.float32)
                if c > 0:
                    xt[0] = xl[2048 * c - 128:2048 * c]
                xt[1:17] = xl[2048 * c:2048 * (c + 1)].reshape(16, 128, 1024)
                if c < 7:
                    xt[17] = xl[2048 * (c + 1):2048 * (c + 1) + 128]
                xt[18:20] = xc.reshape(2, 128, 1024)
                ins.append(dict(xt=xt, mcols=mcols, wqkv=inp['w_qkv'][j], gq=_rep(inp['q_norm_g'][j]), gk=_rep(inp['k_norm_g'][j]),
                                rope=rope_tables(c), sink=_rep(inp['sink_logit'][j]), masks=att_masks(c)))
            ra = _run(progs['att'], ins)
            yTs = [np.asarray(ra[c]['OT']) for c in CORES]
            wproj = inp['w_o'][j]
        ins = []
        for c in CORES:
            xt = np.concatenate([xl[2048 * c:2048 * (c + 1)], xc], 0).reshape(18, 128, 1024)
            ins.append(dict(yT=yTs[c], xt=np.ascontiguousarray(xt), g1=g1, w=wproj))
        r3 = _run(progs['e3'], ins)
        rows = np.zeros((2, 3, 128, 1024), np.float32)
        for s in range(2):
            rows[s, 0] = _rep(modx[layer, s, 4]); rows[s, 1] = _rep(modx[layer, s, 3]); rows[s, 2] = _rep(modx[layer, s, 5])
        wr = np.ascontiguousarray(np.concatenate([inp['w_router_g'][layer], inp['w_router_e'][layer]], 1))
        br = _rep(np.concatenate([inp['b_router_g'][layer], inp['b_router_e'][layer]]))
        w1L, w3L, w2L = wbf[layer]
        ins = []
        for c in CORES:
            ins.append(dict(xt=np.asarray(r3[c]['xo']), rows=rows, wr=wr, br=br, w1L=w1L, w3L=w3L, w2L=w2L, **MC))
        r4 = _run(progs['moe'], ins)
        xl = np.concatenate([np.asarray(r4[c]['xo']).reshape(2304, 1024)[:2048] for c in CORES], 0)
        xc = np.asarray(r4[0]['xo']).reshape(2304, 1024)[2048:]
    return np.ascontiguousarray(xl, dtype=np.float32)[None]


def build_cast():
    nc = bass.Bass("TRN2", target_bir_lowering=False)
    P = Prog(nc)
    wi = P.din("wi", [12, 512, 4096], F32)
    wo = P.dout("wo", [12, 512, 4096], BF16)
    NBUF = 4
    T = [P.sb(f"t{i}", [128, 4096], BF16) for i in range(NBUF)]
    bt = P.bufs(NBUF)
    outs = []
    n = 0
    for m in range(12):
        for r in range(4):
            i = n % NBUF; n += 1
            P.dma("pool", lambda e, i=i, m=m, r=r: e.dma_start(out=T[i][:], in_=wi[m, r * 128:(r + 1) * 128, :]), writes=[bt[i]])
            outs.append(P.dma("sp", lambda e, i=i, m=m, r=r: e.dma_start(out=wo[m, r * 128:(r + 1) * 128, :], in_=T[i][:]), reads=[bt[i]]))
    P.emit(final_wait_ops=outs)
    return nc


def run_cast(inp, progs):
    mats = []
    for layer in range(4):
        mats += [_wlayout(inp['w1'][layer], 8), _wlayout(inp['w3'][layer], 8), _wlayout(inp['w2'][layer], 4)]
    ins = [dict(wi=np.ascontiguousarray(np.stack([m[512 * c:512 * (c + 1)] for m in mats], 0))) for c in range(8)]
    res = run_bass_kernel_spmd(progs['cast'], ins, core_ids=list(range(8)))
    outs = [np.concatenate([np.asarray(res.results[c]['wo'])[m] for c in range(8)], 0) for m in range(12)]
    return [(outs[3 * l], outs[3 * l + 1], outs[3 * l + 2]) for l in range(4)]
```
